# Optimizing a Trainium2 kernel written in Bass

```python
import math
import jax, jax.numpy as jnp
from jax import lax
import numpy as np

D_MODEL = 4096
BATCH = 2
SEQ = 8192
DEPTH = 2

MIX_W = D_MODEL
HEAD_DIM = 128
ATTN_W = MIX_W // 2
N_HEAD_A = ATTN_W // HEAD_DIM
HEADS_PER_GROUP = 4
N_KV_GROUP = N_HEAD_A // HEADS_PER_GROUP
KV_W = N_KV_GROUP * HEAD_DIM
LRU_W = MIX_W // 4
LRU_BLOCKS = 8
LRU_BW = LRU_W // LRU_BLOCKS
LRU_CONV = 4
LRU_C = 8.0
GMLP_W = MIX_W // 4
GMLP_GROUPS = 8
GMLP_GW = GMLP_W // GMLP_GROUPS
CHUNK = 128
CMP_LEN = 32
CMP_STRIDE = 16
SEL_BLOCK = 64
SEL_TOPK = 16
WINDOW = 512
QBLK = 64
D_FF = 256 * (-(-8 * D_MODEL // (3 * 256)))
FFN_CONV = 3
MOD_INIT = 0.5
N_IN = ATTN_W + 6 * KV_W + 3 * N_HEAD_A + 2 * LRU_W + 2 * GMLP_W
NEG = -1e30
FORCE = 1e4

kernel_name = "hybrid_nsa_rglru_gmlp_deepnorm_block"


def layer_norm(x, g, b, eps=1e-5):
    xf = x.astype(jnp.float32)
    mu = jnp.mean(xf, -1, keepdims=True)
    var = jnp.mean(jnp.square(xf - mu), -1, keepdims=True)
    return ((xf - mu) * lax.rsqrt(var + eps) * g + b).astype(x.dtype)


def causal_dwconv(x, w, b):
    K = w.shape[0]
    S = x.shape[1]
    xp = jnp.pad(x, ((0, 0), (K - 1, 0), (0, 0)))
    y = xp[:, 0:S] * w[0]
    for k in range(1, K):
        y = y + xp[:, k:k + S] * w[k]
    return y + b


def split_cols(z, sizes):
    offs = np.cumsum(np.array(sizes))[:-1].tolist()
    return jnp.split(z, offs, axis=-1)


def nsa_compress(kv, pos, w1, b1, w2):
    B, S, G, DH = kv.shape
    r = CMP_LEN // CMP_STRIDE
    n_chunk = S // CMP_STRIDE
    nc = n_chunk - r + 1
    ch = kv.reshape(B, n_chunk, CMP_STRIDE, G, DH)
    blk = jnp.concatenate([ch[:, j:j + nc] for j in range(r)], axis=2)
    blk = blk + pos[:, None, :]
    flat = jnp.swapaxes(blk, 2, 3).reshape(B, nc, G, CMP_LEN * DH)
    return jax.nn.gelu(flat @ w1 + b1) @ w2


def nsa_attention(q, kc, vc, ks, vs, kw, vw, gates):
    B, S, G, HPG, DH = q.shape
    NC = kc.shape[1]
    nsel = S // SEL_BLOCK
    topk = min(SEL_TOPK, nsel)
    nqb = S // QBLK
    scale = DH ** -0.5
    c_start = jnp.arange(NC) * CMP_STRIDE
    c_end = c_start + CMP_LEN - 1
    s_start = jnp.arange(nsel) * SEL_BLOCK
    cover = ((c_start[:, None] < s_start[None, :] + SEL_BLOCK)
             & (c_start[:, None] + CMP_LEN > s_start[None, :])).astype(jnp.float32)
    ks_b = ks.reshape(B, nsel, SEL_BLOCK, G, DH).transpose(0, 3, 1, 2, 4)
    vs_b = vs.reshape(B, nsel, SEL_BLOCK, G, DH).transpose(0, 3, 1, 2, 4)
    kw_p = jnp.pad(kw, ((0, 0), (WINDOW, 0), (0, 0), (0, 0)))
    vw_p = jnp.pad(vw, ((0, 0), (WINDOW, 0), (0, 0), (0, 0)))
    q_b = q.reshape(B, nqb, QBLK, G, HPG, DH).transpose(1, 0, 2, 3, 4, 5)
    g_b = gates.reshape(B, nqb, QBLK, G, HPG, 3).transpose(1, 0, 2, 3, 4, 5)
    bi = jnp.arange(B)[:, None, None, None]
    gi = jnp.arange(G)[None, :, None, None]
    blk_ids = jnp.arange(nsel)

    def block(args):
        i, qi, gt = args
        t = i * QBLK + jnp.arange(QBLK)
        s = jnp.einsum('bqghd,bcgd->bghqc', qi, kc, preferred_element_type=jnp.float32) * scale
        valid = c_end[None, :] <= t[:, None]
        s = jnp.where(valid, s, NEG)
        p_c = jax.nn.softmax(s, axis=-1) * jnp.any(valid, -1)[:, None].astype(jnp.float32)
        o_c = jnp.einsum('bghqc,bcgd->bqghd', p_c.astype(vc.dtype), vc)
        imp = jnp.einsum('bghqc,cj->bgqj', p_c, cover)
        cur = (t // SEL_BLOCK)[:, None]
        forced = (blk_ids[None, :] == 0) | (blk_ids[None, :] == cur) | (blk_ids[None, :] == cur - 1)
        causal = s_start[None, :] <= t[:, None]
        imp = jnp.where(forced, FORCE, jnp.where(causal, imp, NEG))
        _, idx = lax.top_k(imp, topk)
        k_sel = ks_b[bi, gi, idx]
        v_sel = vs_b[bi, gi, idx]
        kpos = idx[..., None] * SEL_BLOCK + jnp.arange(SEL_BLOCK)
        ok = kpos <= t[None, None, :, None, None]
        s = jnp.einsum('bqghd,bgqkld->bghqkl', qi, k_sel, preferred_element_type=jnp.float32) * scale
        s = jnp.where(ok[:, :, None], s, NEG)
        p_s = jax.nn.softmax(s.reshape(B, G, HPG, QBLK, topk * SEL_BLOCK), axis=-1).reshape(s.shape)
        o_s = jnp.einsum('bghqkl,bgqkld->bqghd', p_s.astype(v_sel.dtype), v_sel)
        kwin = lax.dynamic_slice_in_dim(kw_p, i * QBLK, WINDOW + QBLK, axis=1)
        vwin = lax.dynamic_slice_in_dim(vw_p, i * QBLK, WINDOW + QBLK, axis=1)
        wpos = i * QBLK - WINDOW + jnp.arange(WINDOW + QBLK)
        okw = (wpos[None, :] >= 0) & (wpos[None, :] <= t[:, None]) & (t[:, None] - wpos[None, :] < WINDOW)
        s = jnp.einsum('bqghd,bkgd->bghqk', qi, kwin, preferred_element_type=jnp.float32) * scale
        s = jnp.where(okw, s, NEG)
        p_w = jax.nn.softmax(s, axis=-1)
        o_w = jnp.einsum('bghqk,bkgd->bqghd', p_w.astype(vwin.dtype), vwin)
        return gt[..., 0:1] * o_c + gt[..., 1:2] * o_s + gt[..., 2:3] * o_w

    out = lax.map(block, (jnp.arange(nqb), q_b, g_b))
    return out.transpose(1, 0, 2, 3, 4, 5).reshape(B, S, G * HPG * DH)


def rg_lru(x, w_a, b_a, w_x, b_x, lam):
    B, S, W = x.shape
    xb = x.reshape(B, S, LRU_BLOCKS, LRU_BW)
    r = jax.nn.sigmoid(jnp.einsum('bsnc,ncd->bsnd', xb, w_a).reshape(B, S, W) + b_a).astype(jnp.float32)
    ig = jax.nn.sigmoid(jnp.einsum('bsnc,ncd->bsnd', xb, w_x).reshape(B, S, W) + b_x)
    log_a = -LRU_C * r * jax.nn.softplus(-lam.astype(jnp.float32))
    a = jnp.exp(log_a)
    bt = jnp.sqrt(-jnp.expm1(2.0 * log_a)) * (ig * x).astype(jnp.float32)

    def combine(left, right):
        a1, b1 = left
        a2, b2 = right
        return a1 * a2, a2 * b1 + b2

    _, h = lax.associative_scan(combine, (a, bt), axis=1)
    return h.astype(x.dtype)


def spatial_gating(u, v, ln_g, ln_b, w_s, b_s):
    B, S, W = u.shape
    v = layer_norm(v, ln_g, ln_b)
    vb = v.reshape(B, S // CHUNK, CHUNK, GMLP_GROUPS, GMLP_GW)
    ws = jnp.where(jnp.tril(jnp.ones((CHUNK, CHUNK), dtype=bool)), w_s, 0.0)
    y = jnp.einsum('gts,bnsgc->bntgc', ws, vb) + b_s.T[None, None, :, :, None]
    return u * y.reshape(B, S, W)


def token_mixer(h, w_in, cmp_pos, cmp_w1, cmp_b1, cmp_w2, lru_conv_w, lru_conv_b,
                lru_wa, lru_ba, lru_wx, lru_bx, lru_lambda, sgu_ln_g, sgu_ln_b, sgu_w, sgu_b, w_o):
    B, S, _ = h.shape
    z = h @ w_in
    (q, kc, vc, ks, vs, kw, vw, gl, zg, zr, zu, zv) = split_cols(
        z, [ATTN_W] + [KV_W] * 6 + [3 * N_HEAD_A, LRU_W, LRU_W, GMLP_W, GMLP_W])
    kv_shape = (B, S, N_KV_GROUP, HEAD_DIM)
    q = q.reshape(B, S, N_KV_GROUP, HEADS_PER_GROUP, HEAD_DIM)
    kc = nsa_compress(kc.reshape(kv_shape), cmp_pos[0], cmp_w1[0], cmp_b1[0], cmp_w2[0])
    vc = nsa_compress(vc.reshape(kv_shape), cmp_pos[1], cmp_w1[1], cmp_b1[1], cmp_w2[1])
    gates = jax.nn.sigmoid(gl).reshape(B, S, N_KV_GROUP, HEADS_PER_GROUP, 3)
    y_attn = nsa_attention(q, kc, vc, ks.reshape(kv_shape), vs.reshape(kv_shape),
                           kw.reshape(kv_shape), vw.reshape(kv_shape), gates)
    xr = causal_dwconv(zr, lru_conv_w, lru_conv_b)
    y_lru = jax.nn.gelu(zg) * rg_lru(xr, lru_wa, lru_ba, lru_wx, lru_bx, lru_lambda)
    y_sgu = spatial_gating(jax.nn.gelu(zu), jax.nn.gelu(zv), sgu_ln_g, sgu_ln_b, sgu_w, sgu_b)
    return jnp.concatenate([y_attn, y_lru, y_sgu], axis=-1) @ w_o


def conv_ffn(h, w_in, conv_w, conv_b, w_down):
    g, u = jnp.split(h @ w_in, 2, axis=-1)
    g = causal_dwconv(g, conv_w, conv_b)
    return (jax.nn.silu(g) * u) @ w_down


def setup_inputs(seed: int = 0) -> dict:
    key = jax.random.key(seed)
    ks = jax.random.split(key, 32)
    f32 = jnp.float32
    L, D = DEPTH, D_MODEL
    beta = (8 * DEPTH) ** -0.25

    def nrm(k, shape, s):
        return jax.random.normal(k, shape, f32) * s

    u = jax.random.uniform(ks[15], (L, LRU_W), f32, 0.9, 0.999)
    p = u ** (1.0 / LRU_C)
    return {
        "x": nrm(ks[0], (BATCH, SEQ, D), 1.0),
        "c": nrm(ks[1], (BATCH, D), 1.0),
        "w_mod": nrm(ks[2], (L, D, 6 * D), MOD_INIT * D ** -0.5),
        "b_mod": nrm(ks[3], (L, 6 * D), 0.02),
        "w_in": nrm(ks[4], (L, D, N_IN), D ** -0.5),
        "cmp_pos": nrm(ks[5], (L, 2, CMP_LEN, HEAD_DIM), 0.1),
        "cmp_w1": nrm(ks[6], (L, 2, CMP_LEN * HEAD_DIM, HEAD_DIM), (CMP_LEN * HEAD_DIM) ** -0.5),
        "cmp_b1": nrm(ks[7], (L, 2, HEAD_DIM), 0.02),
        "cmp_w2": nrm(ks[8], (L, 2, HEAD_DIM, HEAD_DIM), HEAD_DIM ** -0.5),
        "lru_conv_w": nrm(ks[9], (L, LRU_CONV, LRU_W), LRU_CONV ** -0.5),
        "lru_conv_b": nrm(ks[10], (L, LRU_W), 0.02),
        "lru_wa": nrm(ks[11], (L, LRU_BLOCKS, LRU_BW, LRU_BW), LRU_BW ** -0.5),
        "lru_ba": nrm(ks[12], (L, LRU_W), 0.02),
        "lru_wx": nrm(ks[13], (L, LRU_BLOCKS, LRU_BW, LRU_BW), LRU_BW ** -0.5),
        "lru_bx": nrm(ks[14], (L, LRU_W), 0.02),
        "lru_lambda": jnp.log(p) - jnp.log1p(-p),
        "sgu_ln_g": 1.0 + nrm(ks[16], (L, GMLP_W), 0.02),
        "sgu_ln_b": nrm(ks[17], (L, GMLP_W), 0.02),
        "sgu_w": nrm(ks[18], (L, GMLP_GROUPS, CHUNK, CHUNK), CHUNK ** -0.5),
        "sgu_b": 1.0 + nrm(ks[19], (L, GMLP_GROUPS, CHUNK), 0.1),
        "w_o": nrm(ks[20], (L, MIX_W, D), beta * MIX_W ** -0.5),
        "ln1_g": 1.0 + nrm(ks[21], (L, D), 0.02),
        "ln1_b": nrm(ks[22], (L, D), 0.02),
        "ffn_w_in": nrm(ks[23], (L, D, 2 * D_FF), D ** -0.5),
        "ffn_conv_w": nrm(ks[24], (L, FFN_CONV, D_FF), FFN_CONV ** -0.5),
        "ffn_conv_b": nrm(ks[25], (L, D_FF), 0.02),
        "ffn_w_down": nrm(ks[26], (L, D_FF, D), beta * D_FF ** -0.5),
        "ln2_g": 1.0 + nrm(ks[27], (L, D), 0.02),
        "ln2_b": nrm(ks[28], (L, D), 0.02),
    }


def reference(x, c, w_mod, b_mod, w_in, cmp_pos, cmp_w1, cmp_b1, cmp_w2, lru_conv_w, lru_conv_b,
              lru_wa, lru_ba, lru_wx, lru_bx, lru_lambda, sgu_ln_g, sgu_ln_b, sgu_w, sgu_b, w_o,
              ln1_g, ln1_b, ffn_w_in, ffn_conv_w, ffn_conv_b, ffn_w_down, ln2_g, ln2_b):
    alpha = (2.0 * DEPTH) ** 0.25
    cs = jax.nn.silu(c)
    for l in range(DEPTH):
        mod = (cs @ w_mod[l] + b_mod[l])[:, None, :]
        sh_m, sc_m, g_m, sh_f, sc_f, g_f = jnp.split(mod, 6, axis=-1)
        h = x * (1.0 + sc_m) + sh_m
        y = token_mixer(h, w_in[l], cmp_pos[l], cmp_w1[l], cmp_b1[l], cmp_w2[l], lru_conv_w[l],
                        lru_conv_b[l], lru_wa[l], lru_ba[l], lru_wx[l], lru_bx[l], lru_lambda[l],
                        sgu_ln_g[l], sgu_ln_b[l], sgu_w[l], sgu_b[l], w_o[l])
        x = layer_norm(alpha * x + (1.0 + g_m) * y, ln1_g[l], ln1_b[l])
        h = x * (1.0 + sc_f) + sh_f
        y = conv_ffn(h, ffn_w_in[l], ffn_conv_w[l], ffn_conv_b[l], ffn_w_down[l])
        x = layer_norm(alpha * x + (1.0 + g_f) * y, ln2_g[l], ln2_b[l])
    return x
```

```python
import numpy as np
from contextlib import ExitStack
import concourse.bass as bass
import concourse.mybir as mybir
from concourse.bass_utils import run_bass_kernel_spmd

F32 = mybir.dt.float32
BF16 = mybir.dt.bfloat16
AF = mybir.ActivationFunctionType
ALU = mybir.AluOpType
AX = mybir.AxisListType

ENGS = ['pe', 'act', 'dve', 'pool', 'sp']
NDS = 8


class T:
    __slots__ = ('t', 'w', 'r')

    def __init__(self, t=None):
        self.t = t
        self.w = None
        self.r = []

    def __getitem__(self, k):
        return self.t[k]


class Prog:
    def __init__(self, nc, es, same_engine_wait=True):
        self.nc = nc
        self.es = es
        self.h = {'pe': nc.tensor, 'act': nc.scalar, 'dve': nc.vector, 'pool': nc.gpsimd, 'sp': nc.sync}
        self.ops = {e: [] for e in ENGS}
        self.sems = {}
        for e in ENGS:
            self.sems[e] = es.enter_context(nc.semaphore('s_' + e))
        self.cnt = {e: 0 for e in ENGS}
        self.known = {e: {} for e in ENGS}
        self.dq = {}
        for q in ('sp', 'pool', 'act'):
            for i in range(NDS):
                self.sems[(q, i)] = es.enter_context(nc.semaphore('d_%s%d' % (q, i)))
            self.dq[q] = 0
        self.dval = {}
        self.sew = same_engine_wait
        self.nwait = 0

    def sb(self, name, shape, dt):
        return T(self.es.enter_context(self.nc.sbuf_tensor("sb_" + name, list(shape), dt)))

    def ps(self, name, shape, dt=F32):
        return T(self.es.enter_context(self.nc.psum_tensor("ps_" + name, list(shape), dt)))

    def _wait(self, eng, tok):
        key, val = tok
        if key == eng and (eng == 'pe' or not self.sew):
            return
        if self.known[eng].get(key, 0) >= val:
            return
        self.known[eng][key] = val
        sem = self.sems[key]
        self.ops[eng].append(lambda h, sem=sem, val=val: h.wait_ge(sem, val))
        self.nwait += 1

    def _deps(self, eng, r, w):
        need = {}
        for t in r:
            if t.w is not None:
                k, v = t.w
                if need.get(k, 0) < v:
                    need[k] = v
        for t in w:
            if t.w is not None:
                k, v = t.w
                if need.get(k, 0) < v:
                    need[k] = v
            for k, v in t.r:
                if need.get(k, 0) < v:
                    need[k] = v
        for k, v in need.items():
            self._wait(eng, (k, v))

    def op(self, eng, fn, r=(), w=()):
        self._deps(eng, r, w)
        self.cnt[eng] += 1
        tok = (eng, self.cnt[eng])
        sem = self.sems[eng]
        self.ops[eng].append(lambda h, fn=fn, sem=sem: fn(h).then_inc(sem, 1))
        for t in w:
            t.w = tok
            t.r = []
        for t in r:
            if t not in w:
                self._addr(t, tok)
        return tok

    def _addr(self, t, tok):
        k, v = tok
        for i, (k2, v2) in enumerate(t.r):
            if k2 == k:
                if v2 < v:
                    t.r[i] = tok
                return
        t.r.append(tok)

    def dma(self, q, out, in_, r=(), w=(), **kw):
        n = self.dq[q]
        self.dq[q] = n + 1
        key = (q, n % NDS)
        val = 16 * (n // NDS + 1)
        if val > 16:
            self._wait(q, (key, val - 16))
        self._deps(q, r, w)
        sem = self.sems[key]
        self.ops[q].append(lambda h, out=out, in_=in_, sem=sem, kw=kw: h.dma_start(out=out, in_=in_, **kw).then_inc(sem, 16))
        self.dval[key] = val
        tok = (key, val)
        for t in w:
            t.w = tok
            t.r = []
        for t in r:
            if t not in w:
                self._addr(t, tok)
        return tok

    def barrier(self):
        toks = [(e, self.cnt[e]) for e in ENGS if self.cnt[e] > 0]
        toks += [(k, v) for k, v in self.dval.items()]
        for e in ENGS:
            for tok in toks:
                if tok[0] != e:
                    self._wait(e, tok)
                elif e != 'pe':
                    self._wait(e, tok)

    def emit(self):
        self.barrier()
        nc = self.nc
        self.total = getattr(self, 'total', 0) + sum(len(v) for v in self.ops.values())
        self._emit_block()
        self.ops = {e: [] for e in ENGS}

    def _emit_block(self):
        nc = self.nc
        with nc.Block() as block:
            @block.tensor
            def _(h):
                for f in self.ops['pe']:
                    f(h)

            @block.scalar
            def _(h):
                for f in self.ops['act']:
                    f(h)

            @block.vector
            def _(h):
                for f in self.ops['dve']:
                    f(h)

            @block.gpsimd
            def _(h):
                for f in self.ops['pool']:
                    f(h)

            @block.sync
            def _(h):
                for f in self.ops['sp']:
                    f(h)


ALPHA = (2.0 * 2) ** 0.25
NT = 2048
TB = 512
KT = 32
NFC = 86
EPS = 1e-5


class Rot:
    def __init__(self, tiles):
        self.tiles = tiles
        self.i = 0

    def next(self):
        t = self.tiles[self.i % len(self.tiles)]
        self.i += 1
        return t


def build_C(NT=NT, with_halo=True):
    HW = 2 if with_halo else 0
    nc = bass.Bass("TRN2", target_bir_lowering=False)

    def din(name, shape, dt=F32):
        return nc.dram_tensor(name, shape, dt, kind="ExternalInput").ap()

    yT = din("yT", [32, 128, NT + HW])
    xT = din("xT", [32, 128, NT + HW])
    hflag = din("hflag", [128, 1])
    modv = din("modv", [128, 192])
    lnp = din("lnp", [128, 128])
    wo_t = din("wo_t", [32, 128, 4096])
    wi_t = din("wi_t", [NFC, 128, 2, 4096])
    wd_t = din("wd_t", [32, 128, NFC * 128])
    fcv = din("fcv", [128, NFC * 4])
    x2T = nc.dram_tensor("x2T", [32, 128, NT], F32, kind="ExternalOutput").ap()
    vs = nc.dram_tensor("vs", [32, 128, TB], F32, kind="Internal").ap()
    x1s = nc.dram_tensor("x1s", [32, 128, TB], F32, kind="Internal").ap()

    with ExitStack() as es:
        P = Prog(nc, es)
        modv_sb = P.sb("modv_sb", [128, 192], F32)
        lnp_sb = P.sb("lnp_sb", [128, 128], F32)
        fcv_sb = P.sb("fcv_sb", [128, NFC * 4], F32)
        hflag_sb = P.sb("hflag_sb", [128, 1], F32)
        ones_sb = P.sb("ones_sb", [128, 128], F32)
        op1 = P.sb("op1", [128, 192], F32)
        P.dma('sp', modv_sb[:], modv, w=[modv_sb])
        P.dma('sp', lnp_sb[:], lnp, w=[lnp_sb])
        P.dma('sp', fcv_sb[:], fcv, w=[fcv_sb])
        P.dma('sp', hflag_sb[:], hflag, w=[hflag_sb])
        P.op('dve', lambda h: h.memset(ones_sb[:], 1.0), w=[ones_sb])
        P.op('dve', lambda h: h.tensor_scalar(out=op1[:], in0=modv_sb[:], scalar1=1.0, scalar2=None, op0=ALU.add), r=[modv_sb], w=[op1])

        R1 = es.enter_context(nc.sbuf_tensor("R1", [128, NFC * TB], BF16))
        aT = [T(R1) for _ in range(NFC)]
        hTt = es.enter_context(nc.sbuf_tensor("hT", [128, KT * TB], BF16))
        hT = [T(hTt) for _ in range(KT)]
        WBN = 43 * 128
        wb = Rot([P.sb("wb%d" % i, [128, WBN], BF16) for i in range(3)])
        ghalo = P.sb("ghalo", [128, NFC * 2], F32)
        tp = Rot([P.sb("tp%d" % i, [128, TB + 2], F32) for i in range(10)])
        xt = vt = sq = t1 = vl = nt_ = xo = gext = acc = sl = tp
        mean = P.sb("mean", [128, TB], F32)
        rstd = P.sb("rstd", [128, TB], F32)
        nmr = P.sb("nmr", [128, TB], F32)
        msq = nmr
        mm = Rot([P.ps("mm%d" % i, [128, TB]) for i in range(4)])
        ps_sum = P.ps("ps_sum", [128, TB])
        ps_sq = P.ps("ps_sq", [128, TB])
        vsT = [T(vs) for _ in range(32)]
        x1sT = [T(x1s) for _ in range(32)]

        def load_w(src2d, ncols):
            wt = wb.next()
            P.dma('pool', wt[:, 0:ncols], src2d, w=[wt], max_dma_last_dim=4096)
            return wt

        def stats_finish(W):
            P.op('act', lambda h: h.activation(out=mean[:, 0:W], in_=ps_sum[:, 0:W], func=AF.Copy, scale=1.0 / 4096), r=[ps_sum], w=[mean])
            P.op('dve', lambda h: h.tensor_tensor(out=msq[:, 0:W], in0=mean[:, 0:W], in1=mean[:, 0:W], op=ALU.mult), r=[mean], w=[msq])
            P.op('dve', lambda h: h.scalar_tensor_tensor(out=rstd[:, 0:W], in0=ps_sq[:, 0:W], scalar=1.0 / 4096, in1=msq[:, 0:W], op0=ALU.mult, op1=ALU.subtract), r=[ps_sq, msq], w=[rstd])
            P.op('dve', lambda h: h.tensor_scalar(out=rstd[:, 0:W], in0=rstd[:, 0:W], scalar1=EPS, scalar2=None, op0=ALU.add), r=[rstd], w=[rstd])
            P.op('act', lambda h: h.activation(out=rstd[:, 0:W], in_=rstd[:, 0:W], func=AF.Sqrt), r=[rstd], w=[rstd])
            P.op('dve', lambda h: h.reciprocal(out=rstd[:, 0:W], in_=rstd[:, 0:W]), r=[rstd], w=[rstd])
            P.op('dve', lambda h: h.scalar_tensor_tensor(out=nmr[:, 0:W], in0=mean[:, 0:W], scalar=-1.0, in1=rstd[:, 0:W], op0=ALU.mult, op1=ALU.mult), r=[mean, rstd], w=[nmr])

        def ln_sublayer(W, main_fn, xsrc_fn, gcol, lng, lnb, out_fn):
            pending = None
            xcur = xt.next()
            ap, trk = xsrc_fn(0)
            P.dma('sp', xcur[:, 0:W], ap, r=trk, w=[xcur])
            for dt in range(32):
                pm = main_fn(dt)
                if pending is not None:
                    pending()
                xnext = None
                if dt + 1 < 32:
                    xnext = xt.next()
                    ap, trk = xsrc_fn(dt + 1)
                    P.dma('sp', xnext[:, 0:W], ap, r=trk, w=[xnext])
                a1 = t1.next()
                v = vt.next()
                s = sq.next()
                P.op('act', lambda h, a1=a1, pm=pm, dt=dt: h.activation(out=a1[:, 0:W], in_=pm[:, 0:W], func=AF.Identity, scale=op1[:, gcol * 32 + dt:gcol * 32 + dt + 1]), r=[pm, op1], w=[a1])
                P.op('dve', lambda h, v=v, a1=a1, xc=xcur: h.scalar_tensor_tensor(out=v[:, 0:W], in0=xc[:, 0:W], scalar=ALPHA, in1=a1[:, 0:W], op0=ALU.mult, op1=ALU.add), r=[xcur, a1], w=[v])
                P.op('act', lambda h, v=v, s=s: h.activation(out=s[:, 0:W], in_=v[:, 0:W], func=AF.Square), r=[v], w=[s])
                P.dma('sp', vs[dt, :, 0:W], v[:, 0:W], r=[v], w=[vsT[dt]])

                def st(v=v, s=s, dt=dt):
                    P.op('pe', lambda h: h.matmul(ps_sum[:, 0:W], ones_sb[:], v[:, 0:W], start=(dt == 0), stop=(dt == 31)), r=[ones_sb, v], w=[ps_sum])
                    P.op('pe', lambda h: h.matmul(ps_sq[:, 0:W], ones_sb[:], s[:, 0:W], start=(dt == 0), stop=(dt == 31)), r=[ones_sb, s], w=[ps_sq])
                pending = st
                xcur = xnext
            pending()
            stats_finish(W)
            vcur = vl.next()
            P.dma('sp', vcur[:, 0:W], vs[0, :, 0:W], r=[vsT[0]], w=[vcur])
            for dt in range(32):
                vnext = None
                if dt + 1 < 32:
                    vnext = vl.next()
                    P.dma('sp', vnext[:, 0:W], vs[dt + 1, :, 0:W], r=[vsT[dt + 1]], w=[vnext])
                n = nt_.next()
                P.op('dve', lambda h, n=n, vc=vcur: h.tensor_tensor(out=n[:, 0:W], in0=vc[:, 0:W], in1=rstd[:, 0:W], op=ALU.mult), r=[vcur, rstd], w=[n])
                P.op('dve', lambda h, n=n: h.tensor_tensor(out=n[:, 0:W], in0=n[:, 0:W], in1=nmr[:, 0:W], op=ALU.add), r=[n, nmr], w=[n])
                o = xo.next()
                P.op('act', lambda h, n=n, o=o, dt=dt: h.activation(out=o[:, 0:W], in_=n[:, 0:W], func=AF.Identity, scale=lnp_sb[:, lng * 32 + dt:lng * 32 + dt + 1], bias=lnp_sb[:, lnb * 32 + dt:lnb * 32 + dt + 1]), r=[n, lnp_sb], w=[o])
                out_fn(dt, o)
                vcur = vnext

        def block(col0, W, halo):
            for kt in range(KT):
                P.dma('pool', R1[:, kt * TB:kt * TB + W], yT[kt, :, col0:col0 + W], w=[aT[kt]], max_dma_last_dim=2048)

            def main1(dt):
                wt = load_w(wo_t[dt], 4096)
                pm = mm.next()
                for kt in range(KT):
                    P.op('pe', lambda h, kt=kt, wt=wt, pm=pm: h.matmul(pm[:, 0:W], wt[:, kt * 128:(kt + 1) * 128], R1[:, kt * TB:kt * TB + W], start=(kt == 0), stop=(kt == KT - 1)), r=[wt, aT[kt]], w=[pm])
                return pm

            def xsrc1(dt):
                return xT[dt, :, col0:col0 + W], []

            def out1(dt, o):
                if not halo:
                    P.dma('sp', x1s[dt, :, 0:W], o[:, 0:W], r=[o], w=[x1sT[dt]])
                P.op('act', lambda h: h.activation(out=hTt[:, dt * TB:dt * TB + W], in_=o[:, 0:W], func=AF.Identity, scale=op1[:, 4 * 32 + dt:4 * 32 + dt + 1], bias=modv_sb[:, 3 * 32 + dt:3 * 32 + dt + 1]), r=[o, op1, modv_sb], w=[hT[dt]])

            ln_sublayer(W, main1, xsrc1, 2, 0, 1, out1)

            for c in range(NFC):
                wg = load_w(wi_t[c, :, 0, :], 4096)
                pg = mm.next()
                for kt in range(KT):
                    P.op('pe', lambda h, kt=kt, wg=wg, pg=pg: h.matmul(pg[:, 0:W], wg[:, kt * 128:(kt + 1) * 128], hTt[:, kt * TB:kt * TB + W], start=(kt == 0), stop=(kt == KT - 1)), r=[wg, hT[kt]], w=[pg])
                if halo:
                    P.op('act', lambda h, c=c, pg=pg: h.activation(out=ghalo[:, 2 * c:2 * c + 2], in_=pg[:, 0:2], func=AF.Identity, scale=hflag_sb[:, 0:1]), r=[pg, hflag_sb], w=[ghalo])
                    continue
                wu = load_w(wi_t[c, :, 1, :], 4096)
                pu = mm.next()
                for kt in range(KT):
                    P.op('pe', lambda h, kt=kt, wu=wu, pu=pu: h.matmul(pu[:, 0:W], wu[:, kt * 128:(kt + 1) * 128], hTt[:, kt * TB:kt * TB + W], start=(kt == 0), stop=(kt == KT - 1)), r=[wu, hT[kt]], w=[pu])
                ge = gext.next()
                ac = acc.next()
                s_ = sl.next()
                P.op('act', lambda h, ge=ge, c=c: h.activation(out=ge[:, 0:2], in_=ghalo[:, 2 * c:2 * c + 2], func=AF.Copy), r=[ghalo], w=[ge])
                P.op('act', lambda h, ge=ge, pg=pg: h.activation(out=ge[:, 2:2 + W], in_=pg[:, 0:W], func=AF.Copy), r=[pg], w=[ge])
                P.op('act', lambda h, ge=ge, c=c: h.activation(out=ghalo[:, 2 * c:2 * c + 2], in_=ge[:, W:W + 2], func=AF.Copy), r=[ge], w=[ghalo])
                f0 = 4 * c
                P.op('dve', lambda h, ge=ge, ac=ac, f0=f0: h.tensor_scalar(out=ac[:, 0:W], in0=ge[:, 2:2 + W], scalar1=fcv_sb[:, f0 + 2:f0 + 3], scalar2=fcv_sb[:, f0 + 3:f0 + 4], op0=ALU.mult, op1=ALU.add), r=[ge, fcv_sb], w=[ac])
                P.op('dve', lambda h, ge=ge, ac=ac, f0=f0: h.scalar_tensor_tensor(out=ac[:, 0:W], in0=ge[:, 1:1 + W], scalar=fcv_sb[:, f0 + 1:f0 + 2], in1=ac[:, 0:W], op0=ALU.mult, op1=ALU.add), r=[ge, fcv_sb, ac], w=[ac])
                P.op('dve', lambda h, ge=ge, ac=ac, f0=f0: h.scalar_tensor_tensor(out=ac[:, 0:W], in0=ge[:, 0:W], scalar=fcv_sb[:, f0:f0 + 1], in1=ac[:, 0:W], op0=ALU.mult, op1=ALU.add), r=[ge, fcv_sb, ac], w=[ac])
                P.op('act', lambda h, ac=ac, s_=s_: h.activation(out=s_[:, 0:W], in_=ac[:, 0:W], func=AF.Silu), r=[ac], w=[s_])
                P.op('dve', lambda h, s_=s_, pu=pu, c=c: h.tensor_tensor(out=R1[:, c * TB:c * TB + W], in0=s_[:, 0:W], in1=pu[:, 0:W], op=ALU.mult), r=[s_, pu], w=[aT[c]])
            if halo:
                return

            def main3(dt):
                pm = mm.next()
                for half in range(2):
                    wt = load_w(wd_t[dt, :, half * 43 * 128:(half + 1) * 43 * 128], 43 * 128)
                    for cc in range(43):
                        c = half * 43 + cc
                        P.op('pe', lambda h, cc=cc, c=c, wt=wt, pm=pm: h.matmul(pm[:, 0:W], wt[:, cc * 128:(cc + 1) * 128], R1[:, c * TB:c * TB + W], start=(c == 0), stop=(c == NFC - 1)), r=[wt, aT[c]], w=[pm])
                return pm

            def xsrc3(dt):
                return x1s[dt, :, 0:W], [x1sT[dt]]

            def out3(dt, o):
                P.dma('sp', x2T[dt, :, col0 - HW:col0 - HW + W], o[:, 0:W], r=[o])

            ln_sublayer(W, main3, xsrc3, 5, 2, 3, out3)

        if with_halo:
            block(0, 2, True)
        else:
            P.op('dve', lambda h: h.memset(ghalo[:], 0.0), w=[ghalo])
        for b in range(NT // TB):
            block(HW + b * TB, TB, False)
        P.emit()
        print("progC: ops", {e: len(P.ops[e]) for e in ENGS}, "waits", P.nwait)
    return nc


def prep_C_weights(w_o, ffn_w_in, ffn_conv_w, ffn_conv_b, ffn_w_down, ln1_g, ln1_b, ln2_g, ln2_b):
    wo_t = np.ascontiguousarray(w_o.reshape(32, 128, 32, 128).transpose(2, 1, 0, 3)).reshape(32, 128, 4096)
    wi = ffn_w_in.reshape(32, 128, 2, NFC, 128)
    wi_t = np.ascontiguousarray(wi.transpose(3, 1, 2, 0, 4)).reshape(NFC, 128, 2, 4096)
    wd = ffn_w_down.reshape(NFC, 128, 32, 128)
    wd_t = np.ascontiguousarray(wd.transpose(2, 1, 0, 3)).reshape(32, 128, NFC * 128)
    fc = np.concatenate([ffn_conv_w, ffn_conv_b[None]], axis=0)
    fcv = np.ascontiguousarray(fc.reshape(4, NFC, 128).transpose(2, 1, 0)).reshape(128, NFC * 4)
    lnp = np.stack([ln1_g, ln1_b, ln2_g, ln2_b], 0).reshape(4, 32, 128)
    lnp = np.ascontiguousarray(lnp.transpose(2, 0, 1)).reshape(128, 128)
    return dict(wo_t=wo_t, wi_t=wi_t, wd_t=wd_t, fcv=fcv, lnp=lnp)


def fm(a):
    return np.ascontiguousarray(a.T).reshape(32, 128, a.shape[0])


NTA = 2048
HALF = 1024
NFM = 56
NOUT = 48
EPS = 1e-5
C_Q, C_KC, C_VC, C_KS, C_VS, C_KW, C_VW, C_GL, C_ZG, C_ZR, C_ZU, C_ZV = 0, 2048, 2560, 3072, 3584, 4096, 4608, 5120, 5168, 6192, 7216, 8240
FM_COLS = ([C_Q + 128 * i for i in range(16)] + [C_KC + 128 * i for i in range(4)] + [C_VC + 128 * i for i in range(4)]
           + [C_KS + 128 * i for i in range(4)] + [C_KW + 128 * i for i in range(4)] + [C_ZG + 128 * i for i in range(8)]
           + [C_ZR + 128 * i for i in range(8)] + [C_ZU + 128 * i for i in range(8)])
TM_COLS = [C_VS, C_VW, C_ZV, C_ZV + 512]


def build_A(NTA=NTA):
    nc = bass.Bass("TRN2", target_bir_lowering=False)

    def din(name, shape, dt=F32):
        return nc.dram_tensor(name, shape, dt, kind="ExternalInput").ap()

    def dout(name, shape, dt=F32):
        return nc.dram_tensor(name, shape, dt, kind="ExternalOutput").ap()

    xT = din("xT", [32, 128, NTA])
    modv = din("modv", [128, 192])
    wfm = din("wfm", [NFM, 128, 4096])
    wtm = din("wtm", [4, 2, 128, 16 * 512])
    wgl = din("wgl", [128, 32 * 48])
    lngb = din("lngb", [2, 128, 1024])
    swT = din("swT", [128, 8 * 128])
    tmask = din("tmask", [128, 128])
    sbb = din("sbb", [128, 8 * 512])
    zfm = dout("zfm", [NOUT, 128, NTA])
    vsw = dout("vsw", [2, NTA // 128, 128, 512])
    glo = dout("glo", [NTA // 128, 128, 48])
    ysg = dout("ysg", [8, 128, NTA])

    with ExitStack() as es:
        P = Prog(nc, es)
        modv_sb = P.sb("modv_sb", [128, 192], F32)
        op1 = P.sb("op1", [128, 192], F32)
        lng = P.sb("lng", [128, 1024], F32)
        lnb = P.sb("lnb", [128, 1024], F32)
        sw32 = P.sb("sw32", [128, 1024], F32)
        swb = P.sb("swb", [128, 1024], BF16)
        tm = P.sb("tm", [128, 128], F32)
        sbb_sb = P.sb("sbb_sb", [128, 8 * 512], F32)
        wgl_sb = P.sb("wgl_sb", [128, 32 * 48], BF16)
        P.dma('sp', modv_sb[:], modv, w=[modv_sb])
        P.dma('sp', lng[:], lngb[0], w=[lng])
        P.dma('sp', lnb[:], lngb[1], w=[lnb])
        P.dma('sp', sw32[:], swT, w=[sw32])
        P.dma('sp', tm[:], tmask, w=[tm])
        P.dma('sp', sbb_sb[:], sbb, w=[sbb_sb])
        P.dma('pool', wgl_sb[:], wgl, w=[wgl_sb], max_dma_last_dim=4096)
        P.op('dve', lambda h: h.tensor_scalar(out=op1[:], in0=modv_sb[:], scalar1=1.0, scalar2=None, op0=ALU.add), r=[modv_sb], w=[op1])
        for g in range(8):
            P.op('dve', lambda h, g=g: h.tensor_tensor(out=swb[:, g * 128:(g + 1) * 128], in0=sw32[:, g * 128:(g + 1) * 128], in1=tm[:], op=ALU.mult), r=[sw32, tm], w=[swb])

        Q = 512
        hTt = es.enter_context(nc.sbuf_tensor("hT", [128, 32 * Q], BF16))
        hT = [T(hTt) for _ in range(32)]
        vnt = es.enter_context(nc.sbuf_tensor("vn", [128, 4 * 1024], BF16))
        vn = [T(vnt) for _ in range(4)]
        wb = Rot([P.sb("wb%d" % i, [128, 16 * 512], BF16) for i in range(3)])
        xin = Rot([P.sb("xin%d" % i, [128, Q], F32) for i in range(2)])
        ot = Rot([P.sb("ot%d" % i, [128, 512], F32) for i in range(4)])
        gvs = [P.sb("gv%d" % i, [128, 1024], F32) for i in range(4)]
        gut = P.sb("gu", [128, Q], F32)
        st6 = P.sb("st6", [128, 12], F32)
        mv = P.sb("mv", [128, 2], F32)
        rs = P.sb("rs", [128, 1], F32)
        mm = Rot([P.ps("mm%d" % i, [128, 512]) for i in range(6)])

        evi = [0]

        def evac(dst, src, r, w):
            evi[0] += 1
            if evi[0] % 2:
                P.op('act', lambda h: h.activation(out=dst, in_=src, func=AF.Copy), r=r, w=w)
            else:
                P.op('dve', lambda h: h.tensor_copy(out=dst, in_=src), r=r, w=w)

        for qi in range(NTA // Q):
            t0 = qi * Q
            for kt in range(32):
                xi = xin.next()
                P.dma('sp', xi[:], xT[kt, :, t0:t0 + Q], w=[xi])
                P.op('act', lambda h, xi=xi, kt=kt: h.activation(out=hTt[:, kt * Q:(kt + 1) * Q], in_=xi[:], func=AF.Identity, scale=op1[:, 32 + kt:33 + kt], bias=modv_sb[:, kt:kt + 1]), r=[xi, op1, modv_sb], w=[hT[kt]])
            for gi in (2, 3, 0, 1):
                wh = []
                for kh in range(2):
                    wt = wb.next()
                    P.dma('pool', wt[:], wtm[gi, kh], w=[wt], max_dma_last_dim=4096)
                    wh.append(wt)
                for ch in range(4):
                    pm = mm.next()
                    for kt in range(32):
                        wt = wh[kt // 16]
                        kk = kt % 16
                        P.op('pe', lambda h, pm=pm, kt=kt, wt=wt, kk=kk, ch=ch: h.matmul(pm[:], hTt[:, kt * Q + ch * 128:kt * Q + (ch + 1) * 128], wt[:, kk * 512:(kk + 1) * 512], start=(kt == 0), stop=(kt == 31)), r=[hT[kt], wt], w=[pm])
                    if gi < 2:
                        o = ot.next()
                        evac(o[:], pm[:], [pm], [o])
                        P.dma('sp', vsw[gi, qi * 4 + ch], o[:], r=[o])
                    else:
                        gv = gvs[ch]
                        P.op('act', lambda h, gv=gv, pm=pm, gi=gi: h.activation(out=gv[:, (gi - 2) * 512:(gi - 1) * 512], in_=pm[:], func=AF.Gelu_apprx_tanh), r=[pm], w=[gv])
                        if gi == 3:
                            P.op('dve', lambda h, gv=gv: h.bn_stats(out=st6[:, 0:6], in_=gv[:, 0:512]), r=[gv], w=[st6])
                            P.op('dve', lambda h, gv=gv: h.bn_stats(out=st6[:, 6:12], in_=gv[:, 512:1024]), r=[gv, st6], w=[st6])
                            P.op('dve', lambda h: h.bn_aggr(out=mv[:], in_=st6[:]), r=[st6], w=[mv])
                            P.op('dve', lambda h: h.tensor_scalar(out=rs[:], in0=mv[:, 1:2], scalar1=EPS, scalar2=None, op0=ALU.add), r=[mv], w=[rs])
                            P.op('act', lambda h: h.activation(out=rs[:], in_=rs[:], func=AF.Sqrt), r=[rs], w=[rs])
                            P.op('dve', lambda h: h.reciprocal(out=rs[:], in_=rs[:]), r=[rs], w=[rs])
                            P.op('dve', lambda h, gv=gv: h.tensor_scalar(out=gv[:], in0=gv[:], scalar1=mv[:, 0:1], scalar2=rs[:, 0:1], op0=ALU.subtract, op1=ALU.mult), r=[gv, mv, rs], w=[gv])
                            P.op('dve', lambda h, gv=gv: h.tensor_tensor(out=gv[:], in0=gv[:], in1=lng[:], op=ALU.mult), r=[gv, lng], w=[gv])
                            P.op('dve', lambda h, gv=gv, ch=ch: h.tensor_tensor(out=vnt[:, ch * 1024:(ch + 1) * 1024], in0=gv[:], in1=lnb[:], op=ALU.add), r=[gv, lnb], w=[vn[ch]])
            for ch in range(4):
                pm = mm.next()
                for kt in range(32):
                    P.op('pe', lambda h, pm=pm, kt=kt, ch=ch: h.matmul(pm[:, 0:48], hTt[:, kt * Q + ch * 128:kt * Q + (ch + 1) * 128], wgl_sb[:, kt * 48:(kt + 1) * 48], start=(kt == 0), stop=(kt == 31)), r=[hT[kt], wgl_sb], w=[pm])
                o = ot.next()
                evac(o[:, 0:48], pm[:, 0:48], [pm], [o])
                P.dma('sp', glo[qi * 4 + ch], o[:, 0:48], r=[o])
            for j in range(NFM):
                wt = wb.next()
                P.dma('pool', wt[:, 0:4096], wfm[j], w=[wt], max_dma_last_dim=4096)
                pm = mm.next()
                for kt in range(32):
                    P.op('pe', lambda h, pm=pm, kt=kt, wt=wt: h.matmul(pm[:], wt[:, kt * 128:(kt + 1) * 128], hTt[:, kt * Q:(kt + 1) * Q], start=(kt == 0), stop=(kt == 31)), r=[hT[kt], wt], w=[pm])
                if j < NOUT:
                    o = ot.next()
                    evac(o[:], pm[:], [pm], [o])
                    P.dma('sp', zfm[j, :, t0:t0 + Q], o[:], r=[o])
                else:
                    g = j - NOUT
                    P.op('act', lambda h, pm=pm: h.activation(out=gut[:], in_=pm[:], func=AF.Gelu_apprx_tanh), r=[pm], w=[gut])
                    pm2 = mm.next()
                    for ch in range(4):
                        P.op('pe', lambda h, pm2=pm2, ch=ch, g=g: h.matmul(pm2[:, ch * 128:(ch + 1) * 128], vnt[:, ch * 1024 + g * 128:ch * 1024 + (g + 1) * 128], swb[:, g * 128:(g + 1) * 128], start=True, stop=True), r=[vn[ch], swb], w=[pm2])
                    o = ot.next()
                    P.op('dve', lambda h, o=o, pm2=pm2, g=g: h.tensor_tensor(out=o[:], in0=pm2[:], in1=sbb_sb[:, g * 512:(g + 1) * 512], op=ALU.add), r=[pm2, sbb_sb], w=[o])
                    P.op('dve', lambda h, o=o: h.tensor_tensor(out=o[:], in0=o[:], in1=gut[:], op=ALU.mult), r=[o, gut], w=[o])
                    P.dma('sp', ysg[g, :, t0:t0 + Q], o[:], r=[o])
        P.emit()
        print("progA: ops", {e: len(P.ops[e]) for e in ENGS}, "waits", P.nwait)
    return nc


def prep_A_weights(w_in, sgu_ln_g, sgu_ln_b, sgu_w, sgu_b):
    w4 = w_in.reshape(32, 128, -1)
    wfm = np.empty((NFM, 128, 32, 128), np.float32)
    for j, c0 in enumerate(FM_COLS):
        wfm[j] = w4[:, :, c0:c0 + 128].transpose(1, 0, 2)
    wfm = wfm.reshape(NFM, 128, 4096)
    wtm = np.empty((4, 2, 128, 16, 512), np.float32)
    for gi, c0 in enumerate(TM_COLS):
        blk = w4[:, :, c0:c0 + 512].reshape(2, 16, 128, 512)
        wtm[gi] = blk.transpose(0, 2, 1, 3)
    wtm = wtm.reshape(4, 2, 128, 16 * 512)
    wgl = np.ascontiguousarray(w4[:, :, C_GL:C_GL + 48].transpose(1, 0, 2)).reshape(128, 32 * 48)
    lngb = np.stack([np.broadcast_to(sgu_ln_g, (128, 1024)), np.broadcast_to(sgu_ln_b, (128, 1024))]).astype(np.float32)
    swT = np.ascontiguousarray(sgu_w.transpose(2, 0, 1)).reshape(128, 8 * 128)
    tmask = (np.arange(128)[None, :] >= np.arange(128)[:, None]).astype(np.float32)
    sbb = np.broadcast_to(np.tile(sgu_b[:, None, :], (1, 4, 1)).reshape(1, 8 * 512), (128, 8 * 512)).astype(np.float32)
    return dict(wfm=wfm, wtm=wtm, wgl=wgl, lngb=np.ascontiguousarray(lngb), swT=swT, tmask=tmask, sbb=np.ascontiguousarray(sbb))


S = 8192
NQB = 128
SCALE = 128 ** -0.5
NEG = -1e30
FORCE = 1e4
DELTAS = {0: 0, 64: 1, 512: 2, 576: 3}


def build_B(qblocks=None, do_lru=True):
    nc = bass.Bass("TRN2", target_bir_lowering=False)
    if qblocks is None:
        qblocks = list(range(NQB))

    def din(name, shape, dt=F32):
        return nc.dram_tensor(name, shape, dt, kind="ExternalInput").ap()

    def dout(name, shape, dt=F32):
        return nc.dram_tensor(name, shape, dt, kind="ExternalOutput").ap()

    qT_d = din("qT", [4, 128, S])
    kT_d = din("kT", [4, 128, S])
    vtm_d = din("vtm", [2, 128, 64, 128])
    glr_d = din("glr", [128, NQB * 6])
    lru_d = din("lruin", [2, 2, 128, S])
    cw1_d = din("cw1", [2, 128, 32 * 128])
    cpos_d = din("cpos", [2, 128, 32])
    cb1_d = din("cb1", [128, 2])
    cw2_d = din("cw2", [2, 128, 128])
    lcv_d = din("lcv", [128, 10])
    lw_d = din("lw", [2, 2, 128, 128])
    lbl_d = din("lbl", [128, 6])
    m4_d = din("m4", [128, 4 * 64])
    tc_d = din("tcc", [128, 64])
    cov_d = din("cov", [128, 4 * 128])
    ex_d = din("ex", [128, S])
    fold_d = din("fold", [128, 64])
    id_d = din("ident", [64, 64])
    oatt = dout("oatt", [2 * NQB, 128, 128])
    ylru = dout("ylru", [2, 128, S])

    with ExitStack() as es0:
        P = Prog(nc, es0)
        if do_lru:
            with ExitStack() as es:
                P.es = es
                TC = 2048
                lcv = P.sb("lcv", [128, 10], F32)
                lbl = P.sb("lbl", [128, 6], F32)
                lwb = P.sb("lwb", [128, 4 * 128], BF16)
                sc = P.sb("sc", [128, 4], F32)
                P.dma('sp', lcv[:], lcv_d, w=[lcv])
                P.dma('sp', lbl[:], lbl_d, w=[lbl])
                for blk in range(2):
                    for ax in range(2):
                        P.dma('pool', lwb[:, (blk * 2 + ax) * 128:(blk * 2 + ax + 1) * 128], lw_d[blk, ax], w=[lwb])
                tmp1 = P.sb("tmp1", [128, 2], F32)
                for blk in range(2):
                    P.op('act', lambda h, blk=blk: h.activation(out=tmp1[:, 0:1], in_=lbl[:, blk * 3 + 2:blk * 3 + 3], func=AF.Exp, scale=-1.0), r=[lbl], w=[tmp1])
                    P.op('act', lambda h: h.activation(out=tmp1[:, 1:2], in_=tmp1[:, 0:1], func=AF.Ln, bias=1.0), r=[tmp1], w=[tmp1])
                    P.op('dve', lambda h, blk=blk: h.tensor_scalar(out=sc[:, 2 * blk:2 * blk + 1], in0=tmp1[:, 1:2], scalar1=-8.0, scalar2=None, op0=ALU.mult), r=[tmp1], w=[sc])
                    P.op('dve', lambda h, blk=blk: h.tensor_scalar(out=sc[:, 2 * blk + 1:2 * blk + 2], in0=tmp1[:, 1:2], scalar1=-16.0, scalar2=None, op0=ALU.mult), r=[tmp1], w=[sc])
                zr = Rot([P.sb("zr%d" % i, [128, TC + 3], F32) for i in range(2)])
                zg = Rot([P.sb("zg%d" % i, [128, TC], F32) for i in range(2)])
                xr = P.sb("xr", [128, TC], F32)
                xrb = P.sb("xrb", [128, TC], BF16)
                rr = P.sb("rr", [128, TC], F32)
                ig = P.sb("ig", [128, TC], F32)
                aa = P.sb("aa", [128, TC], F32)
                bb = P.sb("bb", [128, TC], F32)
                hh = Rot([P.sb("hh%d" % i, [128, TC], F32) for i in range(2)])
                carry = P.sb("carry", [128, 1], F32)
                pl = Rot([P.ps("pl%d" % i, [128, 512]) for i in range(4)])
                for blk in range(2):
                    P.op('dve', lambda h: h.memset(carry[:], 0.0), w=[carry])
                    for tci in range(S // TC):
                        t0 = tci * TC
                        z = zr.next()
                        g_ = zg.next()
                        if tci == 0:
                            P.op('dve', lambda h, z=z: h.memset(z[:, 0:3], 0.0), w=[z])
                            P.dma('sp', z[:, 3:3 + TC], lru_d[1, blk, :, 0:TC], w=[z])
                        else:
                            P.dma('sp', z[:], lru_d[1, blk, :, t0 - 3:t0 + TC], w=[z])
                        P.dma('sp', g_[:], lru_d[0, blk, :, t0:t0 + TC], w=[g_])
                        c0 = blk * 5
                        P.op('dve', lambda h, z=z, c0=c0: h.tensor_scalar(out=xr[:], in0=z[:, 3:3 + TC], scalar1=lcv[:, c0 + 3:c0 + 4], scalar2=lcv[:, c0 + 4:c0 + 5], op0=ALU.mult, op1=ALU.add), r=[z, lcv], w=[xr])
                        for k in range(3):
                            P.op('dve', lambda h, z=z, c0=c0, k=k: h.scalar_tensor_tensor(out=xr[:], in0=z[:, k:k + TC], scalar=lcv[:, c0 + k:c0 + k + 1], in1=xr[:], op0=ALU.mult, op1=ALU.add), r=[z, lcv, xr], w=[xr])
                        P.op('act', lambda h: h.activation(out=xrb[:], in_=xr[:], func=AF.Copy), r=[xr], w=[xrb])
                        for ax, dst in ((0, rr), (1, ig)):
                            for n4 in range(TC // 512):
                                pm = pl.next()
                                P.op('pe', lambda h, pm=pm, ax=ax, n4=n4, blk=blk: h.matmul(pm[:], lwb[:, (blk * 2 + ax) * 128:(blk * 2 + ax + 1) * 128], xrb[:, n4 * 512:(n4 + 1) * 512], start=True, stop=True), r=[lwb, xrb], w=[pm])
                                P.op('act', lambda h, pm=pm, dst=dst, n4=n4, ax=ax, blk=blk: h.activation(out=dst[:, n4 * 512:(n4 + 1) * 512], in_=pm[:], func=AF.Sigmoid, bias=lbl[:, blk * 3 + ax:blk * 3 + ax + 1]), r=[pm, lbl], w=[dst])
                        P.op('act', lambda h, blk=blk: h.activation(out=aa[:], in_=rr[:], func=AF.Exp, scale=sc[:, 2 * blk:2 * blk + 1]), r=[rr, sc], w=[aa])
                        P.op('act', lambda h, blk=blk: h.activation(out=bb[:], in_=rr[:], func=AF.Exp, scale=sc[:, 2 * blk + 1:2 * blk + 2]), r=[rr, sc], w=[bb])
                        P.op('act', lambda h: h.activation(out=bb[:], in_=bb[:], func=AF.Sqrt, scale=-1.0, bias=1.0), r=[bb], w=[bb])
                        P.op('dve', lambda h: h.tensor_tensor(out=ig[:], in0=ig[:], in1=xr[:], op=ALU.mult), r=[ig, xr], w=[ig])
                        P.op('dve', lambda h: h.tensor_tensor(out=bb[:], in0=bb[:], in1=ig[:], op=ALU.mult), r=[bb, ig], w=[bb])
                        hcur = hh.next()
                        P.op('dve', lambda h, hcur=hcur: h.tensor_tensor_scan(out=hcur[:], data0=aa[:], data1=bb[:], initial=carry[:, 0:1], op0=ALU.mult, op1=ALU.add), r=[aa, bb, carry], w=[hcur])
                        P.op('dve', lambda h, hcur=hcur: h.tensor_copy(out=carry[:], in_=hcur[:, TC - 1:TC]), r=[hcur], w=[carry])
                        P.op('act', lambda h, g_=g_: h.activation(out=g_[:], in_=g_[:], func=AF.Gelu_apprx_tanh), r=[g_], w=[g_])
                        P.op('dve', lambda h, hcur=hcur, g_=g_: h.tensor_tensor(out=hcur[:], in0=hcur[:], in1=g_[:], op=ALU.mult), r=[hcur, g_], w=[hcur])
                        P.dma('sp', ylru[blk, :, t0:t0 + TC], hcur[:], r=[hcur])
                P.emit()

        with ExitStack() as es:
            P.es = es
            qT = P.sb("qT", [128, 4, S], BF16)
            ksT = P.sb("ksT", [128, S], BF16)
            kwT = P.sb("kwT", [128, S], BF16)
            vsA = P.sb("vsA", [128, 64, 129], BF16)
            vwA = P.sb("vwA", [128, 64, 129], BF16)
            ex = P.sb("ex", [128, S], BF16)
            kcT = P.sb("kcT", [128, 512], BF16)
            VCA = P.sb("VCA", [128, 4, 257], BF16)
            m4 = P.sb("m4", [128, 4 * 64], F32)
            tcc = P.sb("tcc", [128, 64], F32)
            fold = P.sb("fold", [128, 64], F32)
            ident = P.sb("ident", [64, 64], F32)
            sig = P.sb("sig", [128, NQB * 6], F32)
            for hd in range(4):
                P.dma('pool', qT[:, hd, :], qT_d[hd], w=[qT], max_dma_last_dim=4096)
            P.dma('pool', ksT[:], kT_d[2], w=[ksT], max_dma_last_dim=4096)
            P.dma('pool', kwT[:], kT_d[3], w=[kwT], max_dma_last_dim=4096)
            P.dma('pool', ex[:], ex_d, w=[ex], max_dma_last_dim=4096)
            P.op('dve', lambda h: h.memset(vsA[:, :, 128:129], 1.0), w=[vsA])
            P.op('dve', lambda h: h.memset(vwA[:, :, 128:129], 1.0), w=[vwA])
            for half in range(2):
                P.dma('pool', vsA[:, half * 32:(half + 1) * 32, 0:128], vtm_d[0, :, half * 32:(half + 1) * 32, :], w=[vsA], max_dma_last_dim=512)
                P.dma('pool', vwA[:, half * 32:(half + 1) * 32, 0:128], vtm_d[1, :, half * 32:(half + 1) * 32, :], w=[vwA], max_dma_last_dim=512)
            P.dma('sp', m4[:], m4_d, w=[m4])
            P.dma('sp', tcc[:], tc_d, w=[tcc])
            P.dma('sp', fold[:], fold_d, w=[fold])
            P.dma('sp', ident[:], id_d, w=[ident])
            P.dma('sp', sig[:], glr_d, w=[sig])
            P.op('act', lambda h: h.activation(out=sig[:], in_=sig[:], func=AF.Sigmoid), r=[sig], w=[sig])
            P.op('dve', lambda h: h.memset(VCA[:, :, 256:257], 1.0), w=[VCA])
            for ct in range(4):
                P.dma('pool', VCA[:, ct, 128:256], cov_d[:, ct * 128:(ct + 1) * 128], w=[VCA], max_dma_last_dim=512)

            pS = Rot([P.ps("pS%d" % i, [128, 512]) for i in range(2)])
            pMA = P.ps("pMA", [128, 512])
            pMB = P.ps("pMB", [128, 512])
            pMs = Rot([(pMA, 0), (pMB, 0)])
            pU = Rot([P.ps("pU%d" % i, [128, 512]) for i in range(4)])
            misc = pMA

            if True:
                kr = P.sb("kr", [128, S], BF16)
                w1 = P.sb("w1", [128, 32 * 128], BF16)
                w2 = P.sb("w2", [128, 128], BF16)
                posb = P.sb("posb", [128, 32], BF16)
                cb1 = P.sb("cb1", [128, 2], F32)
                cbias = P.sb("cbias", [128, 1], F32)
                gT = P.sb("gT", [128, 512], BF16)
                P.dma('sp', cb1[:], cb1_d, w=[cb1])
                for kv in range(2):
                    P.dma('pool', kr[:], kT_d[kv], w=[kr], max_dma_last_dim=4096)
                    P.dma('pool', w1[:], cw1_d[kv], w=[w1], max_dma_last_dim=4096)
                    P.dma('pool', w2[:], cw2_d[kv], w=[w2])
                    P.dma('pool', posb[:], cpos_d[kv], w=[posb])
                    for p in range(32):
                        P.op('pe', lambda h, p=p: h.matmul(misc[:, 0:1], w1[:, p * 128:(p + 1) * 128], posb[:, p:p + 1], start=(p == 0), stop=(p == 31)), r=[w1, posb], w=[misc])
                    P.op('dve', lambda h, kv=kv: h.tensor_tensor(out=cbias[:], in0=misc[:, 0:1], in1=cb1[:, kv:kv + 1], op=ALU.add), r=[misc, cb1], w=[cbias])
                    pm = pS.next()
                    for p in range(32):
                        P.op('pe', lambda h, p=p, pm=pm: h.matmul(pm[:, 0:511], w1[:, p * 128:(p + 1) * 128], kr[:, p:p + 16 * 510 + 1:16], start=(p == 0), stop=(p == 31)), r=[w1, kr], w=[pm])
                    P.op('dve', lambda h: h.memset(gT[:], 0.0), w=[gT])
                    P.op('act', lambda h, pm=pm: h.activation(out=gT[:, 0:511], in_=pm[:, 0:511], func=AF.Gelu_apprx_tanh, bias=cbias[:, 0:1]), r=[pm, cbias], w=[gT])
                    if kv == 0:
                        pm2 = pS.next()
                        P.op('pe', lambda h, pm2=pm2: h.matmul(pm2[:], w2[:], gT[:], start=True, stop=True), r=[w2, gT], w=[pm2])
                        P.op('act', lambda h, pm2=pm2: h.activation(out=kcT[:], in_=pm2[:], func=AF.Copy), r=[pm2], w=[kcT])
                    else:
                        pm2 = pS.next()
                        for ct in range(4):
                            P.op('pe', lambda h, pm2=pm2, ct=ct: h.matmul(pm2[:, ct * 128:(ct + 1) * 128], gT[:, ct * 128:(ct + 1) * 128], w2[:], start=True, stop=True), r=[w2, gT], w=[pm2])
                        for ct in range(4):
                            P.op('act', lambda h, pm2=pm2, ct=ct: h.activation(out=VCA[:, ct, 0:128], in_=pm2[:, ct * 128:(ct + 1) * 128], func=AF.Copy), r=[pm2], w=[VCA])

            Et = Rot([P.sb("E%d" % i, [128, 256], F32) for i in range(3)])
            Pt = Rot([P.sb("P%d" % i, [128, 256], BF16) for i in range(4)])
            mk = Rot([P.sb("mk%d" % i, [128, 64], F32) for i in range(3)])
            impn = [P.sb("impn%d" % i, [128, 128], F32) for i in range(2)]
            accs = Rot([P.sb("acc%d" % i, [128, 128], F32) for i in range(4)])
            imp2 = P.sb("imp2", [64, 128], F32)
            imp3 = P.sb("imp3", [64, 128], F32)
            m8 = P.sb("m8", [64, 16], F32)
            sel = P.sb("sel", [64, 128], F32)
            selT = P.sb("selT", [128, 64], BF16)
            sm = Rot([P.sb("sm%d" % i, [128, 2], F32) for i in range(4)])
            P.op('dve', lambda h: h.memset(imp2[:], NEG), w=[imp2])

            pend = [None]

            def tile(i, Kap, Ktr, mask, Vap, Vtr, U, ncol, first, last):
                ps = pS.next()
                P.op('pe', lambda h: h.matmul(ps[:, 0:256], Kap, qT[:, :, i * 64:(i + 1) * 64], start=True, stop=True), r=[Ktr, qT], w=[ps])
                if pend[0] is not None:
                    pend[0]()
                    pend[0] = None
                pt = Pt.next()
                if mask is None:
                    P.op('act', lambda h: h.activation(out=pt[:], in_=ps[:, 0:256], func=AF.Exp, scale=SCALE), r=[ps], w=[pt])
                else:
                    e = Et.next()
                    P.op('act', lambda h: h.activation(out=e[:], in_=ps[:, 0:256], func=AF.Exp, scale=SCALE), r=[ps], w=[e])
                    kind = mask[0]
                    if kind == 'const':
                        mt, map_ = m4, m4[:, mask[1] * 64:(mask[1] + 1) * 64]
                    elif kind == 'c':
                        mt = mk.next()
                        th = float(mask[1])
                        P.op('dve', lambda h: h.tensor_scalar(out=mt[:], in0=tcc[:], scalar1=th, scalar2=None, op0=ALU.is_le), r=[tcc], w=[mt])
                        map_ = mt[:]
                    else:
                        kt = mask[1]
                        pM, po = pMs.next()
                        P.op('pe', lambda h: h.matmul(pM[:, po:po + 64], ex[:, kt * 128:(kt + 1) * 128], selT[:], start=True, stop=True), r=[ex, selT], w=[pM])
                        if mask[2] is None:
                            mt, map_ = pM, pM[:, po:po + 64]
                        else:
                            mt = mk.next()
                            di = mask[2]
                            P.op('dve', lambda h: h.tensor_tensor(out=mt[:], in0=pM[:, po:po + 64], in1=m4[:, di * 64:(di + 1) * 64], op=ALU.mult), r=[pM, m4], w=[mt])
                            map_ = mt[:]
                    P.op('dve', lambda h: h.tensor_tensor(out=pt[:].rearrange("p (h q) -> p h q", h=4), in0=e[:].rearrange("p (h q) -> p h q", h=4), in1=map_[:, None, :].broadcast_to([128, 4, 64]), op=ALU.mult), r=[e, mt], w=[pt])

                def pv():
                    for rt in range(2):
                        P.op('pe', lambda h, rt=rt: h.matmul(U[rt][:, 0:ncol], pt[:, rt * 128:(rt + 1) * 128], Vap, start=first, stop=last), r=[pt, Vtr], w=[U[rt]])
                pend[0] = pv

            def flush():
                if pend[0] is not None:
                    pend[0]()
                    pend[0] = None

            def do_qblock(i):
                gbase = i * 6
                Uc = [pU.next(), pU.next()]
                nct = (4 * i + 2) // 128 + 1
                for ct in range(nct):
                    th = 64 * i - 31 - 2048 * ct
                    tile(i, kcT[:, ct * 128:(ct + 1) * 128], kcT, ('c', th), VCA[:, ct, :], VCA, Uc, 257, ct == 0, ct == nct - 1)
                flush()
                acc = [accs.next(), accs.next()]
                for rt in range(2):
                    s_ = sm.next()
                    P.op('dve', lambda h, s_=s_, rt=rt: h.tensor_scalar(out=s_[:, 0:1], in0=Uc[rt][:, 256:257], scalar1=1e-30, scalar2=None, op0=ALU.max), r=[Uc[rt]], w=[s_])
                    P.op('dve', lambda h, s_=s_: h.reciprocal(out=s_[:, 0:1], in_=s_[:, 0:1]), r=[s_], w=[s_])
                    P.op('dve', lambda h, s_=s_, rt=rt: h.tensor_scalar(out=impn[rt][:], in0=Uc[rt][:, 128:256], scalar1=s_[:, 0:1], scalar2=None, op0=ALU.mult), r=[Uc[rt], s_], w=[impn[rt]])
                    P.op('dve', lambda h, s_=s_, rt=rt: h.tensor_tensor(out=s_[:, 1:2], in0=s_[:, 0:1], in1=sig[:, gbase + rt * 3:gbase + rt * 3 + 1], op=ALU.mult), r=[s_, sig], w=[s_])
                    P.op('dve', lambda h, s_=s_, rt=rt: h.tensor_scalar(out=acc[rt][:], in0=Uc[rt][:, 0:128], scalar1=s_[:, 1:2], scalar2=None, op0=ALU.mult), r=[Uc[rt], s_], w=[acc[rt]])
                if i >= 16:
                    for rt in range(2):
                        P.op('pe', lambda h, rt=rt: h.matmul(misc[0:64, 0:128], fold[:], impn[rt][:], start=(rt == 0), stop=(rt == 1)), r=[fold, impn[rt]], w=[misc])
                    P.op('dve', lambda h: h.tensor_copy(out=imp2[:, 0:i + 1], in_=misc[0:64, 0:i + 1]), r=[misc], w=[imp2])
                    P.op('dve', lambda h: h.memset(imp2[:, 0:1], FORCE), r=[], w=[imp2])
                    P.op('dve', lambda h: h.memset(imp2[:, i - 1:i + 1], FORCE), r=[], w=[imp2])
                    P.op('dve', lambda h: h.max(out=m8[:, 0:8], in_=imp2[:]), r=[imp2], w=[m8])
                    P.op('dve', lambda h: h.match_replace(out=imp3[:], in_to_replace=m8[:, 0:8], in_values=imp2[:], imm_value=-3e38), r=[imp2, m8], w=[imp3])
                    P.op('dve', lambda h: h.max(out=m8[:, 8:16], in_=imp3[:]), r=[imp3], w=[m8])
                    P.op('dve', lambda h: h.tensor_scalar(out=sel[:], in0=imp2[:], scalar1=m8[:, 15:16], scalar2=None, op0=ALU.is_ge), r=[imp2, m8], w=[sel])
                else:
                    P.op('dve', lambda h: h.memset(sel[:], 0.0), w=[sel])
                    P.op('dve', lambda h: h.memset(sel[:, 0:i + 1], 1.0), w=[sel])
                P.op('pe', lambda h: h.transpose(misc[:, 256:320], sel[:], ident[:]), r=[sel, ident], w=[misc])
                P.op('act', lambda h: h.activation(out=selT[:], in_=misc[:, 256:320], func=AF.Copy), r=[misc], w=[selT])
                Uw = [pU.next(), pU.next()]
                kt_lo = max(0, (64 * i - 511) // 128)
                kts = list(range(kt_lo, i // 2 + 1))
                for n, kt in enumerate(kts):
                    dl = 64 * i - 128 * kt
                    mask = ('const', DELTAS[dl]) if dl in DELTAS else None
                    tile(i, kwT[:, kt * 128:(kt + 1) * 128], kwT, mask, vwA[:, kt, :], vwA, Uw, 129, n == 0, n == len(kts) - 1)
                flush()

                def fin(U, col):
                    for rt in range(2):
                        s_ = sm.next()
                        P.op('dve', lambda h, s_=s_, rt=rt: h.reciprocal(out=s_[:, 0:1], in_=U[rt][:, 128:129]), r=[U[rt]], w=[s_])
                        P.op('dve', lambda h, s_=s_, rt=rt: h.tensor_tensor(out=s_[:, 1:2], in0=s_[:, 0:1], in1=sig[:, gbase + rt * 3 + col:gbase + rt * 3 + col + 1], op=ALU.mult), r=[s_, sig], w=[s_])
                        P.op('dve', lambda h, s_=s_, rt=rt: h.scalar_tensor_tensor(out=acc[rt][:], in0=U[rt][:, 0:128], scalar=s_[:, 1:2], in1=acc[rt][:], op0=ALU.mult, op1=ALU.add), r=[U[rt], s_, acc[rt]], w=[acc[rt]])
                fin(Uw, 2)
                Us = [pU.next(), pU.next()]
                kts = list(range(0, i // 2 + 1))
                for n, kt in enumerate(kts):
                    dl = 64 * i - 128 * kt
                    mask = ('sel', kt, DELTAS[dl] if dl in (0, 64) else None)
                    tile(i, ksT[:, kt * 128:(kt + 1) * 128], ksT, mask, vsA[:, kt, :], vsA, Us, 129, n == 0, n == len(kts) - 1)
                flush()
                fin(Us, 1)
                for rt in range(2):
                    P.dma('sp', oatt[i * 2 + rt], acc[rt][:], r=[acc[rt]])
            for i in qblocks:
                do_qblock(i)
            P.emit()
        print("progB: ops", P.total, "waits", P.nwait)
    return nc


def consts_B():
    kl = np.arange(128)[:, None]
    r = np.arange(64)[None, :]
    m4 = []
    for dl in (0, 64, 512, 576):
        m4.append(((kl - r <= dl) & (kl - r > dl - 512)).astype(np.float32))
    m4 = np.concatenate(m4, axis=1)
    tcc = (16 * kl - r).astype(np.float32)
    c = np.arange(512)[:, None]
    j = np.arange(128)[None, :]
    cover = ((16 * c < 64 * j + 64) & (16 * c + 32 > 64 * j)).astype(np.float32)
    cov = np.ascontiguousarray(cover.reshape(4, 128, 128).transpose(1, 0, 2)).reshape(128, 512)
    ex = (np.arange(S)[None, :] // 64 == np.arange(128)[:, None]).astype(np.float32)
    fold = np.tile(np.eye(64, dtype=np.float32), (2, 1))
    ident = np.eye(64, dtype=np.float32)
    return dict(m4=m4, tcc=tcc, cov=cov, ex=ex, fold=fold, ident=ident)


def prep_B_params(g, cmp_pos, cmp_w1, cmp_b1, cmp_w2, lru_conv_w, lru_conv_b, lru_wa, lru_ba, lru_wx, lru_bx, lru_lambda):
    cw1 = np.ascontiguousarray(cmp_w1.reshape(2, 32, 128, 128).transpose(0, 2, 1, 3)).reshape(2, 128, 32 * 128)
    cpos = np.ascontiguousarray(cmp_pos.transpose(0, 2, 1))
    cb1 = np.ascontiguousarray(cmp_b1.T)
    blks = [2 * g, 2 * g + 1]
    lcv = np.zeros((128, 2, 5), np.float32)
    lbl = np.zeros((128, 2, 3), np.float32)
    lw = np.zeros((2, 2, 128, 128), np.float32)
    for bi, n in enumerate(blks):
        sl = slice(n * 128, (n + 1) * 128)
        lcv[:, bi, 0:4] = lru_conv_w[:, sl].T
        lcv[:, bi, 4] = lru_conv_b[sl]
        lbl[:, bi, 0] = lru_ba[sl]
        lbl[:, bi, 1] = lru_bx[sl]
        lbl[:, bi, 2] = lru_lambda[sl]
        lw[bi, 0] = lru_wa[n]
        lw[bi, 1] = lru_wx[n]
    return dict(cw1=cw1, cpos=cpos, cb1=cb1, cw2=np.ascontiguousarray(cmp_w2), lcv=lcv.reshape(128, 10), lw=lw, lbl=lbl.reshape(128, 6))


MCT = 24


def build_M():
    nc = bass.Bass("TRN2", target_bir_lowering=False)
    cT_d = nc.dram_tensor("cT", [128, 64], F32, kind="ExternalInput").ap()
    wm_d = nc.dram_tensor("wm", [2 * MCT, 128, 4096], F32, kind="ExternalInput").ap()
    bm_d = nc.dram_tensor("bm", [128, 2 * MCT], F32, kind="ExternalInput").ap()
    mo_d = nc.dram_tensor("mo", [2 * MCT, 128, 2], F32, kind="ExternalOutput").ap()
    with ExitStack() as es:
        P = Prog(nc, es)
        cs = P.sb("cs", [128, 64], F32)
        bm = P.sb("bm", [128, 2 * MCT], F32)
        P.dma('sp', cs[:], cT_d, w=[cs])
        P.dma('sp', bm[:], bm_d, w=[bm])
        P.op('act', lambda h: h.activation(out=cs[:], in_=cs[:], func=AF.Silu), r=[cs], w=[cs])
        wt = Rot([P.sb("wt%d" % i, [128, 4096], F32) for i in range(3)])
        ot = Rot([P.sb("ot%d" % i, [128, 2], F32) for i in range(3)])
        pm = Rot([P.ps("pm%d" % i, [128, 512]) for i in range(3)])
        for j in range(2 * MCT):
            w = wt.next()
            P.dma('sp', w[:], wm_d[j], w=[w])
            p = pm.next()
            for kt in range(32):
                P.op('pe', lambda h, w=w, p=p, kt=kt: h.matmul(p[:, 0:2], w[:, kt * 128:(kt + 1) * 128], cs[:, 2 * kt:2 * kt + 2], start=(kt == 0), stop=(kt == 31)), r=[w, cs], w=[p])
            o = ot.next()
            P.op('act', lambda h, o=o, p=p, j=j: h.activation(out=o[:], in_=p[:, 0:2], func=AF.Identity, bias=bm[:, j:j + 1]), r=[p, bm], w=[o])
            P.dma('sp', mo_d[j], o[:], r=[o])
        P.emit()
    return nc


def run_M(c, w_mod, b_mod):
    nc = build_M()
    cT = np.ascontiguousarray(c.T.reshape(32, 128, 2).transpose(1, 0, 2)).reshape(128, 64)
    in_maps = []
    for core in range(8):
        c0 = core * MCT * 128
        wm = np.empty((2, MCT, 128, 32, 128), np.float32)
        bm = np.empty((128, 2, MCT), np.float32)
        for l in range(2):
            blk = w_mod[l][:, c0:c0 + MCT * 128].reshape(32, 128, MCT, 128)
            wm[l] = blk.transpose(2, 1, 0, 3)
            bm[:, l, :] = b_mod[l][c0:c0 + MCT * 128].reshape(MCT, 128).T
        in_maps.append(dict(cT=cT, wm=wm.reshape(2 * MCT, 128, 4096), bm=bm.reshape(128, 2 * MCT)))
    res = run_bass_kernel_spmd(nc, in_maps, core_ids=list(range(8)))
    mod = np.empty((2, 2, 24576), np.float32)
    for core in range(8):
        mo = res.results[core]["mo"].reshape(2, MCT, 128, 2)
        c0 = core * MCT * 128
        for l in range(2):
            mod[l][:, c0:c0 + MCT * 128] = mo[l].transpose(2, 0, 1).reshape(2, MCT * 128)
    return mod

import time as _time

_PROGS = {}


def _get(name, fn):
    if name not in _PROGS:
        _PROGS[name] = fn()
    return _PROGS[name]


def _modv(m):
    return np.ascontiguousarray(m.reshape(6, 32, 128).transpose(2, 0, 1)).reshape(128, 192)


def kernel(x, c, w_mod, b_mod, w_in, cmp_pos, cmp_w1, cmp_b1, cmp_w2, lru_conv_w, lru_conv_b,
           lru_wa, lru_ba, lru_wx, lru_bx, lru_lambda, sgu_ln_g, sgu_ln_b, sgu_w, sgu_b, w_o,
           ln1_g, ln1_b, ffn_w_in, ffn_conv_w, ffn_conv_b, ffn_w_down, ln2_g, ln2_b):
    t_start = _time.time()

    def log(msg):
        print("[kernel %.0fs] %s" % (_time.time() - t_start, msg), flush=True)

    f32 = lambda a: np.ascontiguousarray(np.asarray(a, dtype=np.float32))
    x = f32(x)
    c = f32(c)
    B, SEQ, D = x.shape
    NCORE = 8
    JB = 2
    NTA_ = SEQ // JB
    cores = [(b, j) for b in range(B) for j in range(JB)]
    mod = run_M(c, f32(w_mod), f32(b_mod))
    log("mod done")
    xT = [fm(x[b, j * NTA_:(j + 1) * NTA_]) for (b, j) in cores]
    cB = consts_B()
    for l in range(2):
        WA = prep_A_weights(f32(w_in[l]), f32(sgu_ln_g[l]), f32(sgu_ln_b[l]), f32(sgu_w[l]), f32(sgu_b[l]))
        ncA = _get("A", lambda: build_A(NTA_))
        in_maps = [dict(xT=xT[k], modv=_modv(mod[l, b]), **WA) for k, (b, j) in enumerate(cores)]
        resA = run_bass_kernel_spmd(ncA, in_maps, core_ids=list(range(len(cores)))).results
        del WA, in_maps
        log("layer %d A done" % l)
        ncB = _get("B", build_B)
        in_maps = []
        for b in range(B):
            zfm = np.concatenate([resA[b * JB + j]["zfm"] for j in range(JB)], axis=2)
            vsw = np.concatenate([resA[b * JB + j]["vsw"] for j in range(JB)], axis=1)
            glo = np.concatenate([resA[b * JB + j]["glo"] for j in range(JB)], axis=0).reshape(SEQ, 48)
            for g in range(4):
                qT_ = np.ascontiguousarray(zfm[4 * g:4 * g + 4])
                kT_ = np.stack([zfm[16 + g], zfm[20 + g], zfm[24 + g], zfm[28 + g]])
                vtm = np.stack([np.ascontiguousarray(vsw[v][:, :, g * 128:(g + 1) * 128].transpose(1, 0, 2)) for v in range(2)])
                gl = glo[:, g * 12:(g + 1) * 12]
                glr = np.ascontiguousarray(gl.reshape(128, 64, 2, 2, 3).transpose(3, 1, 0, 2, 4)).reshape(128, 128 * 6)
                lruin = np.stack([np.stack([zfm[32 + 2 * g], zfm[32 + 2 * g + 1]]), np.stack([zfm[40 + 2 * g], zfm[40 + 2 * g + 1]])])
                pb = prep_B_params(g, f32(cmp_pos[l]), f32(cmp_w1[l]), f32(cmp_b1[l]), f32(cmp_w2[l]), f32(lru_conv_w[l]),
                                   f32(lru_conv_b[l]), f32(lru_wa[l]), f32(lru_ba[l]), f32(lru_wx[l]), f32(lru_bx[l]), f32(lru_lambda[l]))
                in_maps.append(dict(qT=qT_, kT=kT_, vtm=vtm, glr=glr, lruin=lruin, **pb, **cB))
        resB = run_bass_kernel_spmd(ncB, in_maps, core_ids=list(range(NCORE))).results
        del in_maps
        log("layer %d B done" % l)
        WC = prep_C_weights(f32(w_o[l]), f32(ffn_w_in[l]), f32(ffn_conv_w[l]), f32(ffn_conv_b[l]), f32(ffn_w_down[l]),
                            f32(ln1_g[l]), f32(ln1_b[l]), f32(ln2_g[l]), f32(ln2_b[l]))
        ncC = _get("C", lambda: build_C(SEQ, False))
        in_maps = []
        for b in range(B):
            yT = np.empty((32, 128, SEQ), np.float32)
            for g in range(4):
                oa = resB[b * 4 + g]["oatt"].reshape(128, 2, 2, 64, 128)
                yT[4 * g:4 * g + 4] = oa.transpose(1, 2, 4, 0, 3).reshape(4, 128, SEQ)
                yl = resB[b * 4 + g]["ylru"]
                yT[16 + 2 * g] = yl[0]
                yT[16 + 2 * g + 1] = yl[1]
            for j in range(JB):
                yT[24:32, :, j * NTA_:(j + 1) * NTA_] = resA[b * JB + j]["ysg"]
            xfull = np.concatenate([xT[b * JB + j] for j in range(JB)], axis=2)
            hflag = np.zeros((128, 1), np.float32)
            in_maps.append(dict(yT=yT, xT=xfull, hflag=hflag, modv=_modv(mod[l, b]), **WC))
        del resA, resB
        resC = run_bass_kernel_spmd(ncC, in_maps, core_ids=list(range(B))).results
        del in_maps, WC
        xT = []
        for b in range(B):
            for j in range(JB):
                xT.append(np.ascontiguousarray(resC[b]["x2T"][:, :, j * NTA_:(j + 1) * NTA_]))
        log("layer %d C done" % l)
    out = np.empty((B, SEQ, D), np.float32)
    for k, (b, j) in enumerate(cores):
        out[b, j * NTA_:(j + 1) * NTA_] = xT[k].reshape(D, NTA_).T
    return out
```

```python
import numpy as np
from contextlib import ExitStack
import concourse.bass as bass
import concourse.mybir as mybir
from concourse.bass_utils import run_bass_kernel_spmd

F32 = mybir.dt.float32
BF16 = mybir.dt.bfloat16
AF = mybir.ActivationFunctionType
ALU = mybir.AluOpType
AX = mybir.AxisListType

ENGS = ['pe', 'act', 'dve', 'pool', 'sp']
NDS = 8


class T:
    __slots__ = ('t', 'w', 'r')

    def __init__(self, t=None):
        self.t = t
        self.w = None
        self.r = []

    def __getitem__(self, k):
        return self.t[k]


class Prog:
    def __init__(self, nc, es, same_engine_wait=True):
        self.nc = nc
        self.es = es
        self.h = {'pe': nc.tensor, 'act': nc.scalar, 'dve': nc.vector, 'pool': nc.gpsimd, 'sp': nc.sync}
        self.ops = {e: [] for e in ENGS}
        self.sems = {}
        for e in ENGS:
            self.sems[e] = es.enter_context(nc.semaphore('s_' + e))
        self.cnt = {e: 0 for e in ENGS}
        self.known = {e: {} for e in ENGS}
        self.dq = {}
        for q in ('sp', 'pool', 'act'):
            for i in range(NDS):
                self.sems[(q, i)] = es.enter_context(nc.semaphore('d_%s%d' % (q, i)))
            self.dq[q] = 0
        self.dval = {}
        self.sew = same_engine_wait
        self.nwait = 0

    def sb(self, name, shape, dt):
        return T(self.es.enter_context(self.nc.sbuf_tensor("sb_" + name, list(shape), dt)))

    def ps(self, name, shape, dt=F32):
        return T(self.es.enter_context(self.nc.psum_tensor("ps_" + name, list(shape), dt)))

    def _wait(self, eng, tok):
        key, val = tok
        if key == eng and (eng == 'pe' or not self.sew):
            return
        if self.known[eng].get(key, 0) >= val:
            return
        self.known[eng][key] = val
        sem = self.sems[key]
        self.ops[eng].append(lambda h, sem=sem, val=val: h.wait_ge(sem, val))
        self.nwait += 1

    def _deps(self, eng, r, w):
        need = {}
        for t in r:
            if t.w is not None:
                k, v = t.w
                if need.get(k, 0) < v:
                    need[k] = v
        for t in w:
            if t.w is not None:
                k, v = t.w
                if need.get(k, 0) < v:
                    need[k] = v
            for k, v in t.r:
                if need.get(k, 0) < v:
                    need[k] = v
        for k, v in need.items():
            self._wait(eng, (k, v))

    def op(self, eng, fn, r=(), w=()):
        self._deps(eng, r, w)
        self.cnt[eng] += 1
        tok = (eng, self.cnt[eng])
        sem = self.sems[eng]
        self.ops[eng].append(lambda h, fn=fn, sem=sem: fn(h).then_inc(sem, 1))
        for t in w:
            t.w = tok
            t.r = []
        for t in r:
            if t not in w:
                self._addr(t, tok)
        return tok

    def _addr(self, t, tok):
        k, v = tok
        for i, (k2, v2) in enumerate(t.r):
            if k2 == k:
                if v2 < v:
                    t.r[i] = tok
                return
        t.r.append(tok)

    def dma(self, q, out, in_, r=(), w=(), **kw):
        n = self.dq[q]
        self.dq[q] = n + 1
        key = (q, n % NDS)
        val = 16 * (n // NDS + 1)
        if val > 16:
            self._wait(q, (key, val - 16))
        self._deps(q, r, w)
        sem = self.sems[key]
        self.ops[q].append(lambda h, out=out, in_=in_, sem=sem, kw=kw: h.dma_start(out=out, in_=in_, **kw).then_inc(sem, 16))
        self.dval[key] = val
        tok = (key, val)
        for t in w:
            t.w = tok
            t.r = []
        for t in r:
            if t not in w:
                self._addr(t, tok)
        return tok

    def barrier(self):
        toks = [(e, self.cnt[e]) for e in ENGS if self.cnt[e] > 0]
        toks += [(k, v) for k, v in self.dval.items()]
        for e in ENGS:
            for tok in toks:
                if tok[0] != e:
                    self._wait(e, tok)
                elif e != 'pe':
                    self._wait(e, tok)

    def emit(self):
        self.barrier()
        nc = self.nc
        self.total = getattr(self, 'total', 0) + sum(len(v) for v in self.ops.values())
        self._emit_block()
        self.ops = {e: [] for e in ENGS}

    def _emit_block(self):
        nc = self.nc
        with nc.Block() as block:
            @block.tensor
            def _(h):
                for f in self.ops['pe']:
                    f(h)

            @block.scalar
            def _(h):
                for f in self.ops['act']:
                    f(h)

            @block.vector
            def _(h):
                for f in self.ops['dve']:
                    f(h)

            @block.gpsimd
            def _(h):
                for f in self.ops['pool']:
                    f(h)

            @block.sync
            def _(h):
                for f in self.ops['sp']:
                    f(h)


ALPHA = (2.0 * 2) ** 0.25
NT = 2048
TB = 512
KT = 32
NFC = 86
EPS = 1e-5


class Rot:
    def __init__(self, tiles):
        self.tiles = tiles
        self.i = 0

    def next(self):
        t = self.tiles[self.i % len(self.tiles)]
        self.i += 1
        return t


def build_C(NT=NT, with_halo=True):
    HW = 2 if with_halo else 0
    nc = bass.Bass("TRN2", target_bir_lowering=False)

    def din(name, shape, dt=F32):
        return nc.dram_tensor(name, shape, dt, kind="ExternalInput").ap()

    yT = din("yT", [32, 128, NT + HW])
    xT = din("xT", [32, 128, NT + HW])
    hflag = din("hflag", [128, 1])
    modv = din("modv", [128, 192])
    lnp = din("lnp", [128, 128])
    wo_t = din("wo_t", [32, 128, 4096])
    wi_t = din("wi_t", [NFC, 128, 2, 4096])
    wd_t = din("wd_t", [32, 128, NFC * 128])
    fcv = din("fcv", [128, NFC * 4])
    x2T = nc.dram_tensor("x2T", [32, 128, NT], F32, kind="ExternalOutput").ap()
    vs = nc.dram_tensor("vs", [32, 128, TB], F32, kind="Internal").ap()
    x1s = nc.dram_tensor("x1s", [32, 128, TB], F32, kind="Internal").ap()

    with ExitStack() as es:
        P = Prog(nc, es)
        modv_sb = P.sb("modv_sb", [128, 192], F32)
        lnp_sb = P.sb("lnp_sb", [128, 128], F32)
        fcv_sb = P.sb("fcv_sb", [128, NFC * 4], F32)
        hflag_sb = P.sb("hflag_sb", [128, 1], F32)
        ones_sb = P.sb("ones_sb", [128, 128], F32)
        op1 = P.sb("op1", [128, 192], F32)
        P.dma('sp', modv_sb[:], modv, w=[modv_sb])
        P.dma('sp', lnp_sb[:], lnp, w=[lnp_sb])
        P.dma('sp', fcv_sb[:], fcv, w=[fcv_sb])
        P.dma('sp', hflag_sb[:], hflag, w=[hflag_sb])
        P.op('dve', lambda h: h.memset(ones_sb[:], 1.0), w=[ones_sb])
        P.op('dve', lambda h: h.tensor_scalar(out=op1[:], in0=modv_sb[:], scalar1=1.0, scalar2=None, op0=ALU.add), r=[modv_sb], w=[op1])

        R1 = es.enter_context(nc.sbuf_tensor("R1", [128, NFC * TB], BF16))
        aT = [T(R1) for _ in range(NFC)]
        hTt = es.enter_context(nc.sbuf_tensor("hT", [128, KT * TB], BF16))
        hT = [T(hTt) for _ in range(KT)]
        WBN = 43 * 128
        wb = Rot([P.sb("wb%d" % i, [128, WBN], BF16) for i in range(3)])
        ghalo = P.sb("ghalo", [128, NFC * 2], F32)
        tp = Rot([P.sb("tp%d" % i, [128, TB + 2], F32) for i in range(10)])
        xt = vt = sq = t1 = vl = nt_ = xo = gext = acc = sl = tp
        mean = P.sb("mean", [128, TB], F32)
        rstd = P.sb("rstd", [128, TB], F32)
        nmr = P.sb("nmr", [128, TB], F32)
        msq = nmr
        mm = Rot([P.ps("mm%d" % i, [128, TB]) for i in range(4)])
        ps_sum = P.ps("ps_sum", [128, TB])
        ps_sq = P.ps("ps_sq", [128, TB])
        vsT = [T(vs) for _ in range(32)]
        x1sT = [T(x1s) for _ in range(32)]

        def load_w(src2d, ncols):
            wt = wb.next()
            P.dma('pool', wt[:, 0:ncols], src2d, w=[wt], max_dma_last_dim=4096)
            return wt

        def stats_finish(W):
            P.op('act', lambda h: h.activation(out=mean[:, 0:W], in_=ps_sum[:, 0:W], func=AF.Copy, scale=1.0 / 4096), r=[ps_sum], w=[mean])
            P.op('dve', lambda h: h.tensor_tensor(out=msq[:, 0:W], in0=mean[:, 0:W], in1=mean[:, 0:W], op=ALU.mult), r=[mean], w=[msq])
            P.op('dve', lambda h: h.scalar_tensor_tensor(out=rstd[:, 0:W], in0=ps_sq[:, 0:W], scalar=1.0 / 4096, in1=msq[:, 0:W], op0=ALU.mult, op1=ALU.subtract), r=[ps_sq, msq], w=[rstd])
            P.op('dve', lambda h: h.tensor_scalar(out=rstd[:, 0:W], in0=rstd[:, 0:W], scalar1=EPS, scalar2=None, op0=ALU.add), r=[rstd], w=[rstd])
            P.op('act', lambda h: h.activation(out=rstd[:, 0:W], in_=rstd[:, 0:W], func=AF.Sqrt), r=[rstd], w=[rstd])
            P.op('dve', lambda h: h.reciprocal(out=rstd[:, 0:W], in_=rstd[:, 0:W]), r=[rstd], w=[rstd])
            P.op('dve', lambda h: h.scalar_tensor_tensor(out=nmr[:, 0:W], in0=mean[:, 0:W], scalar=-1.0, in1=rstd[:, 0:W], op0=ALU.mult, op1=ALU.mult), r=[mean, rstd], w=[nmr])

        def ln_sublayer(W, main_fn, xsrc_fn, gcol, lng, lnb, out_fn):
            pending = None
            xcur = xt.next()
            ap, trk = xsrc_fn(0)
            P.dma('sp', xcur[:, 0:W], ap, r=trk, w=[xcur])
            for dt in range(32):
                pm = main_fn(dt)
                if pending is not None:
                    pending()
                xnext = None
                if dt + 1 < 32:
                    xnext = xt.next()
                    ap, trk = xsrc_fn(dt + 1)
                    P.dma('sp', xnext[:, 0:W], ap, r=trk, w=[xnext])
                a1 = t1.next()
                v = vt.next()
                s = sq.next()
                P.op('act', lambda h, a1=a1, pm=pm, dt=dt: h.activation(out=a1[:, 0:W], in_=pm[:, 0:W], func=AF.Identity, scale=op1[:, gcol * 32 + dt:gcol * 32 + dt + 1]), r=[pm, op1], w=[a1])
                P.op('dve', lambda h, v=v, a1=a1, xc=xcur: h.scalar_tensor_tensor(out=v[:, 0:W], in0=xc[:, 0:W], scalar=ALPHA, in1=a1[:, 0:W], op0=ALU.mult, op1=ALU.add), r=[xcur, a1], w=[v])
                P.op('act', lambda h, v=v, s=s: h.activation(out=s[:, 0:W], in_=v[:, 0:W], func=AF.Square), r=[v], w=[s])
                P.dma('sp', vs[dt, :, 0:W], v[:, 0:W], r=[v], w=[vsT[dt]])

                def st(v=v, s=s, dt=dt):
                    P.op('pe', lambda h: h.matmul(ps_sum[:, 0:W], ones_sb[:], v[:, 0:W], start=(dt == 0), stop=(dt == 31)), r=[ones_sb, v], w=[ps_sum])
                    P.op('pe', lambda h: h.matmul(ps_sq[:, 0:W], ones_sb[:], s[:, 0:W], start=(dt == 0), stop=(dt == 31)), r=[ones_sb, s], w=[ps_sq])
                pending = st
                xcur = xnext
            pending()
            stats_finish(W)
            vcur = vl.next()
            P.dma('sp', vcur[:, 0:W], vs[0, :, 0:W], r=[vsT[0]], w=[vcur])
            for dt in range(32):
                vnext = None
                if dt + 1 < 32:
                    vnext = vl.next()
                    P.dma('sp', vnext[:, 0:W], vs[dt + 1, :, 0:W], r=[vsT[dt + 1]], w=[vnext])
                n = nt_.next()
                P.op('dve', lambda h, n=n, vc=vcur: h.tensor_tensor(out=n[:, 0:W], in0=vc[:, 0:W], in1=rstd[:, 0:W], op=ALU.mult), r=[vcur, rstd], w=[n])
                P.op('dve', lambda h, n=n: h.tensor_tensor(out=n[:, 0:W], in0=n[:, 0:W], in1=nmr[:, 0:W], op=ALU.add), r=[n, nmr], w=[n])
                o = xo.next()
                P.op('act', lambda h, n=n, o=o, dt=dt: h.activation(out=o[:, 0:W], in_=n[:, 0:W], func=AF.Identity, scale=lnp_sb[:, lng * 32 + dt:lng * 32 + dt + 1], bias=lnp_sb[:, lnb * 32 + dt:lnb * 32 + dt + 1]), r=[n, lnp_sb], w=[o])
                out_fn(dt, o)
                vcur = vnext

        def block(col0, W, halo):
            for kt in range(KT):
                P.dma('pool', R1[:, kt * TB:kt * TB + W], yT[kt, :, col0:col0 + W], w=[aT[kt]], max_dma_last_dim=2048)

            def main1(dt):
                wt = load_w(wo_t[dt], 4096)
                pm = mm.next()
                for kt in range(KT):
                    P.op('pe', lambda h, kt=kt, wt=wt, pm=pm: h.matmul(pm[:, 0:W], wt[:, kt * 128:(kt + 1) * 128], R1[:, kt * TB:kt * TB + W], start=(kt == 0), stop=(kt == KT - 1)), r=[wt, aT[kt]], w=[pm])
                return pm

            def xsrc1(dt):
                return xT[dt, :, col0:col0 + W], []

            def out1(dt, o):
                if not halo:
                    P.dma('sp', x1s[dt, :, 0:W], o[:, 0:W], r=[o], w=[x1sT[dt]])
                P.op('act', lambda h: h.activation(out=hTt[:, dt * TB:dt * TB + W], in_=o[:, 0:W], func=AF.Identity, scale=op1[:, 4 * 32 + dt:4 * 32 + dt + 1], bias=modv_sb[:, 3 * 32 + dt:3 * 32 + dt + 1]), r=[o, op1, modv_sb], w=[hT[dt]])

            ln_sublayer(W, main1, xsrc1, 2, 0, 1, out1)

            for c in range(NFC):
                wg = load_w(wi_t[c, :, 0, :], 4096)
                pg = mm.next()
                for kt in range(KT):
                    P.op('pe', lambda h, kt=kt, wg=wg, pg=pg: h.matmul(pg[:, 0:W], wg[:, kt * 128:(kt + 1) * 128], hTt[:, kt * TB:kt * TB + W], start=(kt == 0), stop=(kt == KT - 1)), r=[wg, hT[kt]], w=[pg])
                if halo:
                    P.op('act', lambda h, c=c, pg=pg: h.activation(out=ghalo[:, 2 * c:2 * c + 2], in_=pg[:, 0:2], func=AF.Identity, scale=hflag_sb[:, 0:1]), r=[pg, hflag_sb], w=[ghalo])
                    continue
                wu = load_w(wi_t[c, :, 1, :], 4096)
                pu = mm.next()
                for kt in range(KT):
                    P.op('pe', lambda h, kt=kt, wu=wu, pu=pu: h.matmul(pu[:, 0:W], wu[:, kt * 128:(kt + 1) * 128], hTt[:, kt * TB:kt * TB + W], start=(kt == 0), stop=(kt == KT - 1)), r=[wu, hT[kt]], w=[pu])
                ge = gext.next()
                ac = acc.next()
                s_ = sl.next()
                P.op('act', lambda h, ge=ge, c=c: h.activation(out=ge[:, 0:2], in_=ghalo[:, 2 * c:2 * c + 2], func=AF.Copy), r=[ghalo], w=[ge])
                P.op('act', lambda h, ge=ge, pg=pg: h.activation(out=ge[:, 2:2 + W], in_=pg[:, 0:W], func=AF.Copy), r=[pg], w=[ge])
                P.op('act', lambda h, ge=ge, c=c: h.activation(out=ghalo[:, 2 * c:2 * c + 2], in_=ge[:, W:W + 2], func=AF.Copy), r=[ge], w=[ghalo])
                f0 = 4 * c
                P.op('dve', lambda h, ge=ge, ac=ac, f0=f0: h.tensor_scalar(out=ac[:, 0:W], in0=ge[:, 2:2 + W], scalar1=fcv_sb[:, f0 + 2:f0 + 3], scalar2=fcv_sb[:, f0 + 3:f0 + 4], op0=ALU.mult, op1=ALU.add), r=[ge, fcv_sb], w=[ac])
                P.op('dve', lambda h, ge=ge, ac=ac, f0=f0: h.scalar_tensor_tensor(out=ac[:, 0:W], in0=ge[:, 1:1 + W], scalar=fcv_sb[:, f0 + 1:f0 + 2], in1=ac[:, 0:W], op0=ALU.mult, op1=ALU.add), r=[ge, fcv_sb, ac], w=[ac])
                P.op('dve', lambda h, ge=ge, ac=ac, f0=f0: h.scalar_tensor_tensor(out=ac[:, 0:W], in0=ge[:, 0:W], scalar=fcv_sb[:, f0:f0 + 1], in1=ac[:, 0:W], op0=ALU.mult, op1=ALU.add), r=[ge, fcv_sb, ac], w=[ac])
                P.op('act', lambda h, ac=ac, s_=s_: h.activation(out=s_[:, 0:W], in_=ac[:, 0:W], func=AF.Silu), r=[ac], w=[s_])
                P.op('dve', lambda h, s_=s_, pu=pu, c=c: h.tensor_tensor(out=R1[:, c * TB:c * TB + W], in0=s_[:, 0:W], in1=pu[:, 0:W], op=ALU.mult), r=[s_, pu], w=[aT[c]])
            if halo:
                return

            def main3(dt):
                pm = mm.next()
                for half in range(2):
                    wt = load_w(wd_t[dt, :, half * 43 * 128:(half + 1) * 43 * 128], 43 * 128)
                    for cc in range(43):
                        c = half * 43 + cc
                        P.op('pe', lambda h, cc=cc, c=c, wt=wt, pm=pm: h.matmul(pm[:, 0:W], wt[:, cc * 128:(cc + 1) * 128], R1[:, c * TB:c * TB + W], start=(c == 0), stop=(c == NFC - 1)), r=[wt, aT[c]], w=[pm])
                return pm

            def xsrc3(dt):
                return x1s[dt, :, 0:W], [x1sT[dt]]

            def out3(dt, o):
                P.dma('sp', x2T[dt, :, col0 - HW:col0 - HW + W], o[:, 0:W], r=[o])

            ln_sublayer(W, main3, xsrc3, 5, 2, 3, out3)

        if with_halo:
            block(0, 2, True)
        else:
            P.op('dve', lambda h: h.memset(ghalo[:], 0.0), w=[ghalo])
        for b in range(NT // TB):
            block(HW + b * TB, TB, False)
        P.emit()
        print("progC: ops", {e: len(P.ops[e]) for e in ENGS}, "waits", P.nwait)
    return nc


def prep_C_weights(w_o, ffn_w_in, ffn_conv_w, ffn_conv_b, ffn_w_down, ln1_g, ln1_b, ln2_g, ln2_b):
    wo_t = np.ascontiguousarray(w_o.reshape(32, 128, 32, 128).transpose(2, 1, 0, 3)).reshape(32, 128, 4096)
    wi = ffn_w_in.reshape(32, 128, 2, NFC, 128)
    wi_t = np.ascontiguousarray(wi.transpose(3, 1, 2, 0, 4)).reshape(NFC, 128, 2, 4096)
    wd = ffn_w_down.reshape(NFC, 128, 32, 128)
    wd_t = np.ascontiguousarray(wd.transpose(2, 1, 0, 3)).reshape(32, 128, NFC * 128)
    fc = np.concatenate([ffn_conv_w, ffn_conv_b[None]], axis=0)
    fcv = np.ascontiguousarray(fc.reshape(4, NFC, 128).transpose(2, 1, 0)).reshape(128, NFC * 4)
    lnp = np.stack([ln1_g, ln1_b, ln2_g, ln2_b], 0).reshape(4, 32, 128)
    lnp = np.ascontiguousarray(lnp.transpose(2, 0, 1)).reshape(128, 128)
    return dict(wo_t=wo_t, wi_t=wi_t, wd_t=wd_t, fcv=fcv, lnp=lnp)


def fm(a):
    return np.ascontiguousarray(a.T).reshape(32, 128, a.shape[0])


NTA = 2048
HALF = 1024
NFM = 56
NOUT = 48
EPS = 1e-5
C_Q, C_KC, C_VC, C_KS, C_VS, C_KW, C_VW, C_GL, C_ZG, C_ZR, C_ZU, C_ZV = 0, 2048, 2560, 3072, 3584, 4096, 4608, 5120, 5168, 6192, 7216, 8240
FM_COLS = ([C_Q + 128 * i for i in range(16)] + [C_KC + 128 * i for i in range(4)] + [C_VC + 128 * i for i in range(4)]
           + [C_KS + 128 * i for i in range(4)] + [C_KW + 128 * i for i in range(4)] + [C_ZG + 128 * i for i in range(8)]
           + [C_ZR + 128 * i for i in range(8)] + [C_ZU + 128 * i for i in range(8)])
TM_COLS = [C_VS, C_VW, C_ZV, C_ZV + 512]


def build_A(NTA=NTA):
    nc = bass.Bass("TRN2", target_bir_lowering=False)

    def din(name, shape, dt=F32):
        return nc.dram_tensor(name, shape, dt, kind="ExternalInput").ap()

    def dout(name, shape, dt=F32):
        return nc.dram_tensor(name, shape, dt, kind="ExternalOutput").ap()

    xT = din("xT", [32, 128, NTA])
    modv = din("modv", [128, 192])
    wfm = din("wfm", [NFM, 128, 4096])
    wtm = din("wtm", [4, 2, 128, 16 * 512])
    wgl = din("wgl", [128, 32 * 48])
    lngb = din("lngb", [2, 128, 1024])
    swT = din("swT", [128, 8 * 128])
    tmask = din("tmask", [128, 128])
    sbb = din("sbb", [128, 8 * 512])
    zfm = dout("zfm", [NOUT, 128, NTA])
    vsw = dout("vsw", [2, NTA // 128, 128, 512])
    glo = dout("glo", [NTA // 128, 128, 48])
    ysg = dout("ysg", [8, 128, NTA])

    with ExitStack() as es:
        P = Prog(nc, es)
        modv_sb = P.sb("modv_sb", [128, 192], F32)
        op1 = P.sb("op1", [128, 192], F32)
        lng = P.sb("lng", [128, 1024], F32)
        lnb = P.sb("lnb", [128, 1024], F32)
        sw32 = P.sb("sw32", [128, 1024], F32)
        swb = P.sb("swb", [128, 1024], BF16)
        tm = P.sb("tm", [128, 128], F32)
        sbb_sb = P.sb("sbb_sb", [128, 8 * 512], F32)
        wgl_sb = P.sb("wgl_sb", [128, 32 * 48], BF16)
        P.dma('sp', modv_sb[:], modv, w=[modv_sb])
        P.dma('sp', lng[:], lngb[0], w=[lng])
        P.dma('sp', lnb[:], lngb[1], w=[lnb])
        P.dma('sp', sw32[:], swT, w=[sw32])
        P.dma('sp', tm[:], tmask, w=[tm])
        P.dma('sp', sbb_sb[:], sbb, w=[sbb_sb])
        P.dma('pool', wgl_sb[:], wgl, w=[wgl_sb], max_dma_last_dim=4096)
        P.op('dve', lambda h: h.tensor_scalar(out=op1[:], in0=modv_sb[:], scalar1=1.0, scalar2=None, op0=ALU.add), r=[modv_sb], w=[op1])
        for g in range(8):
            P.op('dve', lambda h, g=g: h.tensor_tensor(out=swb[:, g * 128:(g + 1) * 128], in0=sw32[:, g * 128:(g + 1) * 128], in1=tm[:], op=ALU.mult), r=[sw32, tm], w=[swb])

        Q = 512
        hTt = es.enter_context(nc.sbuf_tensor("hT", [128, 32 * Q], BF16))
        hT = [T(hTt) for _ in range(32)]
        vnt = es.enter_context(nc.sbuf_tensor("vn", [128, 4 * 1024], BF16))
        vn = [T(vnt) for _ in range(4)]
        wb = Rot([P.sb("wb%d" % i, [128, 16 * 512], BF16) for i in range(3)])
        xin = Rot([P.sb("xin%d" % i, [128, Q], F32) for i in range(2)])
        ot = Rot([P.sb("ot%d" % i, [128, 512], F32) for i in range(4)])
        gvs = [P.sb("gv%d" % i, [128, 1024], F32) for i in range(4)]
        gut = P.sb("gu", [128, Q], F32)
        st6 = P.sb("st6", [128, 12], F32)
        mv = P.sb("mv", [128, 2], F32)
        rs = P.sb("rs", [128, 1], F32)
        mm = Rot([P.ps("mm%d" % i, [128, 512]) for i in range(6)])

        evi = [0]

        def evac(dst, src, r, w):
            evi[0] += 1
            if evi[0] % 2:
                P.op('act', lambda h: h.activation(out=dst, in_=src, func=AF.Copy), r=r, w=w)
            else:
                P.op('dve', lambda h: h.tensor_copy(out=dst, in_=src), r=r, w=w)

        for qi in range(NTA // Q):
            t0 = qi * Q
            for kt in range(32):
                xi = xin.next()
                P.dma('sp', xi[:], xT[kt, :, t0:t0 + Q], w=[xi])
                P.op('act', lambda h, xi=xi, kt=kt: h.activation(out=hTt[:, kt * Q:(kt + 1) * Q], in_=xi[:], func=AF.Identity, scale=op1[:, 32 + kt:33 + kt], bias=modv_sb[:, kt:kt + 1]), r=[xi, op1, modv_sb], w=[hT[kt]])
            for gi in (2, 3, 0, 1):
                wh = []
                for kh in range(2):
                    wt = wb.next()
                    P.dma('pool', wt[:], wtm[gi, kh], w=[wt], max_dma_last_dim=4096)
                    wh.append(wt)
                for ch in range(4):
                    pm = mm.next()
                    for kt in range(32):
                        wt = wh[kt // 16]
                        kk = kt % 16
                        P.op('pe', lambda h, pm=pm, kt=kt, wt=wt, kk=kk, ch=ch: h.matmul(pm[:], hTt[:, kt * Q + ch * 128:kt * Q + (ch + 1) * 128], wt[:, kk * 512:(kk + 1) * 512], start=(kt == 0), stop=(kt == 31)), r=[hT[kt], wt], w=[pm])
                    if gi < 2:
                        o = ot.next()
                        evac(o[:], pm[:], [pm], [o])
                        P.dma('sp', vsw[gi, qi * 4 + ch], o[:], r=[o])
                    else:
                        gv = gvs[ch]
                        P.op('act', lambda h, gv=gv, pm=pm, gi=gi: h.activation(out=gv[:, (gi - 2) * 512:(gi - 1) * 512], in_=pm[:], func=AF.Gelu_apprx_tanh), r=[pm], w=[gv])
                        if gi == 3:
                            P.op('dve', lambda h, gv=gv: h.bn_stats(out=st6[:, 0:6], in_=gv[:, 0:512]), r=[gv], w=[st6])
                            P.op('dve', lambda h, gv=gv: h.bn_stats(out=st6[:, 6:12], in_=gv[:, 512:1024]), r=[gv, st6], w=[st6])
                            P.op('dve', lambda h: h.bn_aggr(out=mv[:], in_=st6[:]), r=[st6], w=[mv])
                            P.op('dve', lambda h: h.tensor_scalar(out=rs[:], in0=mv[:, 1:2], scalar1=EPS, scalar2=None, op0=ALU.add), r=[mv], w=[rs])
                            P.op('act', lambda h: h.activation(out=rs[:], in_=rs[:], func=AF.Sqrt), r=[rs], w=[rs])
                            P.op('dve', lambda h: h.reciprocal(out=rs[:], in_=rs[:]), r=[rs], w=[rs])
                            P.op('dve', lambda h, gv=gv: h.tensor_scalar(out=gv[:], in0=gv[:], scalar1=mv[:, 0:1], scalar2=rs[:, 0:1], op0=ALU.subtract, op1=ALU.mult), r=[gv, mv, rs], w=[gv])
                            P.op('dve', lambda h, gv=gv: h.tensor_tensor(out=gv[:], in0=gv[:], in1=lng[:], op=ALU.mult), r=[gv, lng], w=[gv])
                            P.op('dve', lambda h, gv=gv, ch=ch: h.tensor_tensor(out=vnt[:, ch * 1024:(ch + 1) * 1024], in0=gv[:], in1=lnb[:], op=ALU.add), r=[gv, lnb], w=[vn[ch]])
            for ch in range(4):
                pm = mm.next()
                for kt in range(32):
                    P.op('pe', lambda h, pm=pm, kt=kt, ch=ch: h.matmul(pm[:, 0:48], hTt[:, kt * Q + ch * 128:kt * Q + (ch + 1) * 128], wgl_sb[:, kt * 48:(kt + 1) * 48], start=(kt == 0), stop=(kt == 31)), r=[hT[kt], wgl_sb], w=[pm])
                o = ot.next()
                evac(o[:, 0:48], pm[:, 0:48], [pm], [o])
                P.dma('sp', glo[qi * 4 + ch], o[:, 0:48], r=[o])
            for j in range(NFM):
                wt = wb.next()
                P.dma('pool', wt[:, 0:4096], wfm[j], w=[wt], max_dma_last_dim=4096)
                pm = mm.next()
                for kt in range(32):
                    P.op('pe', lambda h, pm=pm, kt=kt, wt=wt: h.matmul(pm[:], wt[:, kt * 128:(kt + 1) * 128], hTt[:, kt * Q:(kt + 1) * Q], start=(kt == 0), stop=(kt == 31)), r=[hT[kt], wt], w=[pm])
                if j < NOUT:
                    o = ot.next()
                    evac(o[:], pm[:], [pm], [o])
                    P.dma('sp', zfm[j, :, t0:t0 + Q], o[:], r=[o])
                else:
                    g = j - NOUT
                    P.op('act', lambda h, pm=pm: h.activation(out=gut[:], in_=pm[:], func=AF.Gelu_apprx_tanh), r=[pm], w=[gut])
                    pm2 = mm.next()
                    for ch in range(4):
                        P.op('pe', lambda h, pm2=pm2, ch=ch, g=g: h.matmul(pm2[:, ch * 128:(ch + 1) * 128], vnt[:, ch * 1024 + g * 128:ch * 1024 + (g + 1) * 128], swb[:, g * 128:(g + 1) * 128], start=True, stop=True), r=[vn[ch], swb], w=[pm2])
                    o = ot.next()
                    P.op('dve', lambda h, o=o, pm2=pm2, g=g: h.tensor_tensor(out=o[:], in0=pm2[:], in1=sbb_sb[:, g * 512:(g + 1) * 512], op=ALU.add), r=[pm2, sbb_sb], w=[o])
                    P.op('dve', lambda h, o=o: h.tensor_tensor(out=o[:], in0=o[:], in1=gut[:], op=ALU.mult), r=[o, gut], w=[o])
                    P.dma('sp', ysg[g, :, t0:t0 + Q], o[:], r=[o])
        P.emit()
        print("progA: ops", {e: len(P.ops[e]) for e in ENGS}, "waits", P.nwait)
    return nc


def prep_A_weights(w_in, sgu_ln_g, sgu_ln_b, sgu_w, sgu_b):
    w4 = w_in.reshape(32, 128, -1)
    wfm = np.empty((NFM, 128, 32, 128), np.float32)
    for j, c0 in enumerate(FM_COLS):
        wfm[j] = w4[:, :, c0:c0 + 128].transpose(1, 0, 2)
    wfm = wfm.reshape(NFM, 128, 4096)
    wtm = np.empty((4, 2, 128, 16, 512), np.float32)
    for gi, c0 in enumerate(TM_COLS):
        blk = w4[:, :, c0:c0 + 512].reshape(2, 16, 128, 512)
        wtm[gi] = blk.transpose(0, 2, 1, 3)
    wtm = wtm.reshape(4, 2, 128, 16 * 512)
    wgl = np.ascontiguousarray(w4[:, :, C_GL:C_GL + 48].transpose(1, 0, 2)).reshape(128, 32 * 48)
    lngb = np.stack([np.broadcast_to(sgu_ln_g, (128, 1024)), np.broadcast_to(sgu_ln_b, (128, 1024))]).astype(np.float32)
    swT = np.ascontiguousarray(sgu_w.transpose(2, 0, 1)).reshape(128, 8 * 128)
    tmask = (np.arange(128)[None, :] >= np.arange(128)[:, None]).astype(np.float32)
    sbb = np.broadcast_to(np.tile(sgu_b[:, None, :], (1, 4, 1)).reshape(1, 8 * 512), (128, 8 * 512)).astype(np.float32)
    return dict(wfm=wfm, wtm=wtm, wgl=wgl, lngb=np.ascontiguousarray(lngb), swT=swT, tmask=tmask, sbb=np.ascontiguousarray(sbb))


S = 8192
NQB = 128
SCALE = 128 ** -0.5
NEG = -1e30
FORCE = 1e4
DELTAS = {0: 0, 64: 1, 512: 2, 576: 3}


def build_B(qblocks=None, do_lru=True):
    nc = bass.Bass("TRN2", target_bir_lowering=False)
    if qblocks is None:
        qblocks = list(range(NQB))

    def din(name, shape, dt=F32):
        return nc.dram_tensor(name, shape, dt, kind="ExternalInput").ap()

    def dout(name, shape, dt=F32):
        return nc.dram_tensor(name, shape, dt, kind="ExternalOutput").ap()

    qT_d = din("qT", [4, 128, S])
    kT_d = din("kT", [4, 128, S])
    vtm_d = din("vtm", [2, 128, 64, 128])
    glr_d = din("glr", [128, NQB * 6])
    lru_d = din("lruin", [2, 2, 128, S])
    cw1_d = din("cw1", [2, 128, 32 * 128])
    cpos_d = din("cpos", [2, 128, 32])
    cb1_d = din("cb1", [128, 2])
    cw2_d = din("cw2", [2, 128, 128])
    lcv_d = din("lcv", [128, 10])
    lw_d = din("lw", [2, 2, 128, 128])
    lbl_d = din("lbl", [128, 6])
    m4_d = din("m4", [128, 4 * 64])
    tc_d = din("tcc", [128, 64])
    cov_d = din("cov", [128, 4 * 128])
    ex_d = din("ex", [128, S])
    fold_d = din("fold", [128, 64])
    id_d = din("ident", [64, 64])
    oatt = dout("oatt", [2 * NQB, 128, 128])
    ylru = dout("ylru", [2, 128, S])

    with ExitStack() as es0:
        P = Prog(nc, es0)
        if do_lru:
            with ExitStack() as es:
                P.es = es
                TC = 2048
                lcv = P.sb("lcv", [128, 10], F32)
                lbl = P.sb("lbl", [128, 6], F32)
                lwb = P.sb("lwb", [128, 4 * 128], BF16)
                sc = P.sb("sc", [128, 4], F32)
                P.dma('sp', lcv[:], lcv_d, w=[lcv])
                P.dma('sp', lbl[:], lbl_d, w=[lbl])
                for blk in range(2):
                    for ax in range(2):
                        P.dma('pool', lwb[:, (blk * 2 + ax) * 128:(blk * 2 + ax + 1) * 128], lw_d[blk, ax], w=[lwb])
                tmp1 = P.sb("tmp1", [128, 2], F32)
                for blk in range(2):
                    P.op('act', lambda h, blk=blk: h.activation(out=tmp1[:, 0:1], in_=lbl[:, blk * 3 + 2:blk * 3 + 3], func=AF.Exp, scale=-1.0), r=[lbl], w=[tmp1])
                    P.op('act', lambda h: h.activation(out=tmp1[:, 1:2], in_=tmp1[:, 0:1], func=AF.Ln, bias=1.0), r=[tmp1], w=[tmp1])
                    P.op('dve', lambda h, blk=blk: h.tensor_scalar(out=sc[:, 2 * blk:2 * blk + 1], in0=tmp1[:, 1:2], scalar1=-8.0, scalar2=None, op0=ALU.mult), r=[tmp1], w=[sc])
                    P.op('dve', lambda h, blk=blk: h.tensor_scalar(out=sc[:, 2 * blk + 1:2 * blk + 2], in0=tmp1[:, 1:2], scalar1=-16.0, scalar2=None, op0=ALU.mult), r=[tmp1], w=[sc])
                zr = Rot([P.sb("zr%d" % i, [128, TC + 3], F32) for i in range(2)])
                zg = Rot([P.sb("zg%d" % i, [128, TC], F32) for i in range(2)])
                xr = P.sb("xr", [128, TC], F32)
                xrb = P.sb("xrb", [128, TC], BF16)
                rr = P.sb("rr", [128, TC], F32)
                ig = P.sb("ig", [128, TC], F32)
                aa = P.sb("aa", [128, TC], F32)
                bb = P.sb("bb", [128, TC], F32)
                hh = Rot([P.sb("hh%d" % i, [128, TC], F32) for i in range(2)])
                carry = P.sb("carry", [128, 1], F32)
                pl = Rot([P.ps("pl%d" % i, [128, 512]) for i in range(4)])
                for blk in range(2):
                    P.op('dve', lambda h: h.memset(carry[:], 0.0), w=[carry])
                    for tci in range(S // TC):
                        t0 = tci * TC
                        z = zr.next()
                        g_ = zg.next()
                        if tci == 0:
                            P.op('dve', lambda h, z=z: h.memset(z[:, 0:3], 0.0), w=[z])
                            P.dma('sp', z[:, 3:3 + TC], lru_d[1, blk, :, 0:TC], w=[z])
                        else:
                            P.dma('sp', z[:], lru_d[1, blk, :, t0 - 3:t0 + TC], w=[z])
                        P.dma('sp', g_[:], lru_d[0, blk, :, t0:t0 + TC], w=[g_])
                        c0 = blk * 5
                        P.op('dve', lambda h, z=z, c0=c0: h.tensor_scalar(out=xr[:], in0=z[:, 3:3 + TC], scalar1=lcv[:, c0 + 3:c0 + 4], scalar2=lcv[:, c0 + 4:c0 + 5], op0=ALU.mult, op1=ALU.add), r=[z, lcv], w=[xr])
                        for k in range(3):
                            P.op('dve', lambda h, z=z, c0=c0, k=k: h.scalar_tensor_tensor(out=xr[:], in0=z[:, k:k + TC], scalar=lcv[:, c0 + k:c0 + k + 1], in1=xr[:], op0=ALU.mult, op1=ALU.add), r=[z, lcv, xr], w=[xr])
                        P.op('act', lambda h: h.activation(out=xrb[:], in_=xr[:], func=AF.Copy), r=[xr], w=[xrb])
                        for ax, dst in ((0, rr), (1, ig)):
                            for n4 in range(TC // 512):
                                pm = pl.next()
                                P.op('pe', lambda h, pm=pm, ax=ax, n4=n4, blk=blk: h.matmul(pm[:], lwb[:, (blk * 2 + ax) * 128:(blk * 2 + ax + 1) * 128], xrb[:, n4 * 512:(n4 + 1) * 512], start=True, stop=True), r=[lwb, xrb], w=[pm])
                                P.op('act', lambda h, pm=pm, dst=dst, n4=n4, ax=ax, blk=blk: h.activation(out=dst[:, n4 * 512:(n4 + 1) * 512], in_=pm[:], func=AF.Sigmoid, bias=lbl[:, blk * 3 + ax:blk * 3 + ax + 1]), r=[pm, lbl], w=[dst])
                        P.op('act', lambda h, blk=blk: h.activation(out=aa[:], in_=rr[:], func=AF.Exp, scale=sc[:, 2 * blk:2 * blk + 1]), r=[rr, sc], w=[aa])
                        P.op('act', lambda h, blk=blk: h.activation(out=bb[:], in_=rr[:], func=AF.Exp, scale=sc[:, 2 * blk + 1:2 * blk + 2]), r=[rr, sc], w=[bb])
                        P.op('act', lambda h: h.activation(out=bb[:], in_=bb[:], func=AF.Sqrt, scale=-1.0, bias=1.0), r=[bb], w=[bb])
                        P.op('dve', lambda h: h.tensor_tensor(out=ig[:], in0=ig[:], in1=xr[:], op=ALU.mult), r=[ig, xr], w=[ig])
                        P.op('dve', lambda h: h.tensor_tensor(out=bb[:], in0=bb[:], in1=ig[:], op=ALU.mult), r=[bb, ig], w=[bb])
                        hcur = hh.next()
                        P.op('dve', lambda h, hcur=hcur: h.tensor_tensor_scan(out=hcur[:], data0=aa[:], data1=bb[:], initial=carry[:, 0:1], op0=ALU.mult, op1=ALU.add), r=[aa, bb, carry], w=[hcur])
                        P.op('dve', lambda h, hcur=hcur: h.tensor_copy(out=carry[:], in_=hcur[:, TC - 1:TC]), r=[hcur], w=[carry])
                        P.op('act', lambda h, g_=g_: h.activation(out=g_[:], in_=g_[:], func=AF.Gelu_apprx_tanh), r=[g_], w=[g_])
                        P.op('dve', lambda h, hcur=hcur, g_=g_: h.tensor_tensor(out=hcur[:], in0=hcur[:], in1=g_[:], op=ALU.mult), r=[hcur, g_], w=[hcur])
                        P.dma('sp', ylru[blk, :, t0:t0 + TC], hcur[:], r=[hcur])
                P.emit()

        with ExitStack() as es:
            P.es = es
            qT = P.sb("qT", [128, 4, S], BF16)
            ksT = P.sb("ksT", [128, S], BF16)
            kwT = P.sb("kwT", [128, S], BF16)
            vsA = P.sb("vsA", [128, 64, 129], BF16)
            vwA = P.sb("vwA", [128, 64, 129], BF16)
            ex = P.sb("ex", [128, S], BF16)
            kcT = P.sb("kcT", [128, 512], BF16)
            VCA = P.sb("VCA", [128, 4, 257], BF16)
            m4 = P.sb("m4", [128, 4 * 64], F32)
            tcc = P.sb("tcc", [128, 64], F32)
            fold = P.sb("fold", [128, 64], F32)
            ident = P.sb("ident", [64, 64], F32)
            sig = P.sb("sig", [128, NQB * 6], F32)
            for hd in range(4):
                P.dma('pool', qT[:, hd, :], qT_d[hd], w=[qT], max_dma_last_dim=4096)
            P.dma('pool', ksT[:], kT_d[2], w=[ksT], max_dma_last_dim=4096)
            P.dma('pool', kwT[:], kT_d[3], w=[kwT], max_dma_last_dim=4096)
            P.dma('pool', ex[:], ex_d, w=[ex], max_dma_last_dim=4096)
            P.op('dve', lambda h: h.memset(vsA[:, :, 128:129], 1.0), w=[vsA])
            P.op('dve', lambda h: h.memset(vwA[:, :, 128:129], 1.0), w=[vwA])
            for half in range(2):
                P.dma('pool', vsA[:, half * 32:(half + 1) * 32, 0:128], vtm_d[0, :, half * 32:(half + 1) * 32, :], w=[vsA], max_dma_last_dim=512)
                P.dma('pool', vwA[:, half * 32:(half + 1) * 32, 0:128], vtm_d[1, :, half * 32:(half + 1) * 32, :], w=[vwA], max_dma_last_dim=512)
            P.dma('sp', m4[:], m4_d, w=[m4])
            P.dma('sp', tcc[:], tc_d, w=[tcc])
            P.dma('sp', fold[:], fold_d, w=[fold])
            P.dma('sp', ident[:], id_d, w=[ident])
            P.dma('sp', sig[:], glr_d, w=[sig])
            P.op('act', lambda h: h.activation(out=sig[:], in_=sig[:], func=AF.Sigmoid), r=[sig], w=[sig])
            P.op('dve', lambda h: h.memset(VCA[:, :, 256:257], 1.0), w=[VCA])
            for ct in range(4):
                P.dma('pool', VCA[:, ct, 128:256], cov_d[:, ct * 128:(ct + 1) * 128], w=[VCA], max_dma_last_dim=512)

            pS = Rot([P.ps("pS%d" % i, [128, 512]) for i in range(2)])
            pMA = P.ps("pMA", [128, 512])
            pMB = P.ps("pMB", [128, 512])
            pMs = Rot([(pMA, 0), (pMB, 0)])
            pU = Rot([P.ps("pU%d" % i, [128, 512]) for i in range(4)])
            misc = pMA

            if True:
                kr = P.sb("kr", [128, S], BF16)
                w1 = P.sb("w1", [128, 32 * 128], BF16)
                w2 = P.sb("w2", [128, 128], BF16)
                posb = P.sb("posb", [128, 32], BF16)
                cb1 = P.sb("cb1", [128, 2], F32)
                cbias = P.sb("cbias", [128, 1], F32)
                gT = P.sb("gT", [128, 512], BF16)
                P.dma('sp', cb1[:], cb1_d, w=[cb1])
                for kv in range(2):
                    P.dma('pool', kr[:], kT_d[kv], w=[kr], max_dma_last_dim=4096)
                    P.dma('pool', w1[:], cw1_d[kv], w=[w1], max_dma_last_dim=4096)
                    P.dma('pool', w2[:], cw2_d[kv], w=[w2])
                    P.dma('pool', posb[:], cpos_d[kv], w=[posb])
                    for p in range(32):
                        P.op('pe', lambda h, p=p: h.matmul(misc[:, 0:1], w1[:, p * 128:(p + 1) * 128], posb[:, p:p + 1], start=(p == 0), stop=(p == 31)), r=[w1, posb], w=[misc])
                    P.op('dve', lambda h, kv=kv: h.tensor_tensor(out=cbias[:], in0=misc[:, 0:1], in1=cb1[:, kv:kv + 1], op=ALU.add), r=[misc, cb1], w=[cbias])
                    pm = pS.next()
                    for p in range(32):
                        P.op('pe', lambda h, p=p, pm=pm: h.matmul(pm[:, 0:511], w1[:, p * 128:(p + 1) * 128], kr[:, p:p + 16 * 510 + 1:16], start=(p == 0), stop=(p == 31)), r=[w1, kr], w=[pm])
                    P.op('dve', lambda h: h.memset(gT[:], 0.0), w=[gT])
                    P.op('act', lambda h, pm=pm: h.activation(out=gT[:, 0:511], in_=pm[:, 0:511], func=AF.Gelu_apprx_tanh, bias=cbias[:, 0:1]), r=[pm, cbias], w=[gT])
                    if kv == 0:
                        pm2 = pS.next()
                        P.op('pe', lambda h, pm2=pm2: h.matmul(pm2[:], w2[:], gT[:], start=True, stop=True), r=[w2, gT], w=[pm2])
                        P.op('act', lambda h, pm2=pm2: h.activation(out=kcT[:], in_=pm2[:], func=AF.Copy), r=[pm2], w=[kcT])
                    else:
                        pm2 = pS.next()
                        for ct in range(4):
                            P.op('pe', lambda h, pm2=pm2, ct=ct: h.matmul(pm2[:, ct * 128:(ct + 1) * 128], gT[:, ct * 128:(ct + 1) * 128], w2[:], start=True, stop=True), r=[w2, gT], w=[pm2])
                        for ct in range(4):
                            P.op('act', lambda h, pm2=pm2, ct=ct: h.activation(out=VCA[:, ct, 0:128], in_=pm2[:, ct * 128:(ct + 1) * 128], func=AF.Copy), r=[pm2], w=[VCA])

            Et = Rot([P.sb("E%d" % i, [128, 256], F32) for i in range(3)])
            Pt = Rot([P.sb("P%d" % i, [128, 256], BF16) for i in range(4)])
            mk = Rot([P.sb("mk%d" % i, [128, 64], F32) for i in range(3)])
            impn = [P.sb("impn%d" % i, [128, 128], F32) for i in range(2)]
            accs = Rot([P.sb("acc%d" % i, [128, 128], F32) for i in range(4)])
            imp2 = P.sb("imp2", [64, 128], F32)
            imp3 = P.sb("imp3", [64, 128], F32)
            m8 = P.sb("m8", [64, 16], F32)
            sel = P.sb("sel", [64, 128], F32)
            selT = P.sb("selT", [128, 64], BF16)
            sm = Rot([P.sb("sm%d" % i, [128, 2], F32) for i in range(4)])
            P.op('dve', lambda h: h.memset(imp2[:], NEG), w=[imp2])

            pend = [None]

            def tile(i, Kap, Ktr, mask, Vap, Vtr, U, ncol, first, last):
                ps = pS.next()
                P.op('pe', lambda h: h.matmul(ps[:, 0:256], Kap, qT[:, :, i * 64:(i + 1) * 64], start=True, stop=True), r=[Ktr, qT], w=[ps])
                if pend[0] is not None:
                    pend[0]()
                    pend[0] = None
                pt = Pt.next()
                if mask is None:
                    P.op('act', lambda h: h.activation(out=pt[:], in_=ps[:, 0:256], func=AF.Exp, scale=SCALE), r=[ps], w=[pt])
                else:
                    e = Et.next()
                    P.op('act', lambda h: h.activation(out=e[:], in_=ps[:, 0:256], func=AF.Exp, scale=SCALE), r=[ps], w=[e])
                    kind = mask[0]
                    if kind == 'const':
                        mt, map_ = m4, m4[:, mask[1] * 64:(mask[1] + 1) * 64]
                    elif kind == 'c':
                        mt = mk.next()
                        th = float(mask[1])
                        P.op('dve', lambda h: h.tensor_scalar(out=mt[:], in0=tcc[:], scalar1=th, scalar2=None, op0=ALU.is_le), r=[tcc], w=[mt])
                        map_ = mt[:]
                    else:
                        kt = mask[1]
                        pM, po = pMs.next()
                        P.op('pe', lambda h: h.matmul(pM[:, po:po + 64], ex[:, kt * 128:(kt + 1) * 128], selT[:], start=True, stop=True), r=[ex, selT], w=[pM])
                        if mask[2] is None:
                            mt, map_ = pM, pM[:, po:po + 64]
                        else:
                            mt = mk.next()
                            di = mask[2]
                            P.op('dve', lambda h: h.tensor_tensor(out=mt[:], in0=pM[:, po:po + 64], in1=m4[:, di * 64:(di + 1) * 64], op=ALU.mult), r=[pM, m4], w=[mt])
                            map_ = mt[:]
                    P.op('dve', lambda h: h.tensor_tensor(out=pt[:].rearrange("p (h q) -> p h q", h=4), in0=e[:].rearrange("p (h q) -> p h q", h=4), in1=map_[:, None, :].broadcast_to([128, 4, 64]), op=ALU.mult), r=[e, mt], w=[pt])

                def pv():
                    for rt in range(2):
                        P.op('pe', lambda h, rt=rt: h.matmul(U[rt][:, 0:ncol], pt[:, rt * 128:(rt + 1) * 128], Vap, start=first, stop=last), r=[pt, Vtr], w=[U[rt]])
                pend[0] = pv

            def flush():
                if pend[0] is not None:
                    pend[0]()
                    pend[0] = None

            def do_qblock(i):
                gbase = i * 6
                Uc = [pU.next(), pU.next()]
                nct = (4 * i + 2) // 128 + 1
                for ct in range(nct):
                    th = 64 * i - 31 - 2048 * ct
                    tile(i, kcT[:, ct * 128:(ct + 1) * 128], kcT, ('c', th), VCA[:, ct, :], VCA, Uc, 257, ct == 0, ct == nct - 1)
                flush()
                acc = [accs.next(), accs.next()]
                for rt in range(2):
                    s_ = sm.next()
                    P.op('dve', lambda h, s_=s_, rt=rt: h.tensor_scalar(out=s_[:, 0:1], in0=Uc[rt][:, 256:257], scalar1=1e-30, scalar2=None, op0=ALU.max), r=[Uc[rt]], w=[s_])
                    P.op('dve', lambda h, s_=s_: h.reciprocal(out=s_[:, 0:1], in_=s_[:, 0:1]), r=[s_], w=[s_])
                    P.op('dve', lambda h, s_=s_, rt=rt: h.tensor_scalar(out=impn[rt][:], in0=Uc[rt][:, 128:256], scalar1=s_[:, 0:1], scalar2=None, op0=ALU.mult), r=[Uc[rt], s_], w=[impn[rt]])
                    P.op('dve', lambda h, s_=s_, rt=rt: h.tensor_tensor(out=s_[:, 1:2], in0=s_[:, 0:1], in1=sig[:, gbase + rt * 3:gbase + rt * 3 + 1], op=ALU.mult), r=[s_, sig], w=[s_])
                    P.op('dve', lambda h, s_=s_, rt=rt: h.tensor_scalar(out=acc[rt][:], in0=Uc[rt][:, 0:128], scalar1=s_[:, 1:2], scalar2=None, op0=ALU.mult), r=[Uc[rt], s_], w=[acc[rt]])
                if i >= 16:
                    for rt in range(2):
                        P.op('pe', lambda h, rt=rt: h.matmul(misc[0:64, 0:128], fold[:], impn[rt][:], start=(rt == 0), stop=(rt == 1)), r=[fold, impn[rt]], w=[misc])
                    P.op('dve', lambda h: h.tensor_copy(out=imp2[:, 0:i + 1], in_=misc[0:64, 0:i + 1]), r=[misc], w=[imp2])
                    P.op('dve', lambda h: h.memset(imp2[:, 0:1], FORCE), r=[], w=[imp2])
                    P.op('dve', lambda h: h.memset(imp2[:, i - 1:i + 1], FORCE), r=[], w=[imp2])
                    P.op('dve', lambda h: h.max(out=m8[:, 0:8], in_=imp2[:]), r=[imp2], w=[m8])
                    P.op('dve', lambda h: h.match_replace(out=imp3[:], in_to_replace=m8[:, 0:8], in_values=imp2[:], imm_value=-3e38), r=[imp2, m8], w=[imp3])
                    P.op('dve', lambda h: h.max(out=m8[:, 8:16], in_=imp3[:]), r=[imp3], w=[m8])
                    P.op('dve', lambda h: h.tensor_scalar(out=sel[:], in0=imp2[:], scalar1=m8[:, 15:16], scalar2=None, op0=ALU.is_ge), r=[imp2, m8], w=[sel])
                else:
                    P.op('dve', lambda h: h.memset(sel[:], 0.0), w=[sel])
                    P.op('dve', lambda h: h.memset(sel[:, 0:i + 1], 1.0), w=[sel])
                P.op('pe', lambda h: h.transpose(misc[:, 256:320], sel[:], ident[:]), r=[sel, ident], w=[misc])
                P.op('act', lambda h: h.activation(out=selT[:], in_=misc[:, 256:320], func=AF.Copy), r=[misc], w=[selT])
                Uw = [pU.next(), pU.next()]
                kt_lo = max(0, (64 * i - 511) // 128)
                kts = list(range(kt_lo, i // 2 + 1))
                for n, kt in enumerate(kts):
                    dl = 64 * i - 128 * kt
                    mask = ('const', DELTAS[dl]) if dl in DELTAS else None
                    tile(i, kwT[:, kt * 128:(kt + 1) * 128], kwT, mask, vwA[:, kt, :], vwA, Uw, 129, n == 0, n == len(kts) - 1)
                flush()

                def fin(U, col):
                    for rt in range(2):
                        s_ = sm.next()
                        P.op('dve', lambda h, s_=s_, rt=rt: h.reciprocal(out=s_[:, 0:1], in_=U[rt][:, 128:129]), r=[U[rt]], w=[s_])
                        P.op('dve', lambda h, s_=s_, rt=rt: h.tensor_tensor(out=s_[:, 1:2], in0=s_[:, 0:1], in1=sig[:, gbase + rt * 3 + col:gbase + rt * 3 + col + 1], op=ALU.mult), r=[s_, sig], w=[s_])
                        P.op('dve', lambda h, s_=s_, rt=rt: h.scalar_tensor_tensor(out=acc[rt][:], in0=U[rt][:, 0:128], scalar=s_[:, 1:2], in1=acc[rt][:], op0=ALU.mult, op1=ALU.add), r=[U[rt], s_, acc[rt]], w=[acc[rt]])
                fin(Uw, 2)
                Us = [pU.next(), pU.next()]
                kts = list(range(0, i // 2 + 1))
                for n, kt in enumerate(kts):
                    dl = 64 * i - 128 * kt
                    mask = ('sel', kt, DELTAS[dl] if dl in (0, 64) else None)
                    tile(i, ksT[:, kt * 128:(kt + 1) * 128], ksT, mask, vsA[:, kt, :], vsA, Us, 129, n == 0, n == len(kts) - 1)
                flush()
                fin(Us, 1)
                for rt in range(2):
                    P.dma('sp', oatt[i * 2 + rt], acc[rt][:], r=[acc[rt]])
            for i in qblocks:
                do_qblock(i)
            P.emit()
        print("progB: ops", P.total, "waits", P.nwait)
    return nc


def consts_B():
    kl = np.arange(128)[:, None]
    r = np.arange(64)[None, :]
    m4 = []
    for dl in (0, 64, 512, 576):
        m4.append(((kl - r <= dl) & (kl - r > dl - 512)).astype(np.float32))
    m4 = np.concatenate(m4, axis=1)
    tcc = (16 * kl - r).astype(np.float32)
    c = np.arange(512)[:, None]
    j = np.arange(128)[None, :]
    cover = ((16 * c < 64 * j + 64) & (16 * c + 32 > 64 * j)).astype(np.float32)
    cov = np.ascontiguousarray(cover.reshape(4, 128, 128).transpose(1, 0, 2)).reshape(128, 512)
    ex = (np.arange(S)[None, :] // 64 == np.arange(128)[:, None]).astype(np.float32)
    fold = np.tile(np.eye(64, dtype=np.float32), (2, 1))
    ident = np.eye(64, dtype=np.float32)
    return dict(m4=m4, tcc=tcc, cov=cov, ex=ex, fold=fold, ident=ident)


def prep_B_params(g, cmp_pos, cmp_w1, cmp_b1, cmp_w2, lru_conv_w, lru_conv_b, lru_wa, lru_ba, lru_wx, lru_bx, lru_lambda):
    cw1 = np.ascontiguousarray(cmp_w1.reshape(2, 32, 128, 128).transpose(0, 2, 1, 3)).reshape(2, 128, 32 * 128)
    cpos = np.ascontiguousarray(cmp_pos.transpose(0, 2, 1))
    cb1 = np.ascontiguousarray(cmp_b1.T)
    blks = [2 * g, 2 * g + 1]
    lcv = np.zeros((128, 2, 5), np.float32)
    lbl = np.zeros((128, 2, 3), np.float32)
    lw = np.zeros((2, 2, 128, 128), np.float32)
    for bi, n in enumerate(blks):
        sl = slice(n * 128, (n + 1) * 128)
        lcv[:, bi, 0:4] = lru_conv_w[:, sl].T
        lcv[:, bi, 4] = lru_conv_b[sl]
        lbl[:, bi, 0] = lru_ba[sl]
        lbl[:, bi, 1] = lru_bx[sl]
        lbl[:, bi, 2] = lru_lambda[sl]
        lw[bi, 0] = lru_wa[n]
        lw[bi, 1] = lru_wx[n]
    return dict(cw1=cw1, cpos=cpos, cb1=cb1, cw2=np.ascontiguousarray(cmp_w2), lcv=lcv.reshape(128, 10), lw=lw, lbl=lbl.reshape(128, 6))


MCT = 24


def build_M():
    nc = bass.Bass("TRN2", target_bir_lowering=False)
    cT_d = nc.dram_tensor("cT", [128, 64], F32, kind="ExternalInput").ap()
    wm_d = nc.dram_tensor("wm", [2 * MCT, 128, 4096], F32, kind="ExternalInput").ap()
    bm_d = nc.dram_tensor("bm", [128, 2 * MCT], F32, kind="ExternalInput").ap()
    mo_d = nc.dram_tensor("mo", [2 * MCT, 128, 2], F32, kind="ExternalOutput").ap()
    with ExitStack() as es:
        P = Prog(nc, es)
        cs = P.sb("cs", [128, 64], F32)
        bm = P.sb("bm", [128, 2 * MCT], F32)
        P.dma('sp', cs[:], cT_d, w=[cs])
        P.dma('sp', bm[:], bm_d, w=[bm])
        P.op('act', lambda h: h.activation(out=cs[:], in_=cs[:], func=AF.Silu), r=[cs], w=[cs])
        wt = Rot([P.sb("wt%d" % i, [128, 4096], F32) for i in range(3)])
        ot = Rot([P.sb("ot%d" % i, [128, 2], F32) for i in range(3)])
        pm = Rot([P.ps("pm%d" % i, [128, 512]) for i in range(3)])
        for j in range(2 * MCT):
            w = wt.next()
            P.dma('sp', w[:], wm_d[j], w=[w])
            p = pm.next()
            for kt in range(32):
                P.op('pe', lambda h, w=w, p=p, kt=kt: h.matmul(p[:, 0:2], w[:, kt * 128:(kt + 1) * 128], cs[:, 2 * kt:2 * kt + 2], start=(kt == 0), stop=(kt == 31)), r=[w, cs], w=[p])
            o = ot.next()
            P.op('act', lambda h, o=o, p=p, j=j: h.activation(out=o[:], in_=p[:, 0:2], func=AF.Identity, bias=bm[:, j:j + 1]), r=[p, bm], w=[o])
            P.dma('sp', mo_d[j], o[:], r=[o])
        P.emit()
    return nc


def run_M(c, w_mod, b_mod):
    nc = build_M()
    cT = np.ascontiguousarray(c.T.reshape(32, 128, 2).transpose(1, 0, 2)).reshape(128, 64)
    in_maps = []
    for core in range(8):
        c0 = core * MCT * 128
        wm = np.empty((2, MCT, 128, 32, 128), np.float32)
        bm = np.empty((128, 2, MCT), np.float32)
        for l in range(2):
            blk = w_mod[l][:, c0:c0 + MCT * 128].reshape(32, 128, MCT, 128)
            wm[l] = blk.transpose(2, 1, 0, 3)
            bm[:, l, :] = b_mod[l][c0:c0 + MCT * 128].reshape(MCT, 128).T
        in_maps.append(dict(cT=cT, wm=wm.reshape(2 * MCT, 128, 4096), bm=bm.reshape(128, 2 * MCT)))
    res = run_bass_kernel_spmd(nc, in_maps, core_ids=list(range(8)))
    mod = np.empty((2, 2, 24576), np.float32)
    for core in range(8):
        mo = res.results[core]["mo"].reshape(2, MCT, 128, 2)
        c0 = core * MCT * 128
        for l in range(2):
            mod[l][:, c0:c0 + MCT * 128] = mo[l].transpose(2, 0, 1).reshape(2, MCT * 128)
    return mod

import time as _time

_PROGS = {}


def _get(name, fn):
    if name not in _PROGS:
        _PROGS[name] = fn()
    return _PROGS[name]


def _modv(m):
    return np.ascontiguousarray(m.reshape(6, 32, 128).transpose(2, 0, 1)).reshape(128, 192)


def kernel(x, c, w_mod, b_mod, w_in, cmp_pos, cmp_w1, cmp_b1, cmp_w2, lru_conv_w, lru_conv_b,
           lru_wa, lru_ba, lru_wx, lru_bx, lru_lambda, sgu_ln_g, sgu_ln_b, sgu_w, sgu_b, w_o,
           ln1_g, ln1_b, ffn_w_in, ffn_conv_w, ffn_conv_b, ffn_w_down, ln2_g, ln2_b):
    t_start = _time.time()

    def log(msg):
        print("[kernel %.0fs] %s" % (_time.time() - t_start, msg), flush=True)

    f32 = lambda a: np.ascontiguousarray(np.asarray(a, dtype=np.float32))
    x = f32(x)
    c = f32(c)
    B, SEQ, D = x.shape
    NCORE = 8
    JB = 2
    NTA_ = SEQ // JB
    cores = [(b, j) for b in range(B) for j in range(JB)]
    mod = run_M(c, f32(w_mod), f32(b_mod))
    log("mod done")
    xT = [fm(x[b, j * NTA_:(j + 1) * NTA_]) for (b, j) in cores]
    cB = consts_B()
    for l in range(2):
        WA = prep_A_weights(f32(w_in[l]), f32(sgu_ln_g[l]), f32(sgu_ln_b[l]), f32(sgu_w[l]), f32(sgu_b[l]))
        ncA = _get("A", lambda: build_A(NTA_))
        in_maps = [dict(xT=xT[k], modv=_modv(mod[l, b]), **WA) for k, (b, j) in enumerate(cores)]
        resA = run_bass_kernel_spmd(ncA, in_maps, core_ids=list(range(len(cores)))).results
        del WA, in_maps
        log("layer %d A done" % l)
        ncB = _get("B", build_B)
        in_maps = []
        for b in range(B):
            zfm = np.concatenate([resA[b * JB + j]["zfm"] for j in range(JB)], axis=2)
            vsw = np.concatenate([resA[b * JB + j]["vsw"] for j in range(JB)], axis=1)
            glo = np.concatenate([resA[b * JB + j]["glo"] for j in range(JB)], axis=0).reshape(SEQ, 48)
            for g in range(4):
                qT_ = np.ascontiguousarray(zfm[4 * g:4 * g + 4])
                kT_ = np.stack([zfm[16 + g], zfm[20 + g], zfm[24 + g], zfm[28 + g]])
                vtm = np.stack([np.ascontiguousarray(vsw[v][:, :, g * 128:(g + 1) * 128].transpose(1, 0, 2)) for v in range(2)])
                gl = glo[:, g * 12:(g + 1) * 12]
                glr = np.ascontiguousarray(gl.reshape(128, 64, 2, 2, 3).transpose(3, 1, 0, 2, 4)).reshape(128, 128 * 6)
                lruin = np.stack([np.stack([zfm[32 + 2 * g], zfm[32 + 2 * g + 1]]), np.stack([zfm[40 + 2 * g], zfm[40 + 2 * g + 1]])])
                pb = prep_B_params(g, f32(cmp_pos[l]), f32(cmp_w1[l]), f32(cmp_b1[l]), f32(cmp_w2[l]), f32(lru_conv_w[l]),
                                   f32(lru_conv_b[l]), f32(lru_wa[l]), f32(lru_ba[l]), f32(lru_wx[l]), f32(lru_bx[l]), f32(lru_lambda[l]))
                in_maps.append(dict(qT=qT_, kT=kT_, vtm=vtm, glr=glr, lruin=lruin, **pb, **cB))
        resB = run_bass_kernel_spmd(ncB, in_maps, core_ids=list(range(NCORE))).results
        del in_maps
        log("layer %d B done" % l)
        WC = prep_C_weights(f32(w_o[l]), f32(ffn_w_in[l]), f32(ffn_conv_w[l]), f32(ffn_conv_b[l]), f32(ffn_w_down[l]),
                            f32(ln1_g[l]), f32(ln1_b[l]), f32(ln2_g[l]), f32(ln2_b[l]))
        ncC = _get("C", lambda: build_C(NTA_, True))
        in_maps = []
        for b in range(B):
            yT = np.empty((32, 128, SEQ), np.float32)
            for g in range(4):
                oa = resB[b * 4 + g]["oatt"].reshape(128, 2, 2, 64, 128)
                yT[4 * g:4 * g + 4] = oa.transpose(1, 2, 4, 0, 3).reshape(4, 128, SEQ)
                yl = resB[b * 4 + g]["ylru"]
                yT[16 + 2 * g] = yl[0]
                yT[16 + 2 * g + 1] = yl[1]
            for j in range(JB):
                yT[24:32, :, j * NTA_:(j + 1) * NTA_] = resA[b * JB + j]["ysg"]
            xfull = np.concatenate([xT[b * JB + j] for j in range(JB)], axis=2)
            for j in range(JB):
                t0 = j * NTA_
                yh = np.zeros((32, 128, NTA_ + 2), np.float32)
                xh = np.zeros((32, 128, NTA_ + 2), np.float32)
                lo = max(t0 - 2, 0)
                yh[:, :, 2 - (t0 - lo):] = yT[:, :, lo:t0 + NTA_]
                xh[:, :, 2 - (t0 - lo):] = xfull[:, :, lo:t0 + NTA_]
                hflag = np.full((128, 1), 1.0 if j > 0 else 0.0, np.float32)
                in_maps.append(dict(yT=yh, xT=xh, hflag=hflag, modv=_modv(mod[l, b]), **WC))
        del resA, resB
        resC = run_bass_kernel_spmd(ncC, in_maps, core_ids=list(range(len(cores)))).results
        del in_maps, WC
        xT = [resC[k]["x2T"] for k in range(len(cores))]
        log("layer %d C done" % l)
    out = np.empty((B, SEQ, D), np.float32)
    for k, (b, j) in enumerate(cores):
        out[b, j * NTA_:(j + 1) * NTA_] = xT[k].reshape(D, NTA_).T
    return out
```

```python
import numpy as np
from contextlib import ExitStack
import concourse.bass as bass
import concourse.mybir as mybir
from concourse.bass_utils import run_bass_kernel_spmd

F32 = mybir.dt.float32
BF16 = mybir.dt.bfloat16
AF = mybir.ActivationFunctionType
ALU = mybir.AluOpType
AX = mybir.AxisListType

ENGS = ['pe', 'act', 'dve', 'pool', 'sp']
NDS = 8


class T:
    __slots__ = ('t', 'w', 'r')

    def __init__(self, t=None):
        self.t = t
        self.w = None
        self.r = []

    def __getitem__(self, k):
        return self.t[k]


class Prog:
    def __init__(self, nc, es, same_engine_wait=True):
        self.nc = nc
        self.es = es
        self.h = {'pe': nc.tensor, 'act': nc.scalar, 'dve': nc.vector, 'pool': nc.gpsimd, 'sp': nc.sync}
        self.ops = {e: [] for e in ENGS}
        self.sems = {}
        for e in ENGS:
            self.sems[e] = es.enter_context(nc.semaphore('s_' + e))
        self.cnt = {e: 0 for e in ENGS}
        self.known = {e: {} for e in ENGS}
        self.dq = {}
        for q in ('sp', 'pool', 'act'):
            for i in range(NDS):
                self.sems[(q, i)] = es.enter_context(nc.semaphore('d_%s%d' % (q, i)))
            self.dq[q] = 0
        self.dval = {}
        self.sew = same_engine_wait
        self.nwait = 0

    def sb(self, name, shape, dt):
        return T(self.es.enter_context(self.nc.sbuf_tensor("sb_" + name, list(shape), dt)))

    def ps(self, name, shape, dt=F32):
        return T(self.es.enter_context(self.nc.psum_tensor("ps_" + name, list(shape), dt)))

    def _wait(self, eng, tok):
        key, val = tok
        if key == eng and (eng == 'pe' or not self.sew):
            return
        if self.known[eng].get(key, 0) >= val:
            return
        self.known[eng][key] = val
        sem = self.sems[key]
        self.ops[eng].append(lambda h, sem=sem, val=val: h.wait_ge(sem, val))
        self.nwait += 1

    def _deps(self, eng, r, w):
        need = {}
        for t in r:
            if t.w is not None:
                k, v = t.w
                if need.get(k, 0) < v:
                    need[k] = v
        for t in w:
            if t.w is not None:
                k, v = t.w
                if need.get(k, 0) < v:
                    need[k] = v
            for k, v in t.r:
                if need.get(k, 0) < v:
                    need[k] = v
        for k, v in need.items():
            self._wait(eng, (k, v))

    def op(self, eng, fn, r=(), w=()):
        self._deps(eng, r, w)
        self.cnt[eng] += 1
        tok = (eng, self.cnt[eng])
        sem = self.sems[eng]
        self.ops[eng].append(lambda h, fn=fn, sem=sem: fn(h).then_inc(sem, 1))
        for t in w:
            t.w = tok
            t.r = []
        for t in r:
            if t not in w:
                self._addr(t, tok)
        return tok

    def _addr(self, t, tok):
        k, v = tok
        for i, (k2, v2) in enumerate(t.r):
            if k2 == k:
                if v2 < v:
                    t.r[i] = tok
                return
        t.r.append(tok)

    def dma(self, q, out, in_, r=(), w=(), **kw):
        n = self.dq[q]
        self.dq[q] = n + 1
        key = (q, n % NDS)
        val = 16 * (n // NDS + 1)
        if val > 16:
            self._wait(q, (key, val - 16))
        self._deps(q, r, w)
        sem = self.sems[key]
        self.ops[q].append(lambda h, out=out, in_=in_, sem=sem, kw=kw: h.dma_start(out=out, in_=in_, **kw).then_inc(sem, 16))
        self.dval[key] = val
        tok = (key, val)
        for t in w:
            t.w = tok
            t.r = []
        for t in r:
            if t not in w:
                self._addr(t, tok)
        return tok

    def barrier(self):
        toks = [(e, self.cnt[e]) for e in ENGS if self.cnt[e] > 0]
        toks += [(k, v) for k, v in self.dval.items()]
        for e in ENGS:
            for tok in toks:
                if tok[0] != e:
                    self._wait(e, tok)
                elif e != 'pe':
                    self._wait(e, tok)

    def emit(self):
        self.barrier()
        nc = self.nc
        self.total = getattr(self, 'total', 0) + sum(len(v) for v in self.ops.values())
        self._emit_block()
        self.ops = {e: [] for e in ENGS}

    def _emit_block(self):
        nc = self.nc
        with nc.Block() as block:
            @block.tensor
            def _(h):
                for f in self.ops['pe']:
                    f(h)

            @block.scalar
            def _(h):
                for f in self.ops['act']:
                    f(h)

            @block.vector
            def _(h):
                for f in self.ops['dve']:
                    f(h)

            @block.gpsimd
            def _(h):
                for f in self.ops['pool']:
                    f(h)

            @block.sync
            def _(h):
                for f in self.ops['sp']:
                    f(h)


ALPHA = (2.0 * 2) ** 0.25
NT = 2048
TB = 512
KT = 32
NFC = 86
EPS = 1e-5


class Rot:
    def __init__(self, tiles):
        self.tiles = tiles
        self.i = 0

    def next(self):
        t = self.tiles[self.i % len(self.tiles)]
        self.i += 1
        return t


def build_C(NT=NT, with_halo=True):
    HW = 2 if with_halo else 0
    nc = bass.Bass("TRN2", target_bir_lowering=False)

    def din(name, shape, dt=F32):
        return nc.dram_tensor(name, shape, dt, kind="ExternalInput").ap()

    yT = din("yT", [32, 128, NT + HW])
    xT = din("xT", [32, 128, NT + HW])
    hflag = din("hflag", [128, 1])
    modv = din("modv", [128, 192])
    lnp = din("lnp", [128, 128])
    wo_t = din("wo_t", [32, 128, 4096])
    wi_t = din("wi_t", [NFC, 128, 2, 4096])
    wd_t = din("wd_t", [32, 128, NFC * 128])
    fcv = din("fcv", [128, NFC * 4])
    x2T = nc.dram_tensor("x2T", [32, 128, NT], F32, kind="ExternalOutput").ap()
    vs = nc.dram_tensor("vs", [32, 128, TB], F32, kind="Internal").ap()
    x1s = nc.dram_tensor("x1s", [32, 128, TB], F32, kind="Internal").ap()

    with ExitStack() as es:
        P = Prog(nc, es)
        modv_sb = P.sb("modv_sb", [128, 192], F32)
        lnp_sb = P.sb("lnp_sb", [128, 128], F32)
        fcv_sb = P.sb("fcv_sb", [128, NFC * 4], F32)
        hflag_sb = P.sb("hflag_sb", [128, 1], F32)
        ones_sb = P.sb("ones_sb", [128, 128], F32)
        op1 = P.sb("op1", [128, 192], F32)
        P.dma('sp', modv_sb[:], modv, w=[modv_sb])
        P.dma('sp', lnp_sb[:], lnp, w=[lnp_sb])
        P.dma('sp', fcv_sb[:], fcv, w=[fcv_sb])
        P.dma('sp', hflag_sb[:], hflag, w=[hflag_sb])
        P.op('dve', lambda h: h.memset(ones_sb[:], 1.0), w=[ones_sb])
        P.op('dve', lambda h: h.tensor_scalar(out=op1[:], in0=modv_sb[:], scalar1=1.0, scalar2=None, op0=ALU.add), r=[modv_sb], w=[op1])

        R1 = es.enter_context(nc.sbuf_tensor("R1", [128, NFC * TB], BF16))
        aT = [T(R1) for _ in range(NFC)]
        hTt = es.enter_context(nc.sbuf_tensor("hT", [128, KT * TB], BF16))
        hT = [T(hTt) for _ in range(KT)]
        WBN = 43 * 128
        wb = Rot([P.sb("wb%d" % i, [128, WBN], BF16) for i in range(3)])
        ghalo = P.sb("ghalo", [128, NFC * 2], F32)
        tp = Rot([P.sb("tp%d" % i, [128, TB + 2], F32) for i in range(10)])
        xt = vt = sq = t1 = vl = nt_ = xo = gext = acc = sl = tp
        mean = P.sb("mean", [128, TB], F32)
        rstd = P.sb("rstd", [128, TB], F32)
        nmr = P.sb("nmr", [128, TB], F32)
        msq = nmr
        mm = Rot([P.ps("mm%d" % i, [128, TB]) for i in range(4)])
        ps_sum = P.ps("ps_sum", [128, TB])
        ps_sq = P.ps("ps_sq", [128, TB])
        vsT = [T(vs) for _ in range(32)]
        x1sT = [T(x1s) for _ in range(32)]

        def load_w(src2d, ncols):
            wt = wb.next()
            P.dma('pool', wt[:, 0:ncols], src2d, w=[wt], max_dma_last_dim=4096)
            return wt

        def stats_finish(W):
            P.op('act', lambda h: h.activation(out=mean[:, 0:W], in_=ps_sum[:, 0:W], func=AF.Copy, scale=1.0 / 4096), r=[ps_sum], w=[mean])
            P.op('dve', lambda h: h.tensor_tensor(out=msq[:, 0:W], in0=mean[:, 0:W], in1=mean[:, 0:W], op=ALU.mult), r=[mean], w=[msq])
            P.op('dve', lambda h: h.scalar_tensor_tensor(out=rstd[:, 0:W], in0=ps_sq[:, 0:W], scalar=1.0 / 4096, in1=msq[:, 0:W], op0=ALU.mult, op1=ALU.subtract), r=[ps_sq, msq], w=[rstd])
            P.op('dve', lambda h: h.tensor_scalar(out=rstd[:, 0:W], in0=rstd[:, 0:W], scalar1=EPS, scalar2=None, op0=ALU.add), r=[rstd], w=[rstd])
            P.op('act', lambda h: h.activation(out=rstd[:, 0:W], in_=rstd[:, 0:W], func=AF.Sqrt), r=[rstd], w=[rstd])
            P.op('dve', lambda h: h.reciprocal(out=rstd[:, 0:W], in_=rstd[:, 0:W]), r=[rstd], w=[rstd])
            P.op('dve', lambda h: h.scalar_tensor_tensor(out=nmr[:, 0:W], in0=mean[:, 0:W], scalar=-1.0, in1=rstd[:, 0:W], op0=ALU.mult, op1=ALU.mult), r=[mean, rstd], w=[nmr])

        def ln_sublayer(W, main_fn, xsrc_fn, gcol, lng, lnb, out_fn):
            pending = None
            xcur = xt.next()
            ap, trk = xsrc_fn(0)
            P.dma('sp', xcur[:, 0:W], ap, r=trk, w=[xcur])
            for dt in range(32):
                pm = main_fn(dt)
                if pending is not None:
                    pending()
                xnext = None
                if dt + 1 < 32:
                    xnext = xt.next()
                    ap, trk = xsrc_fn(dt + 1)
                    P.dma('sp', xnext[:, 0:W], ap, r=trk, w=[xnext])
                a1 = t1.next()
                v = vt.next()
                s = sq.next()
                P.op('act', lambda h, a1=a1, pm=pm, dt=dt: h.activation(out=a1[:, 0:W], in_=pm[:, 0:W], func=AF.Identity, scale=op1[:, gcol * 32 + dt:gcol * 32 + dt + 1]), r=[pm, op1], w=[a1])
                P.op('dve', lambda h, v=v, a1=a1, xc=xcur: h.scalar_tensor_tensor(out=v[:, 0:W], in0=xc[:, 0:W], scalar=ALPHA, in1=a1[:, 0:W], op0=ALU.mult, op1=ALU.add), r=[xcur, a1], w=[v])
                P.op('act', lambda h, v=v, s=s: h.activation(out=s[:, 0:W], in_=v[:, 0:W], func=AF.Square), r=[v], w=[s])
                P.dma('sp', vs[dt, :, 0:W], v[:, 0:W], r=[v], w=[vsT[dt]])

                def st(v=v, s=s, dt=dt):
                    P.op('pe', lambda h: h.matmul(ps_sum[:, 0:W], ones_sb[:], v[:, 0:W], start=(dt == 0), stop=(dt == 31)), r=[ones_sb, v], w=[ps_sum])
                    P.op('pe', lambda h: h.matmul(ps_sq[:, 0:W], ones_sb[:], s[:, 0:W], start=(dt == 0), stop=(dt == 31)), r=[ones_sb, s], w=[ps_sq])
                pending = st
                xcur = xnext
            pending()
            stats_finish(W)
            vcur = vl.next()
            P.dma('sp', vcur[:, 0:W], vs[0, :, 0:W], r=[vsT[0]], w=[vcur])
            for dt in range(32):
                vnext = None
                if dt + 1 < 32:
                    vnext = vl.next()
                    P.dma('sp', vnext[:, 0:W], vs[dt + 1, :, 0:W], r=[vsT[dt + 1]], w=[vnext])
                n = nt_.next()
                P.op('dve', lambda h, n=n, vc=vcur: h.tensor_tensor(out=n[:, 0:W], in0=vc[:, 0:W], in1=rstd[:, 0:W], op=ALU.mult), r=[vcur, rstd], w=[n])
                P.op('dve', lambda h, n=n: h.tensor_tensor(out=n[:, 0:W], in0=n[:, 0:W], in1=nmr[:, 0:W], op=ALU.add), r=[n, nmr], w=[n])
                o = xo.next()
                P.op('act', lambda h, n=n, o=o, dt=dt: h.activation(out=o[:, 0:W], in_=n[:, 0:W], func=AF.Identity, scale=lnp_sb[:, lng * 32 + dt:lng * 32 + dt + 1], bias=lnp_sb[:, lnb * 32 + dt:lnb * 32 + dt + 1]), r=[n, lnp_sb], w=[o])
                out_fn(dt, o)
                vcur = vnext

        def block(col0, W, halo):
            for kt in range(KT):
                P.dma('pool', R1[:, kt * TB:kt * TB + W], yT[kt, :, col0:col0 + W], w=[aT[kt]], max_dma_last_dim=2048)

            def main1(dt):
                wt = load_w(wo_t[dt], 4096)
                pm = mm.next()
                for kt in range(KT):
                    P.op('pe', lambda h, kt=kt, wt=wt, pm=pm: h.matmul(pm[:, 0:W], wt[:, kt * 128:(kt + 1) * 128], R1[:, kt * TB:kt * TB + W], start=(kt == 0), stop=(kt == KT - 1)), r=[wt, aT[kt]], w=[pm])
                return pm

            def xsrc1(dt):
                return xT[dt, :, col0:col0 + W], []

            def out1(dt, o):
                if not halo:
                    P.dma('sp', x1s[dt, :, 0:W], o[:, 0:W], r=[o], w=[x1sT[dt]])
                P.op('act', lambda h: h.activation(out=hTt[:, dt * TB:dt * TB + W], in_=o[:, 0:W], func=AF.Identity, scale=op1[:, 4 * 32 + dt:4 * 32 + dt + 1], bias=modv_sb[:, 3 * 32 + dt:3 * 32 + dt + 1]), r=[o, op1, modv_sb], w=[hT[dt]])

            ln_sublayer(W, main1, xsrc1, 2, 0, 1, out1)

            for c in range(NFC):
                wg = load_w(wi_t[c, :, 0, :], 4096)
                pg = mm.next()
                for kt in range(KT):
                    P.op('pe', lambda h, kt=kt, wg=wg, pg=pg: h.matmul(pg[:, 0:W], wg[:, kt * 128:(kt + 1) * 128], hTt[:, kt * TB:kt * TB + W], start=(kt == 0), stop=(kt == KT - 1)), r=[wg, hT[kt]], w=[pg])
                if halo:
                    P.op('act', lambda h, c=c, pg=pg: h.activation(out=ghalo[:, 2 * c:2 * c + 2], in_=pg[:, 0:2], func=AF.Identity, scale=hflag_sb[:, 0:1]), r=[pg, hflag_sb], w=[ghalo])
                    continue
                wu = load_w(wi_t[c, :, 1, :], 4096)
                pu = mm.next()
                for kt in range(KT):
                    P.op('pe', lambda h, kt=kt, wu=wu, pu=pu: h.matmul(pu[:, 0:W], wu[:, kt * 128:(kt + 1) * 128], hTt[:, kt * TB:kt * TB + W], start=(kt == 0), stop=(kt == KT - 1)), r=[wu, hT[kt]], w=[pu])
                ge = gext.next()
                ac = acc.next()
                s_ = sl.next()
                P.op('act', lambda h, ge=ge, c=c: h.activation(out=ge[:, 0:2], in_=ghalo[:, 2 * c:2 * c + 2], func=AF.Copy), r=[ghalo], w=[ge])
                P.op('act', lambda h, ge=ge, pg=pg: h.activation(out=ge[:, 2:2 + W], in_=pg[:, 0:W], func=AF.Copy), r=[pg], w=[ge])
                P.op('act', lambda h, ge=ge, c=c: h.activation(out=ghalo[:, 2 * c:2 * c + 2], in_=ge[:, W:W + 2], func=AF.Copy), r=[ge], w=[ghalo])
                f0 = 4 * c
                P.op('dve', lambda h, ge=ge, ac=ac, f0=f0: h.tensor_scalar(out=ac[:, 0:W], in0=ge[:, 2:2 + W], scalar1=fcv_sb[:, f0 + 2:f0 + 3], scalar2=fcv_sb[:, f0 + 3:f0 + 4], op0=ALU.mult, op1=ALU.add), r=[ge, fcv_sb], w=[ac])
                P.op('dve', lambda h, ge=ge, ac=ac, f0=f0: h.scalar_tensor_tensor(out=ac[:, 0:W], in0=ge[:, 1:1 + W], scalar=fcv_sb[:, f0 + 1:f0 + 2], in1=ac[:, 0:W], op0=ALU.mult, op1=ALU.add), r=[ge, fcv_sb, ac], w=[ac])
                P.op('dve', lambda h, ge=ge, ac=ac, f0=f0: h.scalar_tensor_tensor(out=ac[:, 0:W], in0=ge[:, 0:W], scalar=fcv_sb[:, f0:f0 + 1], in1=ac[:, 0:W], op0=ALU.mult, op1=ALU.add), r=[ge, fcv_sb, ac], w=[ac])
                P.op('act', lambda h, ac=ac, s_=s_: h.activation(out=s_[:, 0:W], in_=ac[:, 0:W], func=AF.Silu), r=[ac], w=[s_])
                P.op('dve', lambda h, s_=s_, pu=pu, c=c: h.tensor_tensor(out=R1[:, c * TB:c * TB + W], in0=s_[:, 0:W], in1=pu[:, 0:W], op=ALU.mult), r=[s_, pu], w=[aT[c]])
            if halo:
                return

            def main3(dt):
                pm = mm.next()
                for half in range(2):
                    wt = load_w(wd_t[dt, :, half * 43 * 128:(half + 1) * 43 * 128], 43 * 128)
                    for cc in range(43):
                        c = half * 43 + cc
                        P.op('pe', lambda h, cc=cc, c=c, wt=wt, pm=pm: h.matmul(pm[:, 0:W], wt[:, cc * 128:(cc + 1) * 128], R1[:, c * TB:c * TB + W], start=(c == 0), stop=(c == NFC - 1)), r=[wt, aT[c]], w=[pm])
                return pm

            def xsrc3(dt):
                return x1s[dt, :, 0:W], [x1sT[dt]]

            def out3(dt, o):
                P.dma('sp', x2T[dt, :, col0 - HW:col0 - HW + W], o[:, 0:W], r=[o])

            ln_sublayer(W, main3, xsrc3, 5, 2, 3, out3)

        if with_halo:
            block(0, 2, True)
        else:
            P.op('dve', lambda h: h.memset(ghalo[:], 0.0), w=[ghalo])
        for b in range(NT // TB):
            block(HW + b * TB, TB, False)
        P.emit()
        print("progC: ops", {e: len(P.ops[e]) for e in ENGS}, "waits", P.nwait)
    return nc


def prep_C_weights(w_o, ffn_w_in, ffn_conv_w, ffn_conv_b, ffn_w_down, ln1_g, ln1_b, ln2_g, ln2_b):
    wo_t = np.ascontiguousarray(w_o.reshape(32, 128, 32, 128).transpose(2, 1, 0, 3)).reshape(32, 128, 4096)
    wi = ffn_w_in.reshape(32, 128, 2, NFC, 128)
    wi_t = np.ascontiguousarray(wi.transpose(3, 1, 2, 0, 4)).reshape(NFC, 128, 2, 4096)
    wd = ffn_w_down.reshape(NFC, 128, 32, 128)
    wd_t = np.ascontiguousarray(wd.transpose(2, 1, 0, 3)).reshape(32, 128, NFC * 128)
    fc = np.concatenate([ffn_conv_w, ffn_conv_b[None]], axis=0)
    fcv = np.ascontiguousarray(fc.reshape(4, NFC, 128).transpose(2, 1, 0)).reshape(128, NFC * 4)
    lnp = np.stack([ln1_g, ln1_b, ln2_g, ln2_b], 0).reshape(4, 32, 128)
    lnp = np.ascontiguousarray(lnp.transpose(2, 0, 1)).reshape(128, 128)
    return dict(wo_t=wo_t, wi_t=wi_t, wd_t=wd_t, fcv=fcv, lnp=lnp)


def fm(a):
    return np.ascontiguousarray(a.T).reshape(32, 128, a.shape[0])


NTA = 2048
HALF = 1024
NFM = 56
NOUT = 48
EPS = 1e-5
C_Q, C_KC, C_VC, C_KS, C_VS, C_KW, C_VW, C_GL, C_ZG, C_ZR, C_ZU, C_ZV = 0, 2048, 2560, 3072, 3584, 4096, 4608, 5120, 5168, 6192, 7216, 8240
FM_COLS = ([C_Q + 128 * i for i in range(16)] + [C_KC + 128 * i for i in range(4)] + [C_VC + 128 * i for i in range(4)]
           + [C_KS + 128 * i for i in range(4)] + [C_KW + 128 * i for i in range(4)] + [C_ZG + 128 * i for i in range(8)]
           + [C_ZR + 128 * i for i in range(8)] + [C_ZU + 128 * i for i in range(8)])
TM_COLS = [C_VS, C_VW, C_ZV, C_ZV + 512]


def build_A(NTA=NTA):
    nc = bass.Bass("TRN2", target_bir_lowering=False)

    def din(name, shape, dt=F32):
        return nc.dram_tensor(name, shape, dt, kind="ExternalInput").ap()

    def dout(name, shape, dt=F32):
        return nc.dram_tensor(name, shape, dt, kind="ExternalOutput").ap()

    xT = din("xT", [32, 128, NTA])
    modv = din("modv", [128, 192])
    wfm = din("wfm", [NFM, 128, 4096])
    wtm = din("wtm", [4, 2, 128, 16 * 512])
    wgl = din("wgl", [128, 32 * 48])
    lngb = din("lngb", [2, 128, 1024])
    swT = din("swT", [128, 8 * 128])
    tmask = din("tmask", [128, 128])
    sbb = din("sbb", [128, 8 * 512])
    zfm = dout("zfm", [NOUT, 128, NTA])
    vsw = dout("vsw", [2, NTA // 128, 128, 512])
    glo = dout("glo", [NTA // 128, 128, 48])
    ysg = dout("ysg", [8, 128, NTA])

    with ExitStack() as es:
        P = Prog(nc, es)
        modv_sb = P.sb("modv_sb", [128, 192], F32)
        op1 = P.sb("op1", [128, 192], F32)
        lng = P.sb("lng", [128, 1024], F32)
        lnb = P.sb("lnb", [128, 1024], F32)
        sw32 = P.sb("sw32", [128, 1024], F32)
        swb = P.sb("swb", [128, 1024], BF16)
        tm = P.sb("tm", [128, 128], F32)
        sbb_sb = P.sb("sbb_sb", [128, 8 * 512], F32)
        wgl_sb = P.sb("wgl_sb", [128, 32 * 48], BF16)
        P.dma('sp', modv_sb[:], modv, w=[modv_sb])
        P.dma('sp', lng[:], lngb[0], w=[lng])
        P.dma('sp', lnb[:], lngb[1], w=[lnb])
        P.dma('sp', sw32[:], swT, w=[sw32])
        P.dma('sp', tm[:], tmask, w=[tm])
        P.dma('sp', sbb_sb[:], sbb, w=[sbb_sb])
        P.dma('pool', wgl_sb[:], wgl, w=[wgl_sb], max_dma_last_dim=4096)
        P.op('dve', lambda h: h.tensor_scalar(out=op1[:], in0=modv_sb[:], scalar1=1.0, scalar2=None, op0=ALU.add), r=[modv_sb], w=[op1])
        for g in range(8):
            P.op('dve', lambda h, g=g: h.tensor_tensor(out=swb[:, g * 128:(g + 1) * 128], in0=sw32[:, g * 128:(g + 1) * 128], in1=tm[:], op=ALU.mult), r=[sw32, tm], w=[swb])

        Q = 512
        hTt = es.enter_context(nc.sbuf_tensor("hT", [128, 32 * Q], BF16))
        hT = [T(hTt) for _ in range(32)]
        vnt = es.enter_context(nc.sbuf_tensor("vn", [128, 4 * 1024], BF16))
        vn = [T(vnt) for _ in range(4)]
        wb = Rot([P.sb("wb%d" % i, [128, 16 * 512], BF16) for i in range(3)])
        xin = Rot([P.sb("xin%d" % i, [128, Q], F32) for i in range(2)])
        ot = Rot([P.sb("ot%d" % i, [128, 512], F32) for i in range(4)])
        gvs = [P.sb("gv%d" % i, [128, 1024], F32) for i in range(4)]
        gut = P.sb("gu", [128, Q], F32)
        st6 = P.sb("st6", [128, 12], F32)
        mv = P.sb("mv", [128, 2], F32)
        rs = P.sb("rs", [128, 1], F32)
        mm = Rot([P.ps("mm%d" % i, [128, 512]) for i in range(6)])

        evi = [0]

        def evac(dst, src, r, w):
            evi[0] += 1
            if evi[0] % 2:
                P.op('act', lambda h: h.activation(out=dst, in_=src, func=AF.Copy), r=r, w=w)
            else:
                P.op('dve', lambda h: h.tensor_copy(out=dst, in_=src), r=r, w=w)

        for qi in range(NTA // Q):
            t0 = qi * Q
            for kt in range(32):
                xi = xin.next()
                P.dma('sp', xi[:], xT[kt, :, t0:t0 + Q], w=[xi])
                P.op('act', lambda h, xi=xi, kt=kt: h.activation(out=hTt[:, kt * Q:(kt + 1) * Q], in_=xi[:], func=AF.Identity, scale=op1[:, 32 + kt:33 + kt], bias=modv_sb[:, kt:kt + 1]), r=[xi, op1, modv_sb], w=[hT[kt]])
            for gi in (2, 3, 0, 1):
                wh = []
                for kh in range(2):
                    wt = wb.next()
                    P.dma('pool', wt[:], wtm[gi, kh], w=[wt], max_dma_last_dim=4096)
                    wh.append(wt)
                for ch in range(4):
                    pm = mm.next()
                    for kt in range(32):
                        wt = wh[kt // 16]
                        kk = kt % 16
                        P.op('pe', lambda h, pm=pm, kt=kt, wt=wt, kk=kk, ch=ch: h.matmul(pm[:], hTt[:, kt * Q + ch * 128:kt * Q + (ch + 1) * 128], wt[:, kk * 512:(kk + 1) * 512], start=(kt == 0), stop=(kt == 31)), r=[hT[kt], wt], w=[pm])
                    if gi < 2:
                        o = ot.next()
                        evac(o[:], pm[:], [pm], [o])
                        P.dma('sp', vsw[gi, qi * 4 + ch], o[:], r=[o])
                    else:
                        gv = gvs[ch]
                        P.op('act', lambda h, gv=gv, pm=pm, gi=gi: h.activation(out=gv[:, (gi - 2) * 512:(gi - 1) * 512], in_=pm[:], func=AF.Gelu_apprx_tanh), r=[pm], w=[gv])
                        if gi == 3:
                            P.op('dve', lambda h, gv=gv: h.bn_stats(out=st6[:, 0:6], in_=gv[:, 0:512]), r=[gv], w=[st6])
                            P.op('dve', lambda h, gv=gv: h.bn_stats(out=st6[:, 6:12], in_=gv[:, 512:1024]), r=[gv, st6], w=[st6])
                            P.op('dve', lambda h: h.bn_aggr(out=mv[:], in_=st6[:]), r=[st6], w=[mv])
                            P.op('dve', lambda h: h.tensor_scalar(out=rs[:], in0=mv[:, 1:2], scalar1=EPS, scalar2=None, op0=ALU.add), r=[mv], w=[rs])
                            P.op('act', lambda h: h.activation(out=rs[:], in_=rs[:], func=AF.Sqrt), r=[rs], w=[rs])
                            P.op('dve', lambda h: h.reciprocal(out=rs[:], in_=rs[:]), r=[rs], w=[rs])
                            P.op('dve', lambda h, gv=gv: h.tensor_scalar(out=gv[:], in0=gv[:], scalar1=mv[:, 0:1], scalar2=rs[:, 0:1], op0=ALU.subtract, op1=ALU.mult), r=[gv, mv, rs], w=[gv])
                            P.op('dve', lambda h, gv=gv: h.tensor_tensor(out=gv[:], in0=gv[:], in1=lng[:], op=ALU.mult), r=[gv, lng], w=[gv])
                            P.op('dve', lambda h, gv=gv, ch=ch: h.tensor_tensor(out=vnt[:, ch * 1024:(ch + 1) * 1024], in0=gv[:], in1=lnb[:], op=ALU.add), r=[gv, lnb], w=[vn[ch]])
            for ch in range(4):
                pm = mm.next()
                for kt in range(32):
                    P.op('pe', lambda h, pm=pm, kt=kt, ch=ch: h.matmul(pm[:, 0:48], hTt[:, kt * Q + ch * 128:kt * Q + (ch + 1) * 128], wgl_sb[:, kt * 48:(kt + 1) * 48], start=(kt == 0), stop=(kt == 31)), r=[hT[kt], wgl_sb], w=[pm])
                o = ot.next()
                evac(o[:, 0:48], pm[:, 0:48], [pm], [o])
                P.dma('sp', glo[qi * 4 + ch], o[:, 0:48], r=[o])
            for j in range(NFM):
                wt = wb.next()
                P.dma('pool', wt[:, 0:4096], wfm[j], w=[wt], max_dma_last_dim=4096)
                pm = mm.next()
                for kt in range(32):
                    P.op('pe', lambda h, pm=pm, kt=kt, wt=wt: h.matmul(pm[:], wt[:, kt * 128:(kt + 1) * 128], hTt[:, kt * Q:(kt + 1) * Q], start=(kt == 0), stop=(kt == 31)), r=[hT[kt], wt], w=[pm])
                if j < NOUT:
                    o = ot.next()
                    evac(o[:], pm[:], [pm], [o])
                    P.dma('sp', zfm[j, :, t0:t0 + Q], o[:], r=[o])
                else:
                    g = j - NOUT
                    P.op('act', lambda h, pm=pm: h.activation(out=gut[:], in_=pm[:], func=AF.Gelu_apprx_tanh), r=[pm], w=[gut])
                    pm2 = mm.next()
                    for ch in range(4):
                        P.op('pe', lambda h, pm2=pm2, ch=ch, g=g: h.matmul(pm2[:, ch * 128:(ch + 1) * 128], vnt[:, ch * 1024 + g * 128:ch * 1024 + (g + 1) * 128], swb[:, g * 128:(g + 1) * 128], start=True, stop=True), r=[vn[ch], swb], w=[pm2])
                    o = ot.next()
                    P.op('dve', lambda h, o=o, pm2=pm2, g=g: h.tensor_tensor(out=o[:], in0=pm2[:], in1=sbb_sb[:, g * 512:(g + 1) * 512], op=ALU.add), r=[pm2, sbb_sb], w=[o])
                    P.op('dve', lambda h, o=o: h.tensor_tensor(out=o[:], in0=o[:], in1=gut[:], op=ALU.mult), r=[o, gut], w=[o])
                    P.dma('sp', ysg[g, :, t0:t0 + Q], o[:], r=[o])
        P.emit()
        print("progA: ops", {e: len(P.ops[e]) for e in ENGS}, "waits", P.nwait)
    return nc


def prep_A_weights(w_in, sgu_ln_g, sgu_ln_b, sgu_w, sgu_b):
    w4 = w_in.reshape(32, 128, -1)
    wfm = np.empty((NFM, 128, 32, 128), np.float32)
    for j, c0 in enumerate(FM_COLS):
        wfm[j] = w4[:, :, c0:c0 + 128].transpose(1, 0, 2)
    wfm = wfm.reshape(NFM, 128, 4096)
    wtm = np.empty((4, 2, 128, 16, 512), np.float32)
    for gi, c0 in enumerate(TM_COLS):
        blk = w4[:, :, c0:c0 + 512].reshape(2, 16, 128, 512)
        wtm[gi] = blk.transpose(0, 2, 1, 3)
    wtm = wtm.reshape(4, 2, 128, 16 * 512)
    wgl = np.ascontiguousarray(w4[:, :, C_GL:C_GL + 48].transpose(1, 0, 2)).reshape(128, 32 * 48)
    lngb = np.stack([np.broadcast_to(sgu_ln_g, (128, 1024)), np.broadcast_to(sgu_ln_b, (128, 1024))]).astype(np.float32)
    swT = np.ascontiguousarray(sgu_w.transpose(2, 0, 1)).reshape(128, 8 * 128)
    tmask = (np.arange(128)[None, :] >= np.arange(128)[:, None]).astype(np.float32)
    sbb = np.broadcast_to(np.tile(sgu_b[:, None, :], (1, 4, 1)).reshape(1, 8 * 512), (128, 8 * 512)).astype(np.float32)
    return dict(wfm=wfm, wtm=wtm, wgl=wgl, lngb=np.ascontiguousarray(lngb), swT=swT, tmask=tmask, sbb=np.ascontiguousarray(sbb))


S = 8192
NQB = 128
SCALE = 128 ** -0.5
NEG = -1e30
FORCE = 1e4
DELTAS = {0: 0, 64: 1, 512: 2, 576: 3}


def build_B(qblocks=None, do_lru=True):
    nc = bass.Bass("TRN2", target_bir_lowering=False)
    if qblocks is None:
        qblocks = list(range(NQB))

    def din(name, shape, dt=F32):
        return nc.dram_tensor(name, shape, dt, kind="ExternalInput").ap()

    def dout(name, shape, dt=F32):
        return nc.dram_tensor(name, shape, dt, kind="ExternalOutput").ap()

    qT_d = din("qT", [4, 128, S])
    kT_d = din("kT", [4, 128, S])
    vtm_d = din("vtm", [2, 128, 64, 128])
    glr_d = din("glr", [128, NQB * 6])
    lru_d = din("lruin", [2, 2, 128, S])
    cw1_d = din("cw1", [2, 128, 32 * 128])
    cpos_d = din("cpos", [2, 128, 32])
    cb1_d = din("cb1", [128, 2])
    cw2_d = din("cw2", [2, 128, 128])
    lcv_d = din("lcv", [128, 10])
    lw_d = din("lw", [2, 2, 128, 128])
    lbl_d = din("lbl", [128, 6])
    m4_d = din("m4", [128, 4 * 64])
    tc_d = din("tcc", [128, 64])
    cov_d = din("cov", [128, 4 * 128])
    ex_d = din("ex", [128, S])
    fold_d = din("fold", [128, 64])
    id_d = din("ident", [64, 64])
    oatt = dout("oatt", [2 * NQB, 128, 128])
    ylru = dout("ylru", [2, 128, S])

    with ExitStack() as es0:
        P = Prog(nc, es0)
        if do_lru:
            with ExitStack() as es:
                P.es = es
                TC = 2048
                lcv = P.sb("lcv", [128, 10], F32)
                lbl = P.sb("lbl", [128, 6], F32)
                lwb = P.sb("lwb", [128, 4 * 128], BF16)
                sc = P.sb("sc", [128, 4], F32)
                P.dma('sp', lcv[:], lcv_d, w=[lcv])
                P.dma('sp', lbl[:], lbl_d, w=[lbl])
                for blk in range(2):
                    for ax in range(2):
                        P.dma('pool', lwb[:, (blk * 2 + ax) * 128:(blk * 2 + ax + 1) * 128], lw_d[blk, ax], w=[lwb])
                tmp1 = P.sb("tmp1", [128, 2], F32)
                for blk in range(2):
                    P.op('act', lambda h, blk=blk: h.activation(out=tmp1[:, 0:1], in_=lbl[:, blk * 3 + 2:blk * 3 + 3], func=AF.Exp, scale=-1.0), r=[lbl], w=[tmp1])
                    P.op('act', lambda h: h.activation(out=tmp1[:, 1:2], in_=tmp1[:, 0:1], func=AF.Ln, bias=1.0), r=[tmp1], w=[tmp1])
                    P.op('dve', lambda h, blk=blk: h.tensor_scalar(out=sc[:, 2 * blk:2 * blk + 1], in0=tmp1[:, 1:2], scalar1=-8.0, scalar2=None, op0=ALU.mult), r=[tmp1], w=[sc])
                    P.op('dve', lambda h, blk=blk: h.tensor_scalar(out=sc[:, 2 * blk + 1:2 * blk + 2], in0=tmp1[:, 1:2], scalar1=-16.0, scalar2=None, op0=ALU.mult), r=[tmp1], w=[sc])
                zr = Rot([P.sb("zr%d" % i, [128, TC + 3], F32) for i in range(2)])
                zg = Rot([P.sb("zg%d" % i, [128, TC], F32) for i in range(2)])
                xr = P.sb("xr", [128, TC], F32)
                xrb = P.sb("xrb", [128, TC], BF16)
                rr = P.sb("rr", [128, TC], F32)
                ig = P.sb("ig", [128, TC], F32)
                aa = P.sb("aa", [128, TC], F32)
                bb = P.sb("bb", [128, TC], F32)
                hh = Rot([P.sb("hh%d" % i, [128, TC], F32) for i in range(2)])
                carry = P.sb("carry", [128, 1], F32)
                pl = Rot([P.ps("pl%d" % i, [128, 512]) for i in range(4)])
                for blk in range(2):
                    P.op('dve', lambda h: h.memset(carry[:], 0.0), w=[carry])
                    for tci in range(S // TC):
                        t0 = tci * TC
                        z = zr.next()
                        g_ = zg.next()
                        if tci == 0:
                            P.op('dve', lambda h, z=z: h.memset(z[:, 0:3], 0.0), w=[z])
                            P.dma('sp', z[:, 3:3 + TC], lru_d[1, blk, :, 0:TC], w=[z])
                        else:
                            P.dma('sp', z[:], lru_d[1, blk, :, t0 - 3:t0 + TC], w=[z])
                        P.dma('sp', g_[:], lru_d[0, blk, :, t0:t0 + TC], w=[g_])
                        c0 = blk * 5
                        P.op('dve', lambda h, z=z, c0=c0: h.tensor_scalar(out=xr[:], in0=z[:, 3:3 + TC], scalar1=lcv[:, c0 + 3:c0 + 4], scalar2=lcv[:, c0 + 4:c0 + 5], op0=ALU.mult, op1=ALU.add), r=[z, lcv], w=[xr])
                        for k in range(3):
                            P.op('dve', lambda h, z=z, c0=c0, k=k: h.scalar_tensor_tensor(out=xr[:], in0=z[:, k:k + TC], scalar=lcv[:, c0 + k:c0 + k + 1], in1=xr[:], op0=ALU.mult, op1=ALU.add), r=[z, lcv, xr], w=[xr])
                        P.op('act', lambda h: h.activation(out=xrb[:], in_=xr[:], func=AF.Copy), r=[xr], w=[xrb])
                        for ax, dst in ((0, rr), (1, ig)):
                            for n4 in range(TC // 512):
                                pm = pl.next()
                                P.op('pe', lambda h, pm=pm, ax=ax, n4=n4, blk=blk: h.matmul(pm[:], lwb[:, (blk * 2 + ax) * 128:(blk * 2 + ax + 1) * 128], xrb[:, n4 * 512:(n4 + 1) * 512], start=True, stop=True), r=[lwb, xrb], w=[pm])
                                P.op('act', lambda h, pm=pm, dst=dst, n4=n4, ax=ax, blk=blk: h.activation(out=dst[:, n4 * 512:(n4 + 1) * 512], in_=pm[:], func=AF.Sigmoid, bias=lbl[:, blk * 3 + ax:blk * 3 + ax + 1]), r=[pm, lbl], w=[dst])
                        P.op('act', lambda h, blk=blk: h.activation(out=aa[:], in_=rr[:], func=AF.Exp, scale=sc[:, 2 * blk:2 * blk + 1]), r=[rr, sc], w=[aa])
                        P.op('act', lambda h, blk=blk: h.activation(out=bb[:], in_=rr[:], func=AF.Exp, scale=sc[:, 2 * blk + 1:2 * blk + 2]), r=[rr, sc], w=[bb])
                        P.op('act', lambda h: h.activation(out=bb[:], in_=bb[:], func=AF.Sqrt, scale=-1.0, bias=1.0), r=[bb], w=[bb])
                        P.op('dve', lambda h: h.tensor_tensor(out=ig[:], in0=ig[:], in1=xr[:], op=ALU.mult), r=[ig, xr], w=[ig])
                        P.op('dve', lambda h: h.tensor_tensor(out=bb[:], in0=bb[:], in1=ig[:], op=ALU.mult), r=[bb, ig], w=[bb])
                        hcur = hh.next()
                        P.op('dve', lambda h, hcur=hcur: h.tensor_tensor_scan(out=hcur[:], data0=aa[:], data1=bb[:], initial=carry[:, 0:1], op0=ALU.mult, op1=ALU.add), r=[aa, bb, carry], w=[hcur])
                        P.op('dve', lambda h, hcur=hcur: h.tensor_copy(out=carry[:], in_=hcur[:, TC - 1:TC]), r=[hcur], w=[carry])
                        P.op('act', lambda h, g_=g_: h.activation(out=g_[:], in_=g_[:], func=AF.Gelu_apprx_tanh), r=[g_], w=[g_])
                        P.op('dve', lambda h, hcur=hcur, g_=g_: h.tensor_tensor(out=hcur[:], in0=hcur[:], in1=g_[:], op=ALU.mult), r=[hcur, g_], w=[hcur])
                        P.dma('sp', ylru[blk, :, t0:t0 + TC], hcur[:], r=[hcur])
                P.emit()

        with ExitStack() as es:
            P.es = es
            qT = P.sb("qT", [128, 4, S], BF16)
            ksT = P.sb("ksT", [128, S], BF16)
            kwT = P.sb("kwT", [128, S], BF16)
            vsA = P.sb("vsA", [128, 64, 129], BF16)
            vwA = P.sb("vwA", [128, 64, 129], BF16)
            ex = P.sb("ex", [128, S], BF16)
            kcT = P.sb("kcT", [128, 512], BF16)
            VCA = P.sb("VCA", [128, 4, 257], BF16)
            m4 = P.sb("m4", [128, 4 * 64], F32)
            tcc = P.sb("tcc", [128, 64], F32)
            fold = P.sb("fold", [128, 64], F32)
            ident = P.sb("ident", [64, 64], F32)
            sig = P.sb("sig", [128, NQB * 6], F32)
            for hd in range(4):
                P.dma('pool', qT[:, hd, :], qT_d[hd], w=[qT], max_dma_last_dim=4096)
            P.dma('pool', ksT[:], kT_d[2], w=[ksT], max_dma_last_dim=4096)
            P.dma('pool', kwT[:], kT_d[3], w=[kwT], max_dma_last_dim=4096)
            P.dma('pool', ex[:], ex_d, w=[ex], max_dma_last_dim=4096)
            P.op('dve', lambda h: h.memset(vsA[:, :, 128:129], 1.0), w=[vsA])
            P.op('dve', lambda h: h.memset(vwA[:, :, 128:129], 1.0), w=[vwA])
            for half in range(2):
                P.dma('pool', vsA[:, half * 32:(half + 1) * 32, 0:128], vtm_d[0, :, half * 32:(half + 1) * 32, :], w=[vsA], max_dma_last_dim=512)
                P.dma('pool', vwA[:, half * 32:(half + 1) * 32, 0:128], vtm_d[1, :, half * 32:(half + 1) * 32, :], w=[vwA], max_dma_last_dim=512)
            P.dma('sp', m4[:], m4_d, w=[m4])
            P.dma('sp', tcc[:], tc_d, w=[tcc])
            P.dma('sp', fold[:], fold_d, w=[fold])
            P.dma('sp', ident[:], id_d, w=[ident])
            P.dma('sp', sig[:], glr_d, w=[sig])
            P.op('act', lambda h: h.activation(out=sig[:], in_=sig[:], func=AF.Sigmoid), r=[sig], w=[sig])
            P.op('dve', lambda h: h.memset(VCA[:, :, 256:257], 1.0), w=[VCA])
            for ct in range(4):
                P.dma('pool', VCA[:, ct, 128:256], cov_d[:, ct * 128:(ct + 1) * 128], w=[VCA], max_dma_last_dim=512)

            pS = Rot([P.ps("pS%d" % i, [128, 512]) for i in range(2)])
            pMA = P.ps("pMA", [128, 512])
            pMB = P.ps("pMB", [128, 512])
            pMs = Rot([(pMA, 0), (pMB, 0)])
            pU = Rot([P.ps("pU%d" % i, [128, 512]) for i in range(4)])
            misc = pMA

            if True:
                kr = P.sb("kr", [128, S], BF16)
                w1 = P.sb("w1", [128, 32 * 128], BF16)
                w2 = P.sb("w2", [128, 128], BF16)
                posb = P.sb("posb", [128, 32], BF16)
                cb1 = P.sb("cb1", [128, 2], F32)
                cbias = P.sb("cbias", [128, 1], F32)
                gT = P.sb("gT", [128, 512], BF16)
                P.dma('sp', cb1[:], cb1_d, w=[cb1])
                for kv in range(2):
                    P.dma('pool', kr[:], kT_d[kv], w=[kr], max_dma_last_dim=4096)
                    P.dma('pool', w1[:], cw1_d[kv], w=[w1], max_dma_last_dim=4096)
                    P.dma('pool', w2[:], cw2_d[kv], w=[w2])
                    P.dma('pool', posb[:], cpos_d[kv], w=[posb])
                    for p in range(32):
                        P.op('pe', lambda h, p=p: h.matmul(misc[:, 0:1], w1[:, p * 128:(p + 1) * 128], posb[:, p:p + 1], start=(p == 0), stop=(p == 31)), r=[w1, posb], w=[misc])
                    P.op('dve', lambda h, kv=kv: h.tensor_tensor(out=cbias[:], in0=misc[:, 0:1], in1=cb1[:, kv:kv + 1], op=ALU.add), r=[misc, cb1], w=[cbias])
                    pm = pS.next()
                    for p in range(32):
                        P.op('pe', lambda h, p=p, pm=pm: h.matmul(pm[:, 0:511], w1[:, p * 128:(p + 1) * 128], kr[:, p:p + 16 * 510 + 1:16], start=(p == 0), stop=(p == 31)), r=[w1, kr], w=[pm])
                    P.op('dve', lambda h: h.memset(gT[:], 0.0), w=[gT])
                    P.op('act', lambda h, pm=pm: h.activation(out=gT[:, 0:511], in_=pm[:, 0:511], func=AF.Gelu_apprx_tanh, bias=cbias[:, 0:1]), r=[pm, cbias], w=[gT])
                    if kv == 0:
                        pm2 = pS.next()
                        P.op('pe', lambda h, pm2=pm2: h.matmul(pm2[:], w2[:], gT[:], start=True, stop=True), r=[w2, gT], w=[pm2])
                        P.op('act', lambda h, pm2=pm2: h.activation(out=kcT[:], in_=pm2[:], func=AF.Copy), r=[pm2], w=[kcT])
                    else:
                        pm2 = pS.next()
                        for ct in range(4):
                            P.op('pe', lambda h, pm2=pm2, ct=ct: h.matmul(pm2[:, ct * 128:(ct + 1) * 128], gT[:, ct * 128:(ct + 1) * 128], w2[:], start=True, stop=True), r=[w2, gT], w=[pm2])
                        for ct in range(4):
                            P.op('act', lambda h, pm2=pm2, ct=ct: h.activation(out=VCA[:, ct, 0:128], in_=pm2[:, ct * 128:(ct + 1) * 128], func=AF.Copy), r=[pm2], w=[VCA])

            Et = Rot([P.sb("E%d" % i, [128, 256], F32) for i in range(3)])
            Pt = Rot([P.sb("P%d" % i, [128, 256], BF16) for i in range(4)])
            mk = Rot([P.sb("mk%d" % i, [128, 64], F32) for i in range(3)])
            impn = [P.sb("impn%d" % i, [128, 128], F32) for i in range(2)]
            accs = Rot([P.sb("acc%d" % i, [128, 128], F32) for i in range(4)])
            imp2 = P.sb("imp2", [64, 128], F32)
            imp3 = P.sb("imp3", [64, 128], F32)
            m8 = P.sb("m8", [64, 16], F32)
            sel = P.sb("sel", [64, 128], F32)
            selT = P.sb("selT", [128, 64], BF16)
            sm = Rot([P.sb("sm%d" % i, [128, 2], F32) for i in range(4)])
            P.op('dve', lambda h: h.memset(imp2[:], NEG), w=[imp2])

            pend = [None]

            def tile(i, Kap, Ktr, mask, Vap, Vtr, U, ncol, first, last):
                ps = pS.next()
                P.op('pe', lambda h: h.matmul(ps[:, 0:256], Kap, qT[:, :, i * 64:(i + 1) * 64], start=True, stop=True), r=[Ktr, qT], w=[ps])
                if pend[0] is not None:
                    pend[0]()
                    pend[0] = None
                pt = Pt.next()
                if mask is None:
                    P.op('act', lambda h: h.activation(out=pt[:], in_=ps[:, 0:256], func=AF.Exp, scale=SCALE), r=[ps], w=[pt])
                else:
                    e = Et.next()
                    P.op('act', lambda h: h.activation(out=e[:], in_=ps[:, 0:256], func=AF.Exp, scale=SCALE), r=[ps], w=[e])
                    kind = mask[0]
                    if kind == 'const':
                        mt, map_ = m4, m4[:, mask[1] * 64:(mask[1] + 1) * 64]
                    elif kind == 'c':
                        mt = mk.next()
                        th = float(mask[1])
                        P.op('dve', lambda h: h.tensor_scalar(out=mt[:], in0=tcc[:], scalar1=th, scalar2=None, op0=ALU.is_le), r=[tcc], w=[mt])
                        map_ = mt[:]
                    else:
                        kt = mask[1]
                        pM, po = pMs.next()
                        P.op('pe', lambda h: h.matmul(pM[:, po:po + 64], ex[:, kt * 128:(kt + 1) * 128], selT[:], start=True, stop=True), r=[ex, selT], w=[pM])
                        if mask[2] is None:
                            mt, map_ = pM, pM[:, po:po + 64]
                        else:
                            mt = mk.next()
                            di = mask[2]
                            P.op('dve', lambda h: h.tensor_tensor(out=mt[:], in0=pM[:, po:po + 64], in1=m4[:, di * 64:(di + 1) * 64], op=ALU.mult), r=[pM, m4], w=[mt])
                            map_ = mt[:]
                    P.op('dve', lambda h: h.tensor_tensor(out=pt[:].rearrange("p (h q) -> p h q", h=4), in0=e[:].rearrange("p (h q) -> p h q", h=4), in1=map_[:, None, :].broadcast_to([128, 4, 64]), op=ALU.mult), r=[e, mt], w=[pt])

                def pv():
                    for rt in range(2):
                        P.op('pe', lambda h, rt=rt: h.matmul(U[rt][:, 0:ncol], pt[:, rt * 128:(rt + 1) * 128], Vap, start=first, stop=last), r=[pt, Vtr], w=[U[rt]])
                pend[0] = pv

            def flush():
                if pend[0] is not None:
                    pend[0]()
                    pend[0] = None

            def do_qblock(i):
                gbase = i * 6
                Uc = [pU.next(), pU.next()]
                nct = (4 * i + 2) // 128 + 1
                for ct in range(nct):
                    th = 64 * i - 31 - 2048 * ct
                    tile(i, kcT[:, ct * 128:(ct + 1) * 128], kcT, ('c', th), VCA[:, ct, :], VCA, Uc, 257, ct == 0, ct == nct - 1)
                flush()
                acc = [accs.next(), accs.next()]
                for rt in range(2):
                    s_ = sm.next()
                    P.op('dve', lambda h, s_=s_, rt=rt: h.tensor_scalar(out=s_[:, 0:1], in0=Uc[rt][:, 256:257], scalar1=1e-30, scalar2=None, op0=ALU.max), r=[Uc[rt]], w=[s_])
                    P.op('dve', lambda h, s_=s_: h.reciprocal(out=s_[:, 0:1], in_=s_[:, 0:1]), r=[s_], w=[s_])
                    P.op('dve', lambda h, s_=s_, rt=rt: h.tensor_scalar(out=impn[rt][:], in0=Uc[rt][:, 128:256], scalar1=s_[:, 0:1], scalar2=None, op0=ALU.mult), r=[Uc[rt], s_], w=[impn[rt]])
                    P.op('dve', lambda h, s_=s_, rt=rt: h.tensor_tensor(out=s_[:, 1:2], in0=s_[:, 0:1], in1=sig[:, gbase + rt * 3:gbase + rt * 3 + 1], op=ALU.mult), r=[s_, sig], w=[s_])
                    P.op('dve', lambda h, s_=s_, rt=rt: h.tensor_scalar(out=acc[rt][:], in0=Uc[rt][:, 0:128], scalar1=s_[:, 1:2], scalar2=None, op0=ALU.mult), r=[Uc[rt], s_], w=[acc[rt]])
                if i >= 16:
                    for rt in range(2):
                        P.op('pe', lambda h, rt=rt: h.matmul(misc[0:64, 0:128], fold[:], impn[rt][:], start=(rt == 0), stop=(rt == 1)), r=[fold, impn[rt]], w=[misc])
                    P.op('dve', lambda h: h.tensor_copy(out=imp2[:, 0:i + 1], in_=misc[0:64, 0:i + 1]), r=[misc], w=[imp2])
                    P.op('dve', lambda h: h.memset(imp2[:, 0:1], FORCE), r=[], w=[imp2])
                    P.op('dve', lambda h: h.memset(imp2[:, i - 1:i + 1], FORCE), r=[], w=[imp2])
                    P.op('dve', lambda h: h.max(out=m8[:, 0:8], in_=imp2[:]), r=[imp2], w=[m8])
                    P.op('dve', lambda h: h.match_replace(out=imp3[:], in_to_replace=m8[:, 0:8], in_values=imp2[:], imm_value=-3e38), r=[imp2, m8], w=[imp3])
                    P.op('dve', lambda h: h.max(out=m8[:, 8:16], in_=imp3[:]), r=[imp3], w=[m8])
                    P.op('dve', lambda h: h.tensor_scalar(out=sel[:], in0=imp2[:], scalar1=m8[:, 15:16], scalar2=None, op0=ALU.is_ge), r=[imp2, m8], w=[sel])
                else:
                    P.op('dve', lambda h: h.memset(sel[:], 0.0), w=[sel])
                    P.op('dve', lambda h: h.memset(sel[:, 0:i + 1], 1.0), w=[sel])
                P.op('pe', lambda h: h.transpose(misc[:, 256:320], sel[:], ident[:]), r=[sel, ident], w=[misc])
                P.op('act', lambda h: h.activation(out=selT[:], in_=misc[:, 256:320], func=AF.Copy), r=[misc], w=[selT])
                Uw = [pU.next(), pU.next()]
                kt_lo = max(0, (64 * i - 511) // 128)
                kts = list(range(kt_lo, i // 2 + 1))
                for n, kt in enumerate(kts):
                    dl = 64 * i - 128 * kt
                    mask = ('const', DELTAS[dl]) if dl in DELTAS else None
                    tile(i, kwT[:, kt * 128:(kt + 1) * 128], kwT, mask, vwA[:, kt, :], vwA, Uw, 129, n == 0, n == len(kts) - 1)
                flush()

                def fin(U, col):
                    for rt in range(2):
                        s_ = sm.next()
                        P.op('dve', lambda h, s_=s_, rt=rt: h.reciprocal(out=s_[:, 0:1], in_=U[rt][:, 128:129]), r=[U[rt]], w=[s_])
                        P.op('dve', lambda h, s_=s_, rt=rt: h.tensor_tensor(out=s_[:, 1:2], in0=s_[:, 0:1], in1=sig[:, gbase + rt * 3 + col:gbase + rt * 3 + col + 1], op=ALU.mult), r=[s_, sig], w=[s_])
                        P.op('dve', lambda h, s_=s_, rt=rt: h.scalar_tensor_tensor(out=acc[rt][:], in0=U[rt][:, 0:128], scalar=s_[:, 1:2], in1=acc[rt][:], op0=ALU.mult, op1=ALU.add), r=[U[rt], s_, acc[rt]], w=[acc[rt]])
                fin(Uw, 2)
                Us = [pU.next(), pU.next()]
                kts = list(range(0, i // 2 + 1))
                for n, kt in enumerate(kts):
                    dl = 64 * i - 128 * kt
                    mask = ('sel', kt, DELTAS[dl] if dl in (0, 64) else None)
                    tile(i, ksT[:, kt * 128:(kt + 1) * 128], ksT, mask, vsA[:, kt, :], vsA, Us, 129, n == 0, n == len(kts) - 1)
                flush()
                fin(Us, 1)
                for rt in range(2):
                    P.dma('sp', oatt[i * 2 + rt], acc[rt][:], r=[acc[rt]])
            for i in qblocks:
                do_qblock(i)
            P.emit()
        print("progB: ops", P.total, "waits", P.nwait)
    return nc


def consts_B():
    kl = np.arange(128)[:, None]
    r = np.arange(64)[None, :]
    m4 = []
    for dl in (0, 64, 512, 576):
        m4.append(((kl - r <= dl) & (kl - r > dl - 512)).astype(np.float32))
    m4 = np.concatenate(m4, axis=1)
    tcc = (16 * kl - r).astype(np.float32)
    c = np.arange(512)[:, None]
    j = np.arange(128)[None, :]
    cover = ((16 * c < 64 * j + 64) & (16 * c + 32 > 64 * j)).astype(np.float32)
    cov = np.ascontiguousarray(cover.reshape(4, 128, 128).transpose(1, 0, 2)).reshape(128, 512)
    ex = (np.arange(S)[None, :] // 64 == np.arange(128)[:, None]).astype(np.float32)
    fold = np.tile(np.eye(64, dtype=np.float32), (2, 1))
    ident = np.eye(64, dtype=np.float32)
    return dict(m4=m4, tcc=tcc, cov=cov, ex=ex, fold=fold, ident=ident)


def prep_B_params(g, cmp_pos, cmp_w1, cmp_b1, cmp_w2, lru_conv_w, lru_conv_b, lru_wa, lru_ba, lru_wx, lru_bx, lru_lambda):
    cw1 = np.ascontiguousarray(cmp_w1.reshape(2, 32, 128, 128).transpose(0, 2, 1, 3)).reshape(2, 128, 32 * 128)
    cpos = np.ascontiguousarray(cmp_pos.transpose(0, 2, 1))
    cb1 = np.ascontiguousarray(cmp_b1.T)
    blks = [2 * g, 2 * g + 1]
    lcv = np.zeros((128, 2, 5), np.float32)
    lbl = np.zeros((128, 2, 3), np.float32)
    lw = np.zeros((2, 2, 128, 128), np.float32)
    for bi, n in enumerate(blks):
        sl = slice(n * 128, (n + 1) * 128)
        lcv[:, bi, 0:4] = lru_conv_w[:, sl].T
        lcv[:, bi, 4] = lru_conv_b[sl]
        lbl[:, bi, 0] = lru_ba[sl]
        lbl[:, bi, 1] = lru_bx[sl]
        lbl[:, bi, 2] = lru_lambda[sl]
        lw[bi, 0] = lru_wa[n]
        lw[bi, 1] = lru_wx[n]
    return dict(cw1=cw1, cpos=cpos, cb1=cb1, cw2=np.ascontiguousarray(cmp_w2), lcv=lcv.reshape(128, 10), lw=lw, lbl=lbl.reshape(128, 6))


MCT = 24


def build_M():
    nc = bass.Bass("TRN2", target_bir_lowering=False)
    cT_d = nc.dram_tensor("cT", [128, 64], F32, kind="ExternalInput").ap()
    wm_d = nc.dram_tensor("wm", [2 * MCT, 128, 4096], F32, kind="ExternalInput").ap()
    bm_d = nc.dram_tensor("bm", [128, 2 * MCT], F32, kind="ExternalInput").ap()
    mo_d = nc.dram_tensor("mo", [2 * MCT, 128, 2], F32, kind="ExternalOutput").ap()
    with ExitStack() as es:
        P = Prog(nc, es)
        cs = P.sb("cs", [128, 64], F32)
        bm = P.sb("bm", [128, 2 * MCT], F32)
        P.dma('sp', cs[:], cT_d, w=[cs])
        P.dma('sp', bm[:], bm_d, w=[bm])
        P.op('act', lambda h: h.activation(out=cs[:], in_=cs[:], func=AF.Silu), r=[cs], w=[cs])
        wt = Rot([P.sb("wt%d" % i, [128, 4096], F32) for i in range(3)])
        ot = Rot([P.sb("ot%d" % i, [128, 2], F32) for i in range(3)])
        pm = Rot([P.ps("pm%d" % i, [128, 512]) for i in range(3)])
        for j in range(2 * MCT):
            w = wt.next()
            P.dma('sp', w[:], wm_d[j], w=[w])
            p = pm.next()
            for kt in range(32):
                P.op('pe', lambda h, w=w, p=p, kt=kt: h.matmul(p[:, 0:2], w[:, kt * 128:(kt + 1) * 128], cs[:, 2 * kt:2 * kt + 2], start=(kt == 0), stop=(kt == 31)), r=[w, cs], w=[p])
            o = ot.next()
            P.op('act', lambda h, o=o, p=p, j=j: h.activation(out=o[:], in_=p[:, 0:2], func=AF.Identity, bias=bm[:, j:j + 1]), r=[p, bm], w=[o])
            P.dma('sp', mo_d[j], o[:], r=[o])
        P.emit()
    return nc


def run_M(c, w_mod, b_mod):
    nc = build_M()
    cT = np.ascontiguousarray(c.T.reshape(32, 128, 2).transpose(1, 0, 2)).reshape(128, 64)
    in_maps = []
    for core in range(8):
        c0 = core * MCT * 128
        wm = np.empty((2, MCT, 128, 32, 128), np.float32)
        bm = np.empty((128, 2, MCT), np.float32)
        for l in range(2):
            blk = w_mod[l][:, c0:c0 + MCT * 128].reshape(32, 128, MCT, 128)
            wm[l] = blk.transpose(2, 1, 0, 3)
            bm[:, l, :] = b_mod[l][c0:c0 + MCT * 128].reshape(MCT, 128).T
        in_maps.append(dict(cT=cT, wm=wm.reshape(2 * MCT, 128, 4096), bm=bm.reshape(128, 2 * MCT)))
    res = run_bass_kernel_spmd(nc, in_maps, core_ids=list(range(8)))
    mod = np.empty((2, 2, 24576), np.float32)
    for core in range(8):
        mo = res.results[core]["mo"].reshape(2, MCT, 128, 2)
        c0 = core * MCT * 128
        for l in range(2):
            mod[l][:, c0:c0 + MCT * 128] = mo[l].transpose(2, 0, 1).reshape(2, MCT * 128)
    return mod

import time as _time

_PROGS = {}


def _get(name, fn):
    if name not in _PROGS:
        _PROGS[name] = fn()
    return _PROGS[name]


def _modv(m):
    return np.ascontiguousarray(m.reshape(6, 32, 128).transpose(2, 0, 1)).reshape(128, 192)


def kernel(x, c, w_mod, b_mod, w_in, cmp_pos, cmp_w1, cmp_b1, cmp_w2, lru_conv_w, lru_conv_b,
           lru_wa, lru_ba, lru_wx, lru_bx, lru_lambda, sgu_ln_g, sgu_ln_b, sgu_w, sgu_b, w_o,
           ln1_g, ln1_b, ffn_w_in, ffn_conv_w, ffn_conv_b, ffn_w_down, ln2_g, ln2_b):
    t_start = _time.time()

    def log(msg):
        print("[kernel %.0fs] %s" % (_time.time() - t_start, msg), flush=True)

    f32 = lambda a: np.ascontiguousarray(np.asarray(a, dtype=np.float32))
    x = f32(x)
    c = f32(c)
    B, SEQ, D = x.shape
    NCORE = 8
    JB = 2
    NTA_ = SEQ // JB
    cores = [(b, j) for b in range(B) for j in range(JB)]
    mod = run_M(c, f32(w_mod), f32(b_mod))
    log("mod done")
    xT = [fm(x[b, j * NTA_:(j + 1) * NTA_]) for (b, j) in cores]
    cB = consts_B()
    for l in range(2):
        WA = prep_A_weights(f32(w_in[l]), f32(sgu_ln_g[l]), f32(sgu_ln_b[l]), f32(sgu_w[l]), f32(sgu_b[l]))
        ncA = _get("A", lambda: build_A(NTA_))
        in_maps = [dict(xT=xT[k], modv=_modv(mod[l, b]), **WA) for k, (b, j) in enumerate(cores)]
        resA = run_bass_kernel_spmd(ncA, in_maps, core_ids=list(range(len(cores)))).results
        del WA, in_maps
        log("layer %d A done" % l)
        ncB = _get("B", build_B)
        in_maps = []
        for b in range(B):
            zfm = np.concatenate([resA[b * JB + j]["zfm"] for j in range(JB)], axis=2)
            vsw = np.concatenate([resA[b * JB + j]["vsw"] for j in range(JB)], axis=1)
            glo = np.concatenate([resA[b * JB + j]["glo"] for j in range(JB)], axis=0).reshape(SEQ, 48)
            for g in range(4):
                qT_ = np.ascontiguousarray(zfm[4 * g:4 * g + 4])
                kT_ = np.stack([zfm[16 + g], zfm[20 + g], zfm[24 + g], zfm[28 + g]])
                vtm = np.stack([np.ascontiguousarray(vsw[v][:, :, g * 128:(g + 1) * 128].transpose(1, 0, 2)) for v in range(2)])
                gl = glo[:, g * 12:(g + 1) * 12]
                glr = np.ascontiguousarray(gl.reshape(128, 64, 2, 2, 3).transpose(3, 1, 0, 2, 4)).reshape(128, 128 * 6)
                lruin = np.stack([np.stack([zfm[32 + 2 * g], zfm[32 + 2 * g + 1]]), np.stack([zfm[40 + 2 * g], zfm[40 + 2 * g + 1]])])
                pb = prep_B_params(g, f32(cmp_pos[l]), f32(cmp_w1[l]), f32(cmp_b1[l]), f32(cmp_w2[l]), f32(lru_conv_w[l]),
                                   f32(lru_conv_b[l]), f32(lru_wa[l]), f32(lru_ba[l]), f32(lru_wx[l]), f32(lru_bx[l]), f32(lru_lambda[l]))
                in_maps.append(dict(qT=qT_, kT=kT_, vtm=vtm, glr=glr, lruin=lruin, **pb, **cB))
        resB = run_bass_kernel_spmd(ncB, in_maps, core_ids=list(range(NCORE))).results
        del in_maps
        log("layer %d B done" % l)
        WC = prep_C_weights(f32(w_o[l]), f32(ffn_w_in[l]), f32(ffn_conv_w[l]), f32(ffn_conv_b[l]), f32(ffn_w_down[l]),
                            f32(ln1_g[l]), f32(ln1_b[l]), f32(ln2_g[l]), f32(ln2_b[l]))
        JC = 4
        NTC_ = SEQ // JC
        ncC = _get("C", lambda: build_C(NTC_, True))
        in_maps = []
        for b in range(B):
            yT = np.empty((32, 128, SEQ), np.float32)
            for g in range(4):
                oa = resB[b * 4 + g]["oatt"].reshape(128, 2, 2, 64, 128)
                yT[4 * g:4 * g + 4] = oa.transpose(1, 2, 4, 0, 3).reshape(4, 128, SEQ)
                yl = resB[b * 4 + g]["ylru"]
                yT[16 + 2 * g] = yl[0]
                yT[16 + 2 * g + 1] = yl[1]
            for j in range(JB):
                yT[24:32, :, j * NTA_:(j + 1) * NTA_] = resA[b * JB + j]["ysg"]
            xfull = np.concatenate([xT[b * JB + j] for j in range(JB)], axis=2)
            for j in range(JC):
                t0 = j * NTC_
                yh = np.zeros((32, 128, NTC_ + 2), np.float32)
                xh = np.zeros((32, 128, NTC_ + 2), np.float32)
                lo = max(t0 - 2, 0)
                yh[:, :, 2 - (t0 - lo):] = yT[:, :, lo:t0 + NTC_]
                xh[:, :, 2 - (t0 - lo):] = xfull[:, :, lo:t0 + NTC_]
                hflag = np.full((128, 1), 1.0 if j > 0 else 0.0, np.float32)
                in_maps.append(dict(yT=yh, xT=xh, hflag=hflag, modv=_modv(mod[l, b]), **WC))
        del resA, resB
        resC = run_bass_kernel_spmd(ncC, in_maps, core_ids=list(range(B * JC))).results
        del in_maps, WC
        xT = []
        for b in range(B):
            xb = np.concatenate([resC[b * JC + j]["x2T"] for j in range(JC)], axis=2)
            for j in range(JB):
                xT.append(np.ascontiguousarray(xb[:, :, j * NTA_:(j + 1) * NTA_]))
        log("layer %d C done" % l)
    out = np.empty((B, SEQ, D), np.float32)
    for k, (b, j) in enumerate(cores):
        out[b, j * NTA_:(j + 1) * NTA_] = xT[k].reshape(D, NTA_).T
    return out
```
